# Optimizing a Trainium2 kernel written in Bass

```python
import math
import jax, jax.numpy as jnp
from jax import lax
import numpy as np

D_MODEL = 1024
BATCH = 16
SEQ = 256
DEPTH = 2
DEC_BATCH = 2
DEC_SEQ = 2048
PAST_LEN = 256

GRID_W = 64
N_AB = (DEPTH + 1) // 2
N_NA = DEPTH // 2
D_FF = 2816
N_ADA = 9
EPS = 1e-6
CHUNK = 64
QBLK = 128
H_A = 4
DQK_A = 64
DV_A = 128
H_B = 4
DK_B = 128
DV_B = 128
CONV_K = 5
H_C = 16
DH_C = 64
WIN_R = 8
WIN_C = 16

AB_SIZES = (H_A * DQK_A, H_A * DQK_A, H_A * DV_A, H_A * DV_A, 2 * H_A, 2 * H_A,
            H_B * DK_B, H_B * DK_B, H_B * DV_B, H_B * DV_B, 2 * H_B, 2 * H_B)
AB_IN = sum(AB_SIZES)
AB_OUT = H_A * DV_A + H_B * DV_B
QKV_B = 2 * H_B * DK_B + H_B * DV_B
NA_W = H_C * DH_C

kernel_name = 'hybrid_mlstm_deltanet_natten_prefix_dit_step'


def rmsnorm(x, g):
    xf = x.astype(jnp.float32)
    y = xf * lax.rsqrt(jnp.mean(xf * xf, axis=-1, keepdims=True) + EPS)
    return (y * g.astype(jnp.float32)).astype(x.dtype)


def head_rms(x):
    return x * lax.rsqrt(jnp.mean(x * x, axis=-1, keepdims=True) + EPS)


def l2norm(x):
    return x * lax.rsqrt(jnp.sum(x * x, axis=-1, keepdims=True) + EPS)


def modulate(x, g, shift, scale):
    return rmsnorm(x, g) * (1 + scale) + shift


def swiglu(h, wg, wu, wd):
    return (jax.nn.silu(h @ wg) * (h @ wu)) @ wd


def split_heads(a, n_heads):
    b, t, _ = a.shape
    return a.reshape(b, t, n_heads, -1).transpose(0, 2, 1, 3)


def merge_heads(a):
    b, h, t, d = a.shape
    return a.transpose(0, 2, 1, 3).reshape(b, t, h * d)


def dir_gates(a):
    b, t, _ = a.shape
    return a.reshape(b, t, 2, -1).transpose(0, 2, 3, 1).astype(jnp.float32)


def flip_t(a):
    return jnp.flip(a, axis=2)


def to_chunks(a):
    nc = a.shape[2] // CHUNK
    return jnp.moveaxis(a.reshape(a.shape[:2] + (nc, CHUNK) + a.shape[3:]), 2, 0)


def from_chunks(a):
    a = jnp.moveaxis(a, 0, 2)
    return a.reshape(a.shape[:2] + (a.shape[2] * a.shape[3],) + a.shape[4:])


def short_conv(x, w):
    ch = x.shape[-1]
    return lax.conv_general_dilated(x, w[:, None, :].astype(x.dtype), window_strides=(1,),
                                    padding=[(CONV_K // 2, CONV_K // 2)],
                                    dimension_numbers=('NWC', 'WIO', 'NWC'),
                                    feature_group_count=ch)


def mlstm_chunked(q, k, v, logi, logf, c0, n0, m0):
    tri = jnp.tril(jnp.ones((CHUNK, CHUNK), bool))

    def step(carry, inp):
        cm, nv, m = carry
        qc, kc, vc, ic, fc = inp
        b = jnp.cumsum(fc, axis=-1)
        dmat = jnp.where(tri, b[..., :, None] - b[..., None, :] + ic[..., None, :], -jnp.inf)
        m_inter = b + m[..., None]
        m_t = jnp.maximum(m_inter, jnp.max(dmat, axis=-1))
        a_inter = jnp.exp(m_inter - m_t)
        sw = jnp.einsum('bhtd,bhsd->bhts', qc, kc) * jnp.exp(dmat - m_t[..., None])
        num = (jnp.einsum('bhts,bhsv->bhtv', sw, vc)
               + a_inter[..., None] * jnp.einsum('bhtd,bhdv->bhtv', qc, cm))
        den = jnp.sum(sw, axis=-1) + a_inter * jnp.einsum('bhtd,bhd->bht', qc, nv)
        h = num / jnp.maximum(jnp.abs(den), jnp.exp(-m_t))[..., None]
        b_last = b[..., -1]
        gs = b_last[..., None] - b + ic
        m_new = jnp.maximum(b_last + m, jnp.max(gs, axis=-1))
        carry_decay = jnp.exp(b_last + m - m_new)
        ws = jnp.exp(gs - m_new[..., None])
        cm = carry_decay[..., None, None] * cm + jnp.einsum('bhs,bhsd,bhsv->bhdv', ws, kc, vc)
        nv = carry_decay[..., None] * nv + jnp.einsum('bhs,bhsd->bhd', ws, kc)
        return (cm, nv, m_new), h

    (cm, nv, m), h = lax.scan(step, (c0, n0, m0), tuple(to_chunks(a) for a in (q, k, v, logi, logf)))
    return from_chunks(h), (cm, nv, m)


def gated_delta_chunked(q, k, v, g, beta, s0):
    qc, kc, vc = to_chunks(q), to_chunks(k), to_chunks(v)
    gc = jnp.cumsum(to_chunks(g), axis=-1)
    bc = to_chunks(beta)
    tri = jnp.tril(jnp.ones((CHUNK, CHUNK), bool))
    strict = jnp.tril(jnp.ones((CHUNK, CHUNK), bool), -1)
    decay = jnp.exp(jnp.where(tri, gc[..., :, None] - gc[..., None, :], -jnp.inf))
    kb = kc * bc[..., None]
    mlow = jnp.where(strict, jnp.einsum('nbhtd,nbhsd->nbhts', kb, kc) * decay, 0.0)
    amat = mlow + jnp.eye(CHUNK, dtype=mlow.dtype)
    u = lax.linalg.triangular_solve(amat, vc * bc[..., None], left_side=True, lower=True,
                                    unit_diagonal=True)
    w = lax.linalg.triangular_solve(amat, kb * jnp.exp(gc)[..., None], left_side=True, lower=True,
                                    unit_diagonal=True)
    qk = jnp.where(tri, jnp.einsum('nbhtd,nbhsd->nbhts', qc, kc) * decay, 0.0)
    qg = qc * jnp.exp(gc)[..., None]
    kd = kc * jnp.exp(gc[..., -1:] - gc)[..., None]
    gl = jnp.exp(gc[..., -1])

    def step(s, inp):
        u_i, w_i, qk_i, qg_i, kd_i, gl_i = inp
        v_new = u_i - jnp.einsum('bhtd,bhdv->bhtv', w_i, s)
        o = jnp.einsum('bhtd,bhdv->bhtv', qg_i, s) + jnp.einsum('bhts,bhsv->bhtv', qk_i, v_new)
        s = s * gl_i[..., None, None] + jnp.einsum('bhsd,bhsv->bhdv', kd_i, v_new)
        return s, o

    s, o = lax.scan(step, s0, (u, w, qk, qg, kd, gl))
    return from_chunks(o), s


def ab_mixer(h, w_in, w_out, b_i, b_f, g_m, conv_w, a_log, dt_bias, g_d, c0, n0, m0, s0):
    f32 = jnp.float32
    idx = np.cumsum(AB_SIZES)[:-1].tolist()
    (q_m, k_m, v_m, o_m, i_pre, f_pre,
     q_d, k_d, v_d, z_d, b_pre, a_pre) = jnp.split(h @ w_in, idx, axis=-1)
    qm = split_heads(q_m, H_A).astype(f32) * DQK_A ** -0.5
    km = split_heads(k_m, H_A).astype(f32)
    vm = split_heads(v_m, H_A).astype(f32)
    logi = dir_gates(i_pre) + b_i[None, :, :, None]
    logf = jax.nn.log_sigmoid(dir_gates(f_pre) + b_f[None, :, :, None])
    c0, n0, m0, s0 = c0.astype(f32), n0.astype(f32), m0.astype(f32), s0.astype(f32)
    hf, (cf, nf, mf) = mlstm_chunked(qm, km, vm, logi[:, 0], logf[:, 0], c0[:, 0], n0[:, 0], m0[:, 0])
    hb, (cb, nb, mb) = mlstm_chunked(flip_t(qm), flip_t(km), flip_t(vm), flip_t(logi[:, 1]),
                                     flip_t(logf[:, 1]), c0[:, 1], n0[:, 1], m0[:, 1])
    hm = merge_heads(head_rms(hf + flip_t(hb))) * g_m * jax.nn.sigmoid(o_m.astype(f32))
    qkv = jax.nn.silu(short_conv(jnp.concatenate([q_d, k_d, v_d], axis=-1), conv_w))
    q_d, k_d, v_d = jnp.split(qkv, [H_B * DK_B, 2 * H_B * DK_B], axis=-1)
    qd = l2norm(split_heads(q_d, H_B).astype(f32)) * DK_B ** -0.5
    kd = l2norm(split_heads(k_d, H_B).astype(f32))
    vd = split_heads(v_d, H_B).astype(f32)
    beta = jax.nn.sigmoid(dir_gates(b_pre))
    g = -jnp.exp(a_log.astype(f32))[None, :, :, None] * jax.nn.softplus(dir_gates(a_pre) + dt_bias[None, :, :, None])
    of, sf = gated_delta_chunked(qd, kd, vd, g[:, 0], beta[:, 0], s0[:, 0])
    ob, sb = gated_delta_chunked(flip_t(qd), flip_t(kd), flip_t(vd), flip_t(g[:, 1]),
                                 flip_t(beta[:, 1]), s0[:, 1])
    od = merge_heads(head_rms(of + flip_t(ob)) * g_d) * jax.nn.silu(z_d.astype(f32))
    out = jnp.concatenate([hm, od], axis=-1).astype(h.dtype) @ w_out
    return out, (jnp.stack([cf, cb], axis=1), jnp.stack([nf, nb], axis=1),
                 jnp.stack([mf, mb], axis=1), jnp.stack([sf, sb], axis=1))


def na_project(h, w_in):
    q, k, v = jnp.split(h @ w_in, 3, axis=-1)
    return split_heads(q, H_C), split_heads(k, H_C), split_heads(v, H_C)


def context_attention(q, k, v):
    b, hh, t, d = q.shape
    qb = jnp.moveaxis(q.reshape(b, hh, t // QBLK, QBLK, d), 2, 0)

    def block(qi):
        s = jnp.einsum('bhqd,bhkd->bhqk', qi, k).astype(jnp.float32) * DH_C ** -0.5
        p = jax.nn.softmax(s, axis=-1).astype(v.dtype)
        return jnp.einsum('bhqk,bhkd->bhqd', p, v)

    o = lax.map(block, qb)
    return jnp.moveaxis(o, 0, 2).reshape(b, hh, t, d)


def neighbourhood_attention(q, k, v, kc, vc, rel_bias):
    b, hh, n, d = q.shape
    rows = n // GRID_W
    wr = min(WIN_R, rows)
    r = np.arange(rows)
    krow = np.clip(r - wr // 2, 0, rows - wr)[:, None] + np.arange(wr)[None, :]
    col = np.arange(GRID_W)
    cs = np.clip(col - WIN_C // 2, 0, GRID_W - WIN_C)
    valid = (col[None, :] >= cs[:, None]) & (col[None, :] < cs[:, None] + WIN_C)
    ridx = krow - r[:, None] + WIN_R - 1
    cidx = np.clip(col[None, :] - col[:, None], -(WIN_C - 1), WIN_C - 1) + WIN_C - 1
    bias = rel_bias[:, ridx[:, None, :, None], cidx[None, :, None, :]].astype(jnp.float32)
    bias = jnp.where(valid[None, None, :, None, :], bias, -jnp.inf)
    scale = DH_C ** -0.5
    qg = q.reshape(b, hh, rows, GRID_W, d)
    kband = k.reshape(b, hh, rows, GRID_W, d)[:, :, krow]
    vband = v.reshape(b, hh, rows, GRID_W, d)[:, :, krow]
    s_loc = jnp.einsum('bhrqd,bhrwkd->bhrqwk', qg, kband).astype(jnp.float32) * scale + bias[None]
    s_ctx = jnp.einsum('bhrqd,bhpd->bhrqp', qg, kc).astype(jnp.float32) * scale
    nloc = wr * GRID_W
    p = jax.nn.softmax(jnp.concatenate([s_loc.reshape(b, hh, rows, GRID_W, nloc), s_ctx], axis=-1), axis=-1)
    p_loc = p[..., :nloc].reshape(b, hh, rows, GRID_W, wr, GRID_W).astype(v.dtype)
    p_ctx = p[..., nloc:].astype(vc.dtype)
    o = (jnp.einsum('bhrqwk,bhrwkd->bhrqd', p_loc, vband)
         + jnp.einsum('bhrqp,bhpd->bhrqd', p_ctx, vc))
    return o.reshape(b, hh, n, d)


def setup_inputs(seed: int = 0) -> dict:
    key = jax.random.key(seed)
    ks = iter(jax.random.split(key, 40))

    def nrm(shape, s):
        return jax.random.normal(next(ks), shape, jnp.float32) * s

    inp = {}
    inp['x_prompt'] = nrm((BATCH, SEQ, D_MODEL), 1.0)
    inp['x_sample'] = nrm((DEC_BATCH, DEC_SEQ, D_MODEL), 1.0)
    inp['c'] = nrm((DEC_BATCH, D_MODEL), 1.0)
    inp['state_mlstm_C'] = nrm((DEC_BATCH, N_AB, 2, H_A, DQK_A, DV_A), 0.5)
    inp['state_mlstm_n'] = nrm((DEC_BATCH, N_AB, 2, H_A, DQK_A), 0.5)
    inp['state_mlstm_m'] = nrm((DEC_BATCH, N_AB, 2, H_A), 0.5)
    inp['state_delta_S'] = nrm((DEC_BATCH, N_AB, 2, H_B, DK_B, DV_B), 0.1)
    inp['cache_na_k'] = nrm((DEC_BATCH, N_NA, H_C, PAST_LEN, DH_C), 1.0)
    inp['cache_na_v'] = nrm((DEC_BATCH, N_NA, H_C, PAST_LEN, DH_C), 1.0)
    inp['c_ctx'] = nrm((D_MODEL,), 1.0)
    inp['ada_w'] = nrm((DEPTH, D_MODEL, N_ADA * D_MODEL), 0.5 * D_MODEL ** -0.5)
    inp['ada_b'] = nrm((DEPTH, N_ADA * D_MODEL), 0.02)
    inp['norm_g'] = 1.0 + nrm((DEPTH, 3, D_MODEL), 0.02)
    inp['ffn_wg'] = nrm((DEPTH, 2, D_MODEL, D_FF), D_MODEL ** -0.5)
    inp['ffn_wu'] = nrm((DEPTH, 2, D_MODEL, D_FF), D_MODEL ** -0.5)
    inp['ffn_wd'] = nrm((DEPTH, 2, D_FF, D_MODEL), D_FF ** -0.5)
    inp['ab_w_in'] = nrm((N_AB, D_MODEL, AB_IN), D_MODEL ** -0.5)
    inp['ab_w_out'] = nrm((N_AB, AB_OUT, D_MODEL), AB_OUT ** -0.5)
    inp['mlstm_b_i'] = nrm((N_AB, 2, H_A), 0.1)
    inp['mlstm_b_f'] = 3.0 + nrm((N_AB, 2, H_A), 0.5)
    inp['mlstm_norm_g'] = 1.0 + nrm((N_AB, H_A * DV_A), 0.02)
    inp['delta_conv_w'] = nrm((N_AB, CONV_K, QKV_B), CONV_K ** -0.5)
    inp['delta_a_log'] = jnp.log(jax.random.uniform(next(ks), (N_AB, 2, H_B), jnp.float32, minval=1.0, maxval=16.0))
    dt = jnp.exp(jax.random.uniform(next(ks), (N_AB, 2, H_B), jnp.float32,
                                    minval=math.log(1e-3), maxval=math.log(1e-1)))
    inp['delta_dt_bias'] = dt + jnp.log(-jnp.expm1(-dt))
    inp['delta_norm_g'] = 1.0 + nrm((N_AB, DV_B), 0.02)
    inp['na_w_in'] = nrm((N_NA, D_MODEL, 3 * NA_W), D_MODEL ** -0.5)
    inp['na_w_out'] = nrm((N_NA, NA_W, D_MODEL), NA_W ** -0.5)
    inp['na_rel_bias'] = nrm((N_NA, H_C, 2 * WIN_R - 1, 2 * WIN_C - 1), 0.1)
    inp['final_norm_g'] = 1.0 + nrm((D_MODEL,), 0.02)
    return inp


def reference(x_prompt, x_sample, c, state_mlstm_C, state_mlstm_n, state_mlstm_m, state_delta_S,
              cache_na_k, cache_na_v, c_ctx, ada_w, ada_b, norm_g, ffn_wg, ffn_wu, ffn_wd,
              ab_w_in, ab_w_out, mlstm_b_i, mlstm_b_f, mlstm_norm_g, delta_conv_w, delta_a_log,
              delta_dt_bias, delta_norm_g, na_w_in, na_w_out, na_rel_bias, final_norm_g):
    xp, xs = x_prompt, x_sample
    bp = xp.shape[0]
    f32 = jnp.float32
    new_c, new_n, new_m, new_s, new_k, new_v = [], [], [], [], [], []
    for l in range(DEPTH):
        mp = jnp.split((jax.nn.silu(c_ctx) @ ada_w[l] + ada_b[l])[None, None, :], N_ADA, axis=-1)
        ms = jnp.split((jax.nn.silu(c) @ ada_w[l] + ada_b[l])[:, None, :], N_ADA, axis=-1)
        xp = xp + 0.5 * mp[2] * swiglu(modulate(xp, norm_g[l, 0], mp[0], mp[1]), ffn_wg[l, 0], ffn_wu[l, 0], ffn_wd[l, 0])
        xs = xs + 0.5 * ms[2] * swiglu(modulate(xs, norm_g[l, 0], ms[0], ms[1]), ffn_wg[l, 0], ffn_wu[l, 0], ffn_wd[l, 0])
        hp = modulate(xp, norm_g[l, 1], mp[3], mp[4])
        hs = modulate(xs, norm_g[l, 1], ms[3], ms[4])
        a = l // 2
        if l % 2 == 0:
            prm = (ab_w_in[a], ab_w_out[a], mlstm_b_i[a], mlstm_b_f[a], mlstm_norm_g[a], delta_conv_w[a],
                   delta_a_log[a], delta_dt_bias[a], delta_norm_g[a])
            op, (cn, nn_, mn, sn) = ab_mixer(
                hp, *prm,
                jnp.zeros((bp, 2, H_A, DQK_A, DV_A), f32), jnp.zeros((bp, 2, H_A, DQK_A), f32),
                jnp.zeros((bp, 2, H_A), f32), jnp.zeros((bp, 2, H_B, DK_B, DV_B), f32))
            os_, _ = ab_mixer(hs, *prm, state_mlstm_C[:, a], state_mlstm_n[:, a],
                              state_mlstm_m[:, a], state_delta_S[:, a])
            new_c.append(cn)
            new_n.append(nn_)
            new_m.append(mn)
            new_s.append(sn)
        else:
            qp, kp, vp = na_project(hp, na_w_in[a])
            op = merge_heads(context_attention(qp, kp, vp)) @ na_w_out[a]
            qs, ks_, vs = na_project(hs, na_w_in[a])
            os_ = merge_heads(neighbourhood_attention(qs, ks_, vs, cache_na_k[:, a], cache_na_v[:, a],
                                                      na_rel_bias[a])) @ na_w_out[a]
            new_k.append(kp)
            new_v.append(vp)
        xp = xp + mp[5] * op
        xs = xs + ms[5] * os_
        xp = xp + 0.5 * mp[8] * swiglu(modulate(xp, norm_g[l, 2], mp[6], mp[7]), ffn_wg[l, 1], ffn_wu[l, 1], ffn_wd[l, 1])
        xs = xs + 0.5 * ms[8] * swiglu(modulate(xs, norm_g[l, 2], ms[6], ms[7]), ffn_wg[l, 1], ffn_wu[l, 1], ffn_wd[l, 1])
    y_prompt = rmsnorm(xp, final_norm_g)
    y_sample = rmsnorm(xs, final_norm_g)
    return (y_prompt, y_sample, jnp.stack(new_c, axis=1), jnp.stack(new_n, axis=1),
            jnp.stack(new_m, axis=1), jnp.stack(new_s, axis=1), jnp.stack(new_k, axis=1),
            jnp.stack(new_v, axis=1))
```

```python
import numpy as np
import ml_dtypes
from contextlib import ExitStack
import concourse.bass as bass
import concourse.mybir as mybir
from concourse.bass_utils import run_bass_kernel_spmd

F32 = mybir.dt.float32
BF16 = mybir.dt.bfloat16
AF = mybir.ActivationFunctionType
ALU = mybir.AluOpType
AX = mybir.AxisListType

D = 1024
DFF = 2816
NF = DFF // 128
DEPTH = 2
EPS = 1e-6
NTOK = 1024
BLK = 512
NCORES = 8


class Dep:
    __slots__ = ("name", "w", "rs", "sem", "hist", "excl")

    def __init__(self, name, excl=False):
        self.name = name
        self.excl = excl
        self.w = None
        self.rs = []
        self.sem = None
        self.hist = []


class Op:
    __slots__ = ("eng", "fn", "deps", "dma", "dst", "inc", "val", "idx", "incv")

    def __init__(self, eng, fn, deps, dma, dst, idx, incv=16):
        self.incv = incv
        self.eng = eng
        self.fn = fn
        self.deps = deps
        self.dma = dma
        self.dst = dst
        self.inc = False
        self.val = 0
        self.idx = idx


ENGS = ("pe", "act", "dve", "pool", "sp")


class Prog:
    def __init__(self, nc, es):
        self.nc = nc
        self.es = es
        self.ops = []
        self.ntile = 0

    def sb(self, shape, dt, name=None):
        self.ntile += 1
        return self.es.enter_context(self.nc.sbuf_tensor(name or f"t{self.ntile}", list(shape), dt))

    def psum(self, shape, dt, name=None):
        self.ntile += 1
        return self.es.enter_context(self.nc.psum_tensor(name or f"ps{self.ntile}", list(shape), dt))

    def add(self, eng, fn, reads=(), writes=(), dma=False, incv=16):
        idx = len(self.ops)
        deps = set()
        if any(d.excl for d in reads):
            writes = list(writes) + [d for d in reads if d.excl and d not in writes]
            reads = [d for d in reads if not d.excl]
        for d in reads:
            if d.w is not None:
                deps.add(d.w)
        for d in writes:
            if d.w is not None:
                deps.add(d.w)
            deps.update(d.rs)
        deps.discard(idx)
        last = {}
        keep = set()
        for di in deps:
            y = self.ops[di]
            if y.dma:
                keep.add(di)
            elif y.eng not in last or last[y.eng] < di:
                last[y.eng] = di
        keep.update(last.values())
        deps = keep
        for d in reads:
            d.rs.append(idx)
        for d in writes:
            d.w = idx
            d.rs = []
        op = Op(eng, fn, deps, dma, writes[0] if dma else None, idx, incv)
        self.ops.append(op)
        return op

    def mm(self, out, lhsT, rhs, start, stop, reads, writes):
        return self.add("pe", lambda e: e.matmul(out, lhsT=lhsT, rhs=rhs, start=start, stop=stop), reads, writes)

    def emit(self):
        nc = self.nc
        ops = self.ops
        for op in ops:
            for di in op.deps:
                y = ops[di]
                if y.dma:
                    continue
                if y.eng == "pe" and op.eng == "pe" and not op.dma:
                    continue
                y.inc = True
        engsem = {e: self.es.enter_context(nc.semaphore(f"sem_{e}")) for e in ENGS}
        cnt = {e: 0 for e in ENGS}
        for op in ops:
            if op.dma:
                d = op.dst
                if d.sem is None:
                    d.sem = self.es.enter_context(nc.semaphore(f"dsem_{d.name}"))
                c = (d.hist[-1][1] if d.hist else 0) + op.incv
                d.hist.append((op.idx, c))
                op.val = c
            elif op.inc:
                cnt[op.eng] += 1
                op.val = cnt[op.eng]
        per = {e: [] for e in ENGS}
        for op in ops:
            per[op.eng].append(op)

        def dma_wait_val(y, x):
            v = y.val
            for (i, c) in y.dst.hist:
                if i < x.idx:
                    v = max(v, c)
                else:
                    break
            return v

        def run_engine(ename, eobj):
            waited = {}
            for op in per[ename]:
                need = {}
                for di in op.deps:
                    y = ops[di]
                    if y.dma:
                        s = y.dst.sem
                        v = dma_wait_val(y, op)
                    else:
                        if y.eng == "pe" and op.eng == "pe" and not op.dma:
                            continue
                        s = engsem[y.eng]
                        v = y.val
                    k = id(s)
                    if k not in need or need[k][1] < v:
                        need[k] = (s, v)
                for k, (s, v) in need.items():
                    if waited.get(k, 0) >= v:
                        continue
                    waited[k] = v
                    eobj.wait_ge(s, v)
                ins = op.fn(eobj)
                if op.dma:
                    ins.then_inc(op.dst.sem, op.incv)
                elif op.inc:
                    ins.then_inc(engsem[op.eng], 1)
            last = {}
            for op in per[ename]:
                if op.dma:
                    last[id(op.dst.sem)] = (op.dst.sem, op.dst.hist[-1][1])
            for k, (s, v) in last.items():
                if waited.get(k, 0) < v:
                    eobj.wait_ge(s, v)

        with nc.Block() as block:
            @block.tensor
            def _(e):
                run_engine("pe", e)

            @block.scalar
            def _(e):
                run_engine("act", e)

            @block.vector
            def _(e):
                run_engine("dve", e)

            @block.gpsimd
            def _(e):
                run_engine("pool", e)

            @block.sync
            def _(e):
                run_engine("sp", e)


def build_program(stage="full"):
    nc = bass.Bass("TRN2", target_bir_lowering=False)
    es = ExitStack()
    P = Prog(nc, es)

    def din(name, shape, dt=F32):
        return nc.dram_tensor(name, list(shape), dt, kind="ExternalInput").ap()

    def dout(name, shape, dt=F32):
        return nc.dram_tensor(name, list(shape), dt, kind="ExternalOutput").ap()

    def dint(name, shape, dt):
        return nc.dram_tensor(name, list(shape), dt, kind="Internal").ap()

    xT_in = din("xT_in", [128, 8, NTOK])
    DBG = stage.startswith("dbg")
    ada_w = None if DBG else din("ada_w", [36, 128, 8, 128])
    ada_b = din("ada_b", [128, 36])
    cvec3 = din("cvec3", [128, 8, 2])
    agm_in = dint("agm_in", [128, 72], F32)
    agm_out = dint("agm_out", [4 * 128, 72], F32)
    normg = din("normg", [128, DEPTH * 3 + 1, 8])
    ffn_gu = None if DBG else din("ffn_gu", [DEPTH * 2, NF, 128, 2, 8, 128])
    ffn_d = None if DBG else din("ffn_d", [DEPTH * 2, 8, 128, NF, 128])
    ident_in = din("ident", [128, 128])
    oh_in = din("onehot", [128, 4])
    na_wqk = din("na_wqk", [16, 128, 8, 128])
    na_wv = din("na_wv", [4, 128, 8, 256])
    na_wo = din("na_wo", [8, 128, 8, 128])
    na_wqk_s = din("na_wqk_s", [4, 128, 8, 128])
    na_wv_s = din("na_wv_s", [128, 8, 256])
    na_kc = din("na_kc", [128, 2, 256])
    na_vc = din("na_vc", [128, 2, 4, 64])
    na_bias = din("na_bias", [64, 15, 4, 64])
    abp_wf = din("abp_wf", [20, 128, 8, 128])
    abp_wtA = din("abp_wtA", [4, 128, 8, 256])
    abp_wtB = din("abp_wtB", [4, 128, 8, 256])
    abp_gp = din("abp_gp", [128, 4, 4, 8])
    abp_cw = din("abp_cw", [128, 12, 5])
    abp_ngm = din("abp_ngm", [64, 512])
    abs_wf = din("abs_wf", [5, 128, 8, 128])
    abs_wtA = din("abs_wtA", [1, 128, 8, 256])
    abs_wtB = din("abs_wtB", [1, 128, 8, 256])
    abs_gp = din("abs_gp", [128, 4, 16, 2])
    abs_cw = din("abs_cw", [128, 3, 5])
    abs_ngm = din("abs_ngm", [64, 128])
    ab_ngd = din("ab_ngd", [64, 128])
    ab_wo = din("ab_wo", [8, 128, 8, 128])
    ab_c0 = din("ab_c0", [128, 2, 130])
    ab_m0 = din("ab_m0", [128, 2])
    ab_s0 = din("ab_s0", [128, 2, 128])
    ab_masks = din("ab_masks", [128, 7, 128])
    ab_chind = din("ab_chind", [128, 2])
    c_out = dout("c_out", [2, 64, 2 * 4 * 130])
    s_out = dout("s_out", [2, 128, 2 * 4 * 128])
    m_out = dout("m_out", [4, 4])
    yT_out = dout("yT_out", [128, 8, NTOK])
    kT_out = dout("kT_out", [128, 8, BLK])
    v_out = dout("v_out", [128, 4, 1024])
    ag1_in = dint("ag1_in", [128, 8 * BLK], BF16)
    ag1_out = dint("ag1_out", [4 * 128, 8 * BLK], BF16)
    ag2_in = dint("ag2_in", [128, 2 * 2048], BF16)
    ag2_out = dint("ag2_out", [4 * 128, 2 * 2048], BF16)
    GROUPS = [[0, 1, 2, 3], [4, 5, 6, 7]]

    xT = P.sb([128, 8, NTOK], F32, "xT")
    hT = P.sb([128, 8, NTOK], BF16, "hT")
    ARENA_KB = 104
    arena = P.sb([128, ARENA_KB * 256], F32, "arena")

    def AV(off_kb, shape, dt, parts=128):
        n = 1
        for s_ in shape[1:]:
            n *= s_
        nbytes = n * (2 if dt == BF16 else 4)
        w0 = int(off_kb * 256)
        assert w0 * 4 == int(off_kb * 1024)
        nw = (nbytes + 3) // 4
        assert w0 + nw <= ARENA_KB * 256, (off_kb, shape)
        v = arena[0:parts, w0:w0 + nw]
        if dt == BF16:
            v = v.bitcast(BF16)
        if len(shape) == 2:
            return v
        names = "abcdef"[:len(shape) - 1]
        pat = "p (" + " ".join(names) + ") -> p " + " ".join(names)
        kw = {names[i]: shape[1 + i] for i in range(len(shape) - 1)}
        return v.rearrange(pat, **kw)

    aT = AV(0, [128, NF, NTOK], BF16)
    sq = P.sb([128, 8, BLK], BF16, "sq")
    rstd = P.sb([128, BLK], F32, "rstd")
    tmp = [P.sb([128, BLK], F32, f"tmp{i}") for i in range(2)]
    sg = tmp
    NGU = 3
    gu_s = [P.sb([128, 2, 8, 128], BF16, f"gu{i}") for i in range(NGU)]
    NWD = 2
    wd_s = [P.sb([128, NF, 128], BF16, f"wd{i}") for i in range(NWD)]
    cv = P.sb([128, 8, 2], F32, "cv")
    scT = P.sb([128, 8, 2], BF16, "scT")
    adab = P.sb([128, 36], F32, "adab")
    modpart = P.sb([128, 36, 2], F32, "modpart")
    modall = P.sb([128, 4, 72], F32, "modall")
    ng = P.sb([128, DEPTH * 3 + 1, 8], F32, "ng")
    modT = P.sb([128, DEPTH, 72, 2], F32, "modT")
    Acoef = P.sb([128, DEPTH * 3, 8, 2], F32, "Acoef")
    Gcoef = P.sb([128, DEPTH * 3, 8, 2], F32, "Gcoef")
    ones_bf = P.sb([128, 128], BF16, "ones_bf")
    onespad = P.sb([128, 2, 128], BF16, "onespad")
    ident_f = P.sb([128, 128], F32, "ident_f")
    ident_b = P.sb([128, 128], BF16, "ident_b")
    oh = P.sb([128, 4], F32, "oh")
    epsb = P.sb([128, 4], F32, "epsb")
    mk = P.sb([128, 7, 128], F32, "mk")
    chind = P.sb([128, 2], F32, "chind")
    ones_f = P.sb([128, 128], F32, "ones_f")
    ps2 = [P.psum([128, 1024], F32, f"psb{i}") for i in range(4)]
    ps = [ps2[i // 2][:, (i % 2) * 512:(i % 2 + 1) * 512] for i in range(8)]

    d_x = [[Dep(f"x{b}_{j}") for j in range(8)] for b in range(2)]
    d_h = [[Dep(f"h{b}_{j}") for j in range(8)] for b in range(2)]
    d_a = [Dep(f"a{b}") for b in range(2)]
    d_sq = Dep("sq")
    d_rstd = Dep("rstd")
    d_tmp = [Dep("tmp0"), Dep("tmp1")]
    d_sg = d_tmp
    d_gu = [Dep(f"gu{i}") for i in range(NGU)]
    d_wd = [Dep(f"wd{i}") for i in range(NWD)]
    d_cv = Dep("cv")
    d_sc = Dep("scT")
    d_adab = Dep("adab")
    d_ng = Dep("ng")
    d_mod = Dep("modT")
    d_coef = Dep("coef")
    d_ones = Dep("ones")
    d_ps = [Dep(f"ps{i}", excl=True) for i in range(8)]
    d_y = Dep("yout")

    P.add("dve", lambda e: e.memset(ones_bf[:], 1.0), [], [d_ones])
    P.add("dve", lambda e: e.memset(epsb[:, 0:1], float(D * EPS)), [], [d_ones])
    P.add("dve", lambda e: e.memset(epsb[:, 1:2], 1e-6), [d_ones], [d_ones])
    P.add("dve", lambda e: e.memset(epsb[:, 2:3], 1.0), [d_ones], [d_ones])
    P.add("dve", lambda e: e.memset(epsb[:, 3:4], 1e-6), [d_ones], [d_ones])
    P.add("dve", lambda e: e.memset(ones_f[:], 1.0), [d_ones], [d_ones])
    d_msk = Dep("masks")
    P.add("sp", lambda e: e.dma_start(out=mk[:], in_=ab_masks), [], [d_msk], dma=True)
    P.add("sp", lambda e: e.dma_start(out=chind[:], in_=ab_chind), [d_msk], [d_msk], dma=True)
    P.add("dve", lambda e: e.memset(onespad[:], 0.0), [], [d_ones])
    P.add("dve", lambda e: e.memset(onespad[:, 0, 0:64], 1.0), [d_ones], [d_ones])
    P.add("dve", lambda e: e.memset(onespad[:, 1, 64:128], 1.0), [d_ones], [d_ones])
    d_id = Dep("ident")
    P.add("sp", lambda e: e.dma_start(out=ident_f[:], in_=ident_in), [], [d_id], dma=True)
    P.add("dve", lambda e: e.tensor_copy(out=ident_b[:], in_=ident_f[:]), [d_id], [d_ones])
    d_oh = Dep("oh")
    P.add("sp", lambda e: e.dma_start(out=oh[:], in_=oh_in), [], [d_oh], dma=True)
    for j in range(8):
        P.add("sp", lambda e, j=j: e.dma_start(out=xT[:, j, :], in_=xT_in[:, j, :]), [], [d_x[0][j]], dma=True)
        d_x[1][j].w = d_x[0][j].w
    P.add("sp", lambda e: e.dma_start(out=cv[:], in_=cvec3), [], [d_cv], dma=True)
    P.add("sp", lambda e: e.dma_start(out=adab[:], in_=ada_b), [], [d_adab], dma=True)
    P.add("sp", lambda e: e.dma_start(out=ng[:], in_=normg), [], [d_ng], dma=True)
    P.add("act", lambda e: e.activation(out=scT[:], in_=cv[:], func=AF.Silu), [d_cv], [d_sc])

    cnt_ada = 0
    if DBG:
        P.add("dve", lambda e: e.memset(modT[:].rearrange("p a b c -> p (a b c)"), 0.1), [], [d_mod])
    else:
        for q in range(36):
            s = cnt_ada % NGU
            cnt_ada += 1
            wv = gu_s[s][:, 0, :, :]
            P.add("pool", lambda e, q=q, wv=wv: e.dma_start(out=wv, in_=ada_w[q]), [], [d_gu[s]], dma=True)
            for k in range(8):
                P.mm(ps[7][:, 2 * q:2 * q + 2], wv[:, k, :], scT[:, k, :], k == 0, k == 7, [d_gu[s], d_sc], [d_ps[7]])
        d_mp = Dep("modpart")
        for v in range(2):
            P.add("dve", lambda e, v=v: e.tensor_tensor(
                out=modpart[:, :, v], in0=ps[7][:, 0:72].rearrange("p (q v) -> p q v", v=2)[:, :, v],
                in1=adab[:], op=ALU.add), [d_ps[7], d_adab], [d_mp])
        d_agmi, d_agmo, d_mall = Dep("agmi"), Dep("agmo"), Dep("modall")
        P.add("sp", lambda e: e.dma_start(out=agm_in, in_=modpart[:].rearrange("p q v -> p (q v)")), [d_mp], [d_agmi], dma=True)
        P.add("pool", lambda e: e.collective_compute("AllGather", ALU.bypass, replica_groups=GROUPS,
                                                     ins=[agm_in], outs=[agm_out]),
              [d_agmi], [d_agmo], dma=True, incv=1)
        P.add("sp", lambda e: e.dma_start(out=modall[:], in_=agm_out.rearrange("(r p) c -> p r c", p=128)),
              [d_agmo], [d_mall], dma=True)
        for l in range(DEPTH):
            srcv = modall[:, :, l * 36:(l + 1) * 36].rearrange("p r (j v) -> p r j v", v=2)
            dstv = modT[:, l].rearrange("p (r j) v -> p r j v", r=4)
            for v in range(2):
                P.add("dve", lambda e, srcv=srcv, dstv=dstv, v=v: e.tensor_copy(out=dstv[:, :, :, v], in_=srcv[:, :, :, v]),
                      [d_mall], [d_mod])
    for l in range(DEPTH):
        for s_ in range(3):
            ls = l * 3 + s_
            for v in range(2):
                sc_ap = modT[:, l, (3 * s_ + 1) * 8:(3 * s_ + 2) * 8, v]
                P.add("dve", lambda e, ls=ls, v=v, sc_ap=sc_ap: e.scalar_tensor_tensor(
                    out=Acoef[:, ls, :, v], in0=sc_ap, scalar=1.0, in1=ng[:, ls, :], op0=ALU.add, op1=ALU.mult),
                    [d_mod, d_ng], [d_coef])
                P.add("dve", lambda e, ls=ls, v=v: e.tensor_scalar(
                    out=Acoef[:, ls, :, v], in0=Acoef[:, ls, :, v], scalar1=32.0, scalar2=None, op0=ALU.mult),
                    [d_coef], [d_coef])
                g_ap = modT[:, l, (3 * s_ + 2) * 8:(3 * s_ + 3) * 8, v]
                fac = 1.0 if s_ == 1 else 0.5
                P.add("dve", lambda e, ls=ls, v=v, g_ap=g_ap, fac=fac: e.tensor_scalar(
                    out=Gcoef[:, ls, :, v], in0=g_ap, scalar1=fac, scalar2=None, op0=ALU.mult),
                    [d_mod], [d_coef])

    def rstd_from_ps6():
        P.add("act", lambda e: e.activation(out=rstd[:], in_=ps[6], func=AF.Ln, bias=epsb[:, 0:1], scale=1.0),
              [d_ps[6], d_ones], [d_rstd])
        P.add("act", lambda e: e.activation(out=rstd[:], in_=rstd[:], func=AF.Exp, scale=-0.5),
              [d_rstd], [d_rstd])

    def modulate(l, s_, b):
        ls = l * 3 + s_
        blk = slice(b * BLK, (b + 1) * BLK)
        P.add("act", lambda e: e.activation(out=sq[:], in_=xT[:, :, blk], func=AF.Square),
              d_x[b], [d_sq])
        for k in range(8):
            P.mm(ps[6], ones_bf[:], sq[:, k, :], k == 0, k == 7, [d_ones, d_sq], [d_ps[6]])
        rstd_from_ps6()
        for j in range(8):
            t = j % 2
            P.add("dve", lambda e, j=j, t=t: e.scalar_tensor_tensor(
                out=tmp[t][:], in0=xT[:, j, blk], scalar=Acoef[:, ls, j, b:b + 1], in1=rstd[:],
                op0=ALU.mult, op1=ALU.mult), [d_x[b][j], d_coef, d_rstd], [d_tmp[t]])
            P.add("act", lambda e, j=j, t=t: e.activation(
                out=hT[:, j, blk], in_=tmp[t][:], func=AF.Identity,
                bias=modT[:, l, 3 * s_ * 8 + j, b:b + 1], scale=1.0), [d_tmp[t], d_mod], [d_h[b][j]])

    state = {"gu": cnt_ada, "wd": 0, "psA": 0, "psB": 0, "sg": 0}

    def ffn(l, which):
        fi = l * 2 + which
        s_ = 0 if which == 0 else 2
        ls = l * 3 + s_
        for b in range(2):
            modulate(l, s_, b)
        for f in range(NF):
            s = state["gu"] % NGU
            state["gu"] += 1
            P.add("pool", lambda e, f=f, s=s: e.dma_start(out=gu_s[s][:], in_=ffn_gu[fi, f]), [], [d_gu[s]], dma=True)
            for b in range(2):
                blk = slice(b * BLK, (b + 1) * BLK)
                pa = (state["psA"] % 2) * 2
                state["psA"] += 1
                for w in range(2):
                    for k in range(8):
                        P.mm(ps[pa + w], gu_s[s][:, w, k, :], hT[:, k, blk], k == 0, k == 7,
                             [d_gu[s]] + d_h[b], [d_ps[pa + w]])
                t = state["sg"] % 2
                state["sg"] += 1
                P.add("act", lambda e, pa=pa, t=t: e.activation(out=sg[t][:], in_=ps[pa], func=AF.Silu),
                      [d_ps[pa]], [d_sg[t]])
                P.add("dve", lambda e, pa=pa, t=t, f=f, blk=blk: e.tensor_tensor(
                    out=aT[:, f, blk], in0=sg[t][:], in1=ps[pa + 1], op=ALU.mult),
                    [d_sg[t], d_ps[pa + 1]], [d_a[b]])
        for j in range(8):
            s = state["wd"] % NWD
            state["wd"] += 1
            P.add("pool", lambda e, j=j, s=s: e.dma_start(out=wd_s[s][:], in_=ffn_d[fi, j]), [], [d_wd[s]], dma=True)
            for b in range(2):
                blk = slice(b * BLK, (b + 1) * BLK)
                pb = 4 + (state["psB"] % 2)
                state["psB"] += 1
                for f in range(NF):
                    P.mm(ps[pb], wd_s[s][:, f, :], aT[:, f, blk], f == 0, f == NF - 1,
                         [d_wd[s], d_a[b]], [d_ps[pb]])
                P.add("dve", lambda e, j=j, b=b, blk=blk, pb=pb: e.scalar_tensor_tensor(
                    out=xT[:, j, blk], in0=ps[pb], scalar=Gcoef[:, ls, j, b:b + 1], in1=xT[:, j, blk],
                    op0=ALU.mult, op1=ALU.add), [d_ps[pb], d_coef, d_x[b][j]], [d_x[b][j]])

    def out_proj(l, wo_dram, mix, d_mix, b):
        ls = l * 3 + 1
        blk = slice(b * BLK, (b + 1) * BLK)
        for j in range(8):
            s = state["gu"] % NGU
            state["gu"] += 1
            wv = gu_s[s][:, 0, :, :]
            P.add("pool", lambda e, j=j, wv=wv: e.dma_start(out=wv, in_=wo_dram[j]), [], [d_gu[s]], dma=True)
            pb = 4 + (state["psB"] % 2)
            state["psB"] += 1
            for c in range(8):
                P.mm(ps[pb], wv[:, c, :], mix[:, c, :], c == 0, c == 7, [d_gu[s]] + d_mix, [d_ps[pb]])
            P.add("dve", lambda e, j=j, pb=pb: e.scalar_tensor_tensor(
                out=xT[:, j, blk], in0=ps[pb], scalar=Gcoef[:, ls, j, b:b + 1], in1=xT[:, j, blk],
                op0=ALU.mult, op1=ALU.add), [d_ps[pb], d_coef, d_x[b][j]], [d_x[b][j]])

    def allgather(src_ap_sb, d_src, ag_in_view, ag_in, ag_out, d_agin, d_agout):
        P.add("sp", lambda e: e.dma_start(out=ag_in_view, in_=src_ap_sb), d_src, [d_agin], dma=True)
        P.add("pool", lambda e: e.collective_compute("AllGather", ALU.bypass, replica_groups=GROUPS,
                                                     ins=[ag_in], outs=[ag_out]),
              [d_agin], [d_agout], dma=True, incv=1)

    def na_layer(l, stop=None):
        for b in range(2):
            modulate(l, 1, b)
        if stop == "P0":
            return
        d_ag1i, d_ag1o = Dep("ag1i"), Dep("ag1o")
        allgather(hT[:, :, BLK:2 * BLK], d_h[1], ag1_in.rearrange("p (k t) -> p k t", k=8), ag1_in, ag1_out, d_ag1i, d_ag1o)
        if stop == "P0b":
            return
        qT_p = AV(0, [128, 8, BLK], BF16)
        kT_p = AV(8, [128, 8, BLK], BF16)
        kT32 = AV(16, [128, 8, BLK], F32)
        v32 = AV(32, [128, 4, 1024], F32)
        Vpad = AV(48, [128, 4, 8, 2, 128], BF16)
        PTp = [AV(64 + 2 * i, [128, 4, 256], BF16) for i in range(2)]
        OT_p = AV(68, [128, 8, BLK], BF16)
        rcp = [AV(76 + i, [128, 256], F32) for i in range(2)]
        d_q, d_k, d_k32, d_v32, d_vpad = Dep("qTp"), Dep("kTp"), Dep("kT32"), Dep("v32"), Dep("Vpad")
        d_pt = [Dep("PTp0"), Dep("PTp1")]
        d_ot = Dep("OTp")
        d_rcp = [Dep("rcp0"), Dep("rcp1")]
        for dd in (d_q, d_k, d_k32, d_v32, d_vpad, d_pt[0], d_pt[1], d_ot, d_rcp[0], d_rcp[1]):
            dd.rs = list(d_a[0].rs) + list(d_a[1].rs)
            dd.w = d_a[1].w
        for i_ in range(4):
            P.add("dve", lambda e, i_=i_: e.memset(AV(48 + 4 * i_, [128, 2048], BF16), 0.0), [], [d_vpad])
        pj = 0
        for c in range(16):
            s = state["gu"] % NGU
            state["gu"] += 1
            wv = gu_s[s][:, 0, :, :]
            P.add("pool", lambda e, c=c, wv=wv: e.dma_start(out=wv, in_=na_wqk[c]), [], [d_gu[s]], dma=True)
            pb = 6 + (pj % 2)
            pj += 1
            for k in range(8):
                P.mm(ps[pb], wv[:, k, :], hT[:, k, 0:BLK], k == 0, k == 7, [d_gu[s]] + d_h[0], [d_ps[pb]])
            if c < 8:
                P.add("act", lambda e, c=c, pb=pb: e.activation(out=qT_p[:, c, :], in_=ps[pb], func=AF.Identity,
                                                              scale=0.125), [d_ps[pb]], [d_q])
            else:
                P.add("act", lambda e, c=c, pb=pb: e.activation(out=kT32[:, c - 8, :], in_=ps[pb], func=AF.Identity),
                      [d_ps[pb]], [d_k32])
                P.add("dve", lambda e, c=c, pb=pb: e.tensor_copy(out=kT_p[:, c - 8, :], in_=ps[pb]),
                      [d_ps[pb]], [d_k])
        if stop == "P0c":
            return
        P.add("sp", lambda e: e.dma_start(out=kT_out, in_=kT32), [d_k32], [d_y], dma=True)
        for qv in range(4):
            s = state["gu"] % NGU
            state["gu"] += 1
            wv = gu_s[s][:].rearrange("p a b c -> p (a b c)").rearrange("p (k n) -> p k n", k=8)
            P.add("pool", lambda e, qv=qv, wv=wv: e.dma_start(out=wv, in_=na_wv[qv]), [], [d_gu[s]], dma=True)
            for tt in range(4):
                pb = 6 + (pj % 2)
                pj += 1
                for k in range(8):
                    P.mm(ps[pb][:, 0:256], hT[:, k, tt * 128:(tt + 1) * 128], wv[:, k, :], k == 0, k == 7,
                         [d_gu[s]] + d_h[0], [d_ps[pb]])
                P.add("act", lambda e, tt=tt, qv=qv, pb=pb: e.activation(
                    out=v32[:, tt, qv * 256:(qv + 1) * 256], in_=ps[pb][:, 0:256], func=AF.Identity), [d_ps[pb]], [d_v32])
                for e_ in range(2):
                    src = ps[pb][:, 0:256].rearrange("p (c e d) -> p c e d", c=2, e=2)[:, :, e_, :]
                    P.add("dve", lambda e, tt=tt, qv=qv, e_=e_, src=src: e.tensor_copy(
                        out=Vpad[:, tt, qv * 2:(qv + 1) * 2, e_, e_ * 64:(e_ + 1) * 64], in_=src),
                        [d_ps[pb]], [d_vpad])
        P.add("sp", lambda e: e.dma_start(out=v_out, in_=v32), [d_v32], [d_y], dma=True)
        if stop == "P1":
            return
        it = 0
        for sq_ in range(2):
            t0 = sq_ * 256
            for c in range(8):
                S2 = ps2[it % 2]
                dS = [d_ps[(it % 2) * 2], d_ps[(it % 2) * 2 + 1]]
                for hh in range(2):
                    for sb in range(2):
                        i4 = hh * 2 + sb
                        P.mm(S2[:, i4 * 256:(i4 + 1) * 256],
                             kT_p[hh * 64:(hh + 1) * 64, c, t0 + sb * 128:t0 + (sb + 1) * 128],
                             qT_p[hh * 64:(hh + 1) * 64, c, t0:t0 + 256], True, True,
                             [d_k, d_q], [dS[i4 // 2]])
                pt = PTp[it % 2]
                P.add("act", lambda e, S2=S2, pt=pt: e.activation(
                    out=pt.rearrange("p a b -> p (a b)"), in_=S2[:, :], func=AF.Exp), dS, [d_pt[it % 2]])
                pb = 4 + (it % 2)
                n = 0
                for hh in range(2):
                    for sb in range(2):
                        P.mm(ps[pb][:, 0:256], Vpad[:, sq_ * 2 + sb, c, hh, :], pt[:, hh * 2 + sb, :], n == 0, n == 3,
                             [d_vpad, d_pt[it % 2]], [d_ps[pb]])
                        n += 1
                n = 0
                for hh in range(2):
                    for sb in range(2):
                        P.mm(ps[pb][:, 256:512], onespad[:, hh, :], pt[:, hh * 2 + sb, :], n == 0, n == 3,
                             [d_ones, d_pt[it % 2]], [d_ps[pb]])
                        n += 1
                rc = rcp[it % 2]
                P.add("dve", lambda e, rc=rc, pb=pb: e.reciprocal(out=rc, in_=ps[pb][:, 256:512]),
                      [d_ps[pb]], [d_rcp[it % 2]])
                P.add("dve", lambda e, rc=rc, pb=pb, c=c, t0=t0: e.tensor_tensor(
                    out=OT_p[:, c, t0:t0 + 256], in0=ps[pb][:, 0:256], in1=rc, op=ALU.mult),
                    [d_ps[pb], d_rcp[it % 2]], [d_ot])
                it += 1
        if stop == "P2":
            return
        out_proj(l, na_wo, OT_p, [d_ot], 0)
        if stop == "P3":
            return

        hfull = AV(0, [128, 8, 2048], BF16)
        qT_s = AV(32, [128, 2, 2048], BF16)
        kTpad = AV(40, [128, 2, 2, 2048], BF16)
        v64 = AV(56, [128, 32, 2, 2, 128], BF16)
        biasT = AV(0, [128, 15, 4, 64], BF16)
        kcpad = AV(8, [128, 2, 2, 256], BF16)
        vcb = AV(10, [128, 2, 4, 64], BF16)
        Vcpad = AV(11, [128, 2, 2, 2, 128], BF16)
        PTl = [AV(13 + 2 * i, [128, 2, 8, 64], BF16) for i in range(2)]
        PTc = [AV(17 + 0.5 * i, [128, 2, 2, 64], BF16) for i in range(2)]
        rcs = [AV(18 + 0.25 * i, [128, 64], F32) for i in range(2)]
        OT_s = AV(20, [128, 2, 2048], BF16)
        d_hf = Dep("hfull")
        d_qs, d_ks, d_v64 = Dep("qTs"), Dep("kTs"), Dep("v64")
        d_bias, d_kc, d_vcb, d_vcp = Dep("biasT"), Dep("kcT"), Dep("vcb"), Dep("Vcpad")
        d_ptl = [Dep("PTl0"), Dep("PTl1")]
        d_ptc = [Dep("PTc0"), Dep("PTc1")]
        d_rcs = [Dep("rcs0"), Dep("rcs1")]
        d_ots = Dep("OTs")
        last_prompt = [d_q, d_k, d_k32, d_v32, d_vpad, d_pt[0], d_pt[1], d_ot, d_rcp[0], d_rcp[1]]

        def users_of(deplist):
            rs = []
            for o in deplist:
                rs += list(o.rs)
                if o.w is not None:
                    rs.append(o.w)
            return rs
        for dd in (d_hf, d_qs, d_ks, d_v64):
            dd.rs = users_of(last_prompt) + users_of(d_a)
            dd.w = None
        for m in range(4):
            P.add("sp", lambda e, m=m: e.dma_start(
                out=hfull[:, :, m * BLK:(m + 1) * BLK],
                in_=ag1_out[m * 128:(m + 1) * 128, :].rearrange("p (k t) -> p k t", k=8)),
                [d_ag1o], [d_hf], dma=True)
        for i_ in range(4):
            P.add("dve", lambda e, i_=i_: e.memset(AV(40 + 4 * i_, [128, 2048], BF16), 0.0), [], [d_ks])
        for i_ in range(16):
            P.add("pool", lambda e, i_=i_: e.memset(AV(56 + 2 * i_, [128, 1024], BF16), 0.0), [], [d_v64])
        if stop == "S1":
            return
        for c4 in range(4):
            s = state["gu"] % NGU
            state["gu"] += 1
            wv = gu_s[s][:, 0, :, :]
            P.add("pool", lambda e, c4=c4, wv=wv: e.dma_start(out=wv, in_=na_wqk_s[c4]), [], [d_gu[s]], dma=True)
            for tb in range(4):
                pb = 6 + (pj % 2)
                pj += 1
                for k in range(8):
                    P.mm(ps[pb], wv[:, k, :], hfull[:, k, tb * BLK:(tb + 1) * BLK], k == 0, k == 7,
                         [d_gu[s], d_hf], [d_ps[pb]])
                if c4 < 2:
                    P.add("act", lambda e, c4=c4, tb=tb, pb=pb: e.activation(
                        out=qT_s[:, c4, tb * BLK:(tb + 1) * BLK], in_=ps[pb], func=AF.Identity, scale=0.125),
                        [d_ps[pb]], [d_qs])
                else:
                    P.add("act", lambda e, c4=c4, tb=tb, pb=pb: e.activation(
                        out=kTpad[0:64, c4 - 2, 0, tb * BLK:(tb + 1) * BLK], in_=ps[pb][0:64, :], func=AF.Identity),
                        [d_ps[pb]], [d_ks])
                    P.add("act", lambda e, c4=c4, tb=tb, pb=pb: e.activation(
                        out=kTpad[64:128, c4 - 2, 1, tb * BLK:(tb + 1) * BLK], in_=ps[pb][64:128, :],
                        func=AF.Identity), [d_ps[pb]], [d_ks])
        s = state["gu"] % NGU
        state["gu"] += 1
        wvs = gu_s[s][:].rearrange("p a b c -> p (a b c)").rearrange("p (k n) -> p k n", k=8)
        P.add("pool", lambda e: e.dma_start(out=wvs, in_=na_wv_s), [], [d_gu[s]], dma=True)
        for rp in range(16):
            pb = 6 + (pj % 2)
            pj += 1
            for r2 in range(2):
                rho = rp * 2 + r2
                for k in range(8):
                    P.mm(ps[pb][0:64, r2 * 256:(r2 + 1) * 256], hfull[:, k, rho * 64:(rho + 1) * 64], wvs[:, k, :],
                         k == 0, k == 7, [d_gu[s], d_hf], [d_ps[pb]])
            for e_ in range(2):
                src = ps[pb][0:64, :].rearrange("p (r c e d) -> p r c e d", r=2, c=2, e=2)[:, :, :, e_, :]
                P.add("act", lambda e, rp=rp, e_=e_, src=src: e.activation(
                    out=v64[0:64, rp * 2:rp * 2 + 2, :, e_, e_ * 64:(e_ + 1) * 64], in_=src, func=AF.Identity),
                    [d_ps[pb]], [d_v64])
        if stop == "S2":
            return
        for dd in (d_bias, d_kc, d_vcb, d_vcp, d_ptl[0], d_ptl[1], d_ptc[0], d_ptc[1], d_rcs[0], d_rcs[1], d_ots):
            dd.rs = users_of([d_hf])
            dd.w = None
        P.add("dve", lambda e: e.memset(AV(0, [128, 2048], BF16), 0.0), [], [d_bias])
        P.add("dve", lambda e: e.memset(AV(4, [128, 2048], BF16), 0.0), [], [d_bias])
        P.add("pool", lambda e: e.dma_start(out=biasT[0:64], in_=na_bias), [d_bias], [d_bias], dma=True)
        P.add("dve", lambda e: e.memset(AV(8, [128, 1024], BF16), 0.0), [], [d_kc])
        for hh in range(2):
            P.add("pool", lambda e, hh=hh: e.dma_start(out=kcpad[hh * 64:(hh + 1) * 64, :, hh, :],
                                                       in_=na_kc[hh * 64:(hh + 1) * 64]),
                  [d_kc], [d_kc], dma=True)
        P.add("pool", lambda e: e.dma_start(out=vcb, in_=na_vc), [], [d_vcb], dma=True)
        P.add("dve", lambda e: e.memset(AV(11, [128, 1024], BF16), 0.0), [], [d_vcp])
        for e_ in range(2):
            for c in range(2):
                P.add("dve", lambda e, e_=e_, c=c: e.tensor_copy(
                    out=Vcpad[:, :, c, e_, e_ * 64:(e_ + 1) * 64], in_=vcb[:, :, 2 * c + e_, :]),
                    [d_vcb, d_vcp], [d_vcp])
        for i in range(2):
            P.add("dve", lambda e, i=i: e.memset(AV(13 + 2 * i, [128, 1024], BF16), 0.0), [], [d_ptl[i]])
        it = 0
        for c in range(2):
            for R in range(32):
                k0 = min(max(R - 4, 0), 24)
                bi = it % 2
                S2 = ps2[bi]
                dS = [d_ps[bi * 2], d_ps[bi * 2 + 1]]
                for hh in range(2):
                    for i in range(8):
                        rho = k0 + i
                        dr = rho - R
                        o_ap = S2[0:64, hh * 512 + i * 64: hh * 512 + (i + 1) * 64]
                        P.mm(o_ap, kTpad[:, c, hh, rho * 64:(rho + 1) * 64],
                             qT_s[:, c, R * 64:(R + 1) * 64], True, False,
                             [d_ks, d_qs], [dS[hh]])
                        P.mm(o_ap, ident_b[:, 0:64], biasT[:, dr + 7, 2 * c + hh, :], False, True,
                             [d_ones, d_bias], [dS[hh]])
                pc = 6 + bi
                for hh in range(2):
                    for sb in range(2):
                        i4 = hh * 2 + sb
                        P.mm(ps[pc][:, i4 * 64:(i4 + 1) * 64],
                             kcpad[:, c, hh, sb * 128:(sb + 1) * 128],
                             qT_s[:, c, R * 64:(R + 1) * 64], True, True,
                             [d_kc, d_qs], [d_ps[pc]])
                ptl, ptc = PTl[bi], PTc[bi]
                P.add("act", lambda e, S2=S2, ptl=ptl: e.activation(
                    out=ptl[0:64].rearrange("p a b c -> p (a b c)"), in_=S2[0:64, :], func=AF.Exp), dS, [d_ptl[bi]])
                P.add("act", lambda e, pc=pc, ptc=ptc: e.activation(
                    out=ptc.rearrange("p a b c -> p (a b c)"), in_=ps[pc][:, 0:256], func=AF.Exp),
                    [d_ps[pc]], [d_ptc[bi]])
                pb = 4 + bi
                for which in range(2):
                    n = 0
                    o_ap = ps[pb][:, which * 64:(which + 1) * 64]
                    for hh in range(2):
                        for i in range(8):
                            rho = k0 + i
                            lhs = v64[:, rho, c, hh, :] if which == 0 else onespad[:, hh, :]
                            P.mm(o_ap, lhs, ptl[:, hh, i, :], n == 0, False,
                                 [d_v64 if which == 0 else d_ones, d_ptl[bi]], [d_ps[pb]])
                            n += 1
                    for hh in range(2):
                        for sb in range(2):
                            lhs = Vcpad[:, sb, c, hh, :] if which == 0 else onespad[:, hh, :]
                            P.mm(o_ap, lhs, ptc[:, hh, sb, :], False, (hh == 1 and sb == 1),
                                 [d_vcp if which == 0 else d_ones, d_ptc[bi]], [d_ps[pb]])
                rc = rcs[bi]
                P.add("dve", lambda e, rc=rc, pb=pb: e.reciprocal(out=rc, in_=ps[pb][:, 64:128]),
                      [d_ps[pb]], [d_rcs[bi]])
                P.add("dve", lambda e, rc=rc, pb=pb, c=c, R=R: e.tensor_tensor(
                    out=OT_s[:, c, R * 64:(R + 1) * 64], in0=ps[pb][:, 0:64], in1=rc, op=ALU.mult),
                    [d_ps[pb], d_rcs[bi]], [d_ots])
                it += 1
        if stop == "S3":
            return
        d_ag2i, d_ag2o = Dep("ag2i"), Dep("ag2o")
        allgather(OT_s, [d_ots], ag2_in.rearrange("p (c t) -> p c t", c=2), ag2_in, ag2_out, d_ag2i, d_ag2o)
        cand = AV(28, [128, 4, 8, BLK], BF16)
        mixS = AV(60, [128, 8, BLK], BF16)
        d_cand, d_mixs = Dep("cand"), Dep("mixS")
        d_cand.rs = users_of([d_qs, d_ks, d_v64])
        d_mixs.rs = users_of([d_v64])
        for m in range(4):
            for c in range(2):
                P.add("sp", lambda e, m=m, c=c: e.dma_start(
                    out=cand[:, :, 2 * m + c, :],
                    in_=ag2_out[m * 128:(m + 1) * 128, c * 2048:(c + 1) * 2048].rearrange("p (r t) -> p r t", r=4)),
                    [d_ag2o], [d_cand], dma=True)
        for rr in range(4):
            if rr == 0:
                P.add("dve", lambda e: e.tensor_scalar(
                    out=mixS.rearrange("p a b -> p (a b)"), in0=cand[:, 0].rearrange("p a b -> p (a b)"),
                    scalar1=oh[:, 0:1], scalar2=None, op0=ALU.mult), [d_cand, d_oh], [d_mixs])
            else:
                P.add("dve", lambda e, rr=rr: e.scalar_tensor_tensor(
                    out=mixS.rearrange("p a b -> p (a b)"), in0=cand[:, rr].rearrange("p a b -> p (a b)"),
                    scalar=oh[:, rr:rr + 1], in1=mixS.rearrange("p a b -> p (a b)"), op0=ALU.mult, op1=ALU.add),
                    [d_cand, d_oh, d_mixs], [d_mixs])
        out_proj(l, na_wo, mixS, [d_mixs], 1)
        tail = []
        for o in (d_cand, d_mixs, d_ots, d_v64, d_ks, d_qs, d_bias, d_kc, d_vcb, d_vcp, d_ptl[0], d_ptl[1],
                  d_ptc[0], d_ptc[1], d_rcs[0], d_rcs[1]):
            tail += list(o.rs)
            if o.w is not None:
                tail.append(o.w)
        for b in range(2):
            d_a[b].rs = list(d_a[b].rs) + tail

    def ab_stage(cfg):
        nh, T, seqs, src, d_src = cfg["nh"], cfg["T"], cfg["seqs"], cfg["src"], cfg["d_src"]
        wf, wtA, wtB, gp_d, cw_d, ngm_d, ngd_d = cfg["wf"], cfg["wtA"], cfg["wtB"], cfg["gp"], cfg["cw"], cfg["ngm"], cfg["ngd"]
        d_mix = cfg["d_mix"]
        NT, NCHK, n2 = T // 128, T // 64, 2 * nh
        qmT = AV(0, [128, nh, T], BF16)
        kmT = AV(4, [128, nh, T], BF16)
        qnT = AV(8, [128, nh, T], BF16)
        knT = AV(12, [128, nh, T], BF16)
        vdT = AV(16, [128, nh, T], BF16)
        vaug = AV(20, [128, NT * nh, 130], BF16)
        kmtok = AV(24.5, [128, NT * nh, 64], BF16)
        gates = AV(26.5, [128, NT, 4, n2], F32)
        NQ = 13
        gt = AV(27, [128, NT, NQ, n2], F32)
        om = AV(29, [64, NCHK * nh, 128], BF16, parts=64)
        zd = AV(37, [64, NCHK * nh, 128], BF16, parts=64)
        hacc = AV(45, [64, NCHK * nh, 128], BF16, parts=64)
        oacc = AV(53, [64, NCHK * nh, 128], BF16, parts=64)
        XL = T + 4 * len(seqs)
        xin = AV(45, [128, XL], BF16)
        caccs = [AV(49.5, [128, 512], F32), AV(54, [128, 512], F32)]
        csqs = [AV(49.5, [128, 512], BF16), AV(54, [128, 512], BF16)]
        knT32 = sq[:].rearrange("p a b -> p (a b)").bitcast(F32).rearrange("p (h t) -> p h t", h=nh)
        ctmps = [AV(51.5, [128, 512], F32), AV(56, [128, 512], F32)]
        Caug = AV(78, [128, 2, nh, 130], F32)
        Cbf = AV(82.5, [128, 2, nh, 130], BF16)
        Sst = AV(85, [128, 2, nh, 128], F32)
        Sbf = AV(89, [128, 2, nh, 128], BF16)
        hflat = hT[:].rearrange("p a b -> p (a b)")

        def AVH(off_kb, shape, dt, parts=128):
            n = 1
            for s_ in shape[1:]:
                n *= s_
            nbytes = n * (2 if dt == BF16 else 4)
            e0 = int(off_kb * 512)
            ne = (nbytes + 1) // 2
            assert e0 + ne <= 8 * NTOK
            v = hflat[0:parts, e0:e0 + ne]
            if dt == F32:
                v = v.bitcast(F32)
            if len(shape) == 2:
                return v
            names = "abcdef"[:len(shape) - 1]
            pat = "p (" + " ".join(names) + ") -> p " + " ".join(names)
            kw_ = {names[i]: shape[1 + i] for i in range(len(shape) - 1)}
            return v.rearrange(pat, **kw_)

        def alloc_set(sn):
            o_ = [91.0 if sn == 0 else 0.0]
            AVx = AV if sn == 0 else AVH

            def TMP(kb, shape, dt, parts=128):
                v = AVx(o_[0], shape, dt, parts)
                o_[0] += kb
                return v
            T = {}
            for nm in ("rhs1", "rhs2", "rhs3", "rhsD", "EB", "DT", "EG", "FA", "FB", "FC"):
                T[nm] = TMP(0.5, [128, 128], F32)
            for nm in ("swT", "qbT", "QKdT", "wT", "qgT", "vnew"):
                T[nm] = TMP(0.25, [128, 128], BF16)
            T["kw"] = TMP(0.25, [128, 2, 64], BF16)
            T["wsc"] = TMP(0.25, [128, 2], F32)
            T["Pm"] = [TMP(0.5, [128, 128], F32) for _ in range(2)]
            T["PTm"] = [TMP(0.5, [128, 128], F32) for _ in range(2)]
            T["X"] = TMP(1.0, [128, 256], F32)
            T["kd"] = TMP(0.5, [128, 2, 128], BF16)
            T["kdcc"] = TMP(0.25, [128, 2], F32)
            T["dn"] = TMP(0.25, [128, 2], F32)
            T["_end"] = o_[0]
            return T
        TS = [alloc_set(0), alloc_set(1)]
        o_ = [TS[0]["_end"]]

        def TMP(kb, shape, dt, parts=128):
            v = AV(o_[0], shape, dt, parts)
            o_[0] += kb
            return v
        assert o_[0] <= ARENA_KB, o_[0]
        assert TS[1]["_end"] <= 16.0
        gpt = P.sb([128, 4, NT, n2], F32, f"gpt{nh}")
        expA = P.sb([128, NT, n2], F32, f"expA{nh}")
        cwt = P.sb([128, 3 * nh, 5], F32, f"cwt{nh}")
        ngm = P.sb([64, nh * 128], F32, f"ngm{nh}")
        ngd = P.sb([64, 128], F32, f"ngd{nh}")
        D_ = {}

        ALIAS = {"hacc": ["xin", "cacc0", "ctmp0", "cacc1", "ctmp1"], "oacc": ["xin", "cacc0", "ctmp0", "cacc1", "ctmp1"]}

        def dp(name):
            if name not in D_:
                D_[name] = Dep(f"ab{nh}_{name}")
                rs = list(cfg["prev_users"]) + list(cfg["extra_users"]())
                if name.endswith("_1"):
                    for b_ in range(2):
                        for j_ in range(8):
                            rs += list(d_h[b_][j_].rs)
                            if d_h[b_][j_].w is not None:
                                rs.append(d_h[b_][j_].w)
                for o in ALIAS.get(name, []):
                    if o in D_:
                        rs += list(D_[o].rs)
                        if D_[o].w is not None:
                            rs.append(D_[o].w)
                D_[name].rs = rs
            return D_[name]
        d_par = dp("par")
        d_par.rs = []
        P.add("sp", lambda e: e.dma_start(out=gpt[:], in_=gp_d), [], [d_par], dma=True)
        d_cw = Dep(f"cw{nh}")
        P.add("sp", lambda e: e.dma_start(out=cwt[:], in_=cw_d), [], [d_cw], dma=True)
        d_ngm = Dep(f"ngm{nh}")
        P.add("sp", lambda e: e.dma_start(out=ngm[:], in_=ngm_d), [], [d_ngm], dma=True)
        d_ngd = Dep(f"ngd{nh}")
        P.add("sp", lambda e: e.dma_start(out=ngd[:], in_=ngd_d), [], [d_ngd], dma=True)
        d_expA = Dep(f"expA{nh}")
        P.add("act", lambda e: e.activation(out=expA[:], in_=gpt[:, 3], func=AF.Exp), [d_par], [d_expA])

        P.add("dve", lambda e: e.memset(xin, 0.0), [], [dp("xin")])
        segs = []
        xo = 0
        for (t0, nt) in seqs:
            L = nt * 128
            for a in range(0, L, 512):
                ln = min(512, L - a)
                segs.append((xo + a, t0 * 128 + a, ln))
            xo += L + 4
        seq_x0 = {}
        xo = 0
        for (t0, nt) in seqs:
            seq_x0[t0] = xo
            xo += nt * 128 + 4
        pj = [0]
        segc = [0]

        def xin_pos(tok):
            for (t0, nt) in seqs:
                if t0 * 128 <= tok < (t0 + nt) * 128:
                    return seq_x0[t0] + 2 + (tok - t0 * 128)
            raise ValueError
        for ci in range(5 * nh):
            kind, h = ci // nh, ci % nh
            s = state["gu"] % NGU
            state["gu"] += 1
            wv = gu_s[s][:, 0, :, :]
            P.add("pool", lambda e, ci=ci, wv=wv: e.dma_start(out=wv, in_=wf[ci]), [], [d_gu[s]], dma=True)
            for p0 in range(0, T, 512):
                pb = 6 + (pj[0] % 2)
                pj[0] += 1
                for k in range(8):
                    P.mm(ps[pb], wv[:, k, :], src(k, p0, p0 + 512), k == 0, k == 7, [d_gu[s]] + d_src, [d_ps[pb]])
                if kind == 0:
                    P.add("act", lambda e, h=h, p0=p0, pb=pb: e.activation(
                        out=qmT[:, h, p0:p0 + 512], in_=ps[pb], func=AF.Identity, scale=0.125), [d_ps[pb]], [dp("qmT")])
                elif kind == 1:
                    P.add("act", lambda e, h=h, p0=p0, pb=pb: e.activation(
                        out=kmT[:, h, p0:p0 + 512], in_=ps[pb], func=AF.Identity), [d_ps[pb]], [dp("kmT")])
                else:
                    a = p0
                    while a < p0 + 512:
                        for (t0, nt) in seqs:
                            if t0 * 128 <= a < (t0 + nt) * 128:
                                b_ = min(p0 + 512, (t0 + nt) * 128)
                        xp = xin_pos(a)
                        P.add("act", lambda e, a=a, b_=b_, xp=xp, pb=pb, p0=p0: e.activation(
                            out=xin[:, xp:xp + (b_ - a)], in_=ps[pb][:, a - p0:b_ - p0], func=AF.Identity),
                            [d_ps[pb]], [dp("xin")])
                        a = b_
            if kind >= 2:
                cidx = (kind - 2) * nh + h
                for (xo_, tok0, ln) in segs:
                    bsel = segc[0] % 2
                    segc[0] += 1
                    cacc, csq, ctmp = caccs[bsel], csqs[bsel], ctmps[bsel]
                    CA, CT = "cacc%d" % bsel, "ctmp%d" % bsel
                    for j in range(5):
                        sl = xin[:, xo_ + j:xo_ + j + ln]
                        if j == 0:
                            P.add("dve", lambda e, cacc=cacc, csq=csq, ctmp=ctmp, sl=sl, ln=ln, cidx=cidx: e.tensor_scalar(
                                out=cacc[:, 0:ln], in0=sl, scalar1=cwt[:, cidx, 0:1], scalar2=None, op0=ALU.mult),
                                [dp("xin"), d_cw], [dp(CA)])
                        else:
                            P.add("dve", lambda e, cacc=cacc, csq=csq, ctmp=ctmp, sl=sl, ln=ln, cidx=cidx, j=j: e.scalar_tensor_tensor(
                                out=cacc[:, 0:ln], in0=sl, scalar=cwt[:, cidx, j:j + 1], in1=cacc[:, 0:ln],
                                op0=ALU.mult, op1=ALU.add), [dp("xin"), d_cw, dp(CA)], [dp(CA)])
                    if kind == 4:
                        P.add("act", lambda e, cacc=cacc, csq=csq, ctmp=ctmp, h=h, tok0=tok0, ln=ln: e.activation(
                            out=vdT[:, h, tok0:tok0 + ln], in_=cacc[:, 0:ln], func=AF.Silu), [dp(CA)], [dp("vdT")])
                    else:
                        P.add("act", lambda e, cacc=cacc, csq=csq, ctmp=ctmp, ln=ln: e.activation(out=ctmp[:, 0:ln], in_=cacc[:, 0:ln], func=AF.Exp, scale=-1.0),
                              [dp(CA)], [dp(CT)])
                        P.add("dve", lambda e, cacc=cacc, csq=csq, ctmp=ctmp, ln=ln: e.tensor_scalar(
                            out=ctmp[:, 0:ln], in0=ctmp[:, 0:ln], scalar1=1.0, scalar2=None, op0=ALU.add), [dp(CT)], [dp(CT)])
                        P.add("dve", lambda e, cacc=cacc, csq=csq, ctmp=ctmp, ln=ln: e.reciprocal(
                            out=ctmp[:, 0:ln], in_=ctmp[:, 0:ln]), [dp(CT)], [dp(CT)])
                        P.add("dve", lambda e, cacc=cacc, csq=csq, ctmp=ctmp, ln=ln: e.tensor_tensor(
                            out=ctmp[:, 0:ln], in0=ctmp[:, 0:ln], in1=cacc[:, 0:ln], op=ALU.mult), [dp(CT), dp(CA)], [dp(CT)])
                        P.add("act", lambda e, cacc=cacc, csq=csq, ctmp=ctmp, ln=ln: e.activation(out=csq[:, 0:ln], in_=ctmp[:, 0:ln], func=AF.Square),
                              [dp(CT), dp(CA)], [dp(CA)])
                        P.mm(ps[6][:, 0:ln], ones_bf[:], csq[:, 0:ln], True, True, [d_ones, dp(CA)], [d_ps[6]])
                        P.add("act", lambda e, cacc=cacc, csq=csq, ctmp=ctmp, ln=ln: e.activation(out=rstd[:, 0:ln], in_=ps[6][:, 0:ln], func=AF.Ln,
                                                                   bias=epsb[:, 1:2], scale=1.0), [d_ps[6], d_ones], [d_rstd])
                        P.add("act", lambda e, cacc=cacc, csq=csq, ctmp=ctmp, ln=ln: e.activation(out=rstd[:, 0:ln], in_=rstd[:, 0:ln], func=AF.Exp,
                                                                   scale=-0.5), [d_rstd], [d_rstd])
                        dst = qnT if kind == 2 else knT
                        scl = float(128 ** -0.5) if kind == 2 else 1.0
                        P.add("dve", lambda e, cacc=cacc, csq=csq, ctmp=ctmp, dst=dst, h=h, tok0=tok0, ln=ln, scl=scl: e.scalar_tensor_tensor(
                            out=dst[:, h, tok0:tok0 + ln], in0=ctmp[:, 0:ln], scalar=scl, in1=rstd[:, 0:ln],
                            op0=ALU.mult, op1=ALU.mult), [dp(CT), d_rstd], [dp("qnT" if kind == 2 else "knT")])
                        if kind == 3:
                            P.add("dve", lambda e, cacc=cacc, csq=csq, ctmp=ctmp, h=h, tok0=tok0, ln=ln: e.tensor_tensor(
                                out=knT32[:, h, tok0:tok0 + ln], in0=ctmp[:, 0:ln], in1=rstd[:, 0:ln], op=ALU.mult),
                                [dp(CT), d_rstd], [d_sq])

        P.add("dve", lambda e: e.memset(vaug[:, :, 128:130], 1.0), [], [dp("vaug")])
        nA = (128 * nh + 64 * nh + 4 * n2 + 255) // 256
        for blk in range(nA):
            s = state["gu"] % NGU
            state["gu"] += 1
            wv = gu_s[s][:].rearrange("p a b c -> p (a b c)").rearrange("p (k n) -> p k n", k=8)
            P.add("pool", lambda e, blk=blk, wv=wv: e.dma_start(out=wv, in_=wtA[blk]), [], [d_gu[s]], dma=True)
            for tl_ in range(NT):
                pb = 6 + (pj[0] % 2)
                pj[0] += 1
                for k in range(8):
                    P.mm(ps[pb][:, 0:256], src(k, tl_ * 128, (tl_ + 1) * 128), wv[:, k, :], k == 0, k == 7,
                         [d_gu[s]] + d_src, [d_ps[pb]])
                c0 = blk * 256
                for h in range(nh):
                    a = h * 128
                    if c0 <= a < c0 + 256:
                        P.add("act", lambda e, tl_=tl_, h=h, a=a, c0=c0, pb=pb: e.activation(
                            out=vaug[:, tl_ * nh + h, 0:128], in_=ps[pb][:, a - c0:a - c0 + 128], func=AF.Identity),
                            [d_ps[pb]], [dp("vaug")])
                for h in range(nh):
                    a = 128 * nh + h * 64
                    if c0 <= a < c0 + 256:
                        P.add("dve", lambda e, tl_=tl_, h=h, a=a, c0=c0, pb=pb: e.tensor_copy(
                            out=kmtok[:, tl_ * nh + h, :], in_=ps[pb][:, a - c0:a - c0 + 64]),
                            [d_ps[pb]], [dp("kmtok")])
                a = 128 * nh + 64 * nh
                if c0 <= a < c0 + 256:
                    P.add("dve", lambda e, tl_=tl_, a=a, c0=c0, pb=pb: e.tensor_copy(
                        out=gates[:, tl_].rearrange("p a b -> p (a b)"), in_=ps[pb][:, a - c0:a - c0 + 4 * n2]),
                        [d_ps[pb]], [dp("gates")])
        nB = (256 * nh) // 256
        for blk in range(nB):
            s = state["gu"] % NGU
            state["gu"] += 1
            wv = gu_s[s][:].rearrange("p a b c -> p (a b c)").rearrange("p (k n) -> p k n", k=8)
            P.add("pool", lambda e, blk=blk, wv=wv: e.dma_start(out=wv, in_=wtB[blk]), [], [d_gu[s]], dma=True)
            for ch in range(NCHK):
                pb = 6 + (pj[0] % 2)
                pj[0] += 1
                for k in range(8):
                    P.mm(ps[pb][0:64, 0:256], src(k, ch * 64, (ch + 1) * 64), wv[:, k, :], k == 0, k == 7,
                         [d_gu[s]] + d_src, [d_ps[pb]])
                for half in range(2):
                    gcol = blk * 256 + half * 128
                    isz = gcol >= nh * 128
                    h = (gcol - (nh * 128 if isz else 0)) // 128
                    dst = zd if isz else om
                    P.add("act", lambda e, ch=ch, h=h, half=half, dst=dst, pb=pb: e.activation(
                        out=dst[:, ch * nh + h, :], in_=ps[pb][0:64, half * 128:(half + 1) * 128], func=AF.Identity),
                        [d_ps[pb]], [dp("zd" if isz else "om")])

        gtmp = P.sb([128, 6, NT, n2], F32, f"gtmp{nh}")
        cin = P.sb([128, 2, NT, 2, nh], F32, f"cin{nh}")
        d_g = dp("gt")
        d_gtmp = Dep(f"gtmp{nh}")
        G4 = lambda w: gates[:, :, w, :]
        Q = lambda q: gt[:, :, q, :]
        TM = lambda i: gtmp[:, i]
        ONE = epsb[:, 2:3]
        P.add("dve", lambda e: e.tensor_tensor(out=TM(0), in0=G4(0), in1=gpt[:, 0], op=ALU.add),
              [dp("gates"), d_par], [d_gtmp])
        P.add("dve", lambda e: e.tensor_tensor(out=TM(1), in0=G4(1), in1=gpt[:, 1], op=ALU.add),
              [dp("gates"), d_par], [d_gtmp])
        P.add("act", lambda e: e.activation(out=TM(1), in_=TM(1), func=AF.Exp, scale=-1.0), [d_gtmp], [d_gtmp])
        P.add("act", lambda e: e.activation(out=TM(1), in_=TM(1), func=AF.Ln, bias=ONE, scale=1.0),
              [d_gtmp, d_ones], [d_gtmp])
        P.add("dve", lambda e: e.tensor_scalar(out=Q(1), in0=TM(1), scalar1=-1.0, scalar2=None, op0=ALU.mult),
              [d_gtmp], [d_g])
        P.add("act", lambda e: e.activation(out=TM(2), in_=G4(2), func=AF.Exp, scale=-1.0), [dp("gates")], [d_gtmp])
        P.add("act", lambda e: e.activation(out=TM(2), in_=TM(2), func=AF.Ln, bias=ONE, scale=1.0),
              [d_gtmp, d_ones], [d_gtmp])
        P.add("dve", lambda e: e.tensor_scalar(out=Q(5), in0=TM(2), scalar1=-1.0, scalar2=None, op0=ALU.mult),
              [d_gtmp], [d_g])
        P.add("dve", lambda e: e.tensor_tensor(out=TM(3), in0=G4(3), in1=gpt[:, 2], op=ALU.add),
              [dp("gates"), d_par], [d_gtmp])
        P.add("act", lambda e: e.activation(out=TM(3), in_=TM(3), func=AF.Exp), [d_gtmp], [d_gtmp])
        P.add("act", lambda e: e.activation(out=TM(3), in_=TM(3), func=AF.Ln, bias=ONE, scale=1.0),
              [d_gtmp, d_ones], [d_gtmp])
        P.add("dve", lambda e: e.tensor_tensor(out=Q(4), in0=TM(3), in1=expA[:], op=ALU.mult),
              [d_gtmp, d_expA], [d_g])
        P.add("dve", lambda e: e.tensor_scalar(out=Q(3), in0=Q(4), scalar1=-1.0, scalar2=None, op0=ALU.mult),
              [d_g], [d_g])
        for dr in range(2):
            P.add("dve", lambda e, dr=dr: e.tensor_copy(out=cin[:, dr, :, 0, :], in_=Q(1)[:, :, dr * nh:(dr + 1) * nh]),
                  [d_g, d_gtmp], [d_gtmp])
            P.add("dve", lambda e, dr=dr: e.tensor_copy(out=cin[:, dr, :, 1, :], in_=Q(3)[:, :, dr * nh:(dr + 1) * nh]),
                  [d_g, d_gtmp], [d_gtmp])
        pb = 6 + (pj[0] % 2)
        pj[0] += 1
        W_ = NT * n2
        for dr in range(2):
            P.mm(ps[pb][:, dr * W_:(dr + 1) * W_], mk[:, dr, :], cin[:, dr].rearrange("p t a b -> p (t a b)"), True, True,
                 [d_msk, d_gtmp], [d_ps[pb]])
        for dr in range(2):
            P.mm(ps[pb][:, 256 + dr * W_:256 + (dr + 1) * W_], mk[:, 2, :], cin[:, dr].rearrange("p t a b -> p (t a b)"),
                 True, True, [d_msk, d_gtmp], [d_ps[pb]])
        cums = lambda dr, w: ps[pb][:, dr * W_:(dr + 1) * W_].rearrange("p (t a b) -> p t a b", t=NT, a=2)[:, :, w, :]
        tots = lambda dr, w: ps[pb][:, 256 + dr * W_:256 + (dr + 1) * W_].rearrange("p (t a b) -> p t a b", t=NT, a=2)[:, :, w, :]
        for dr in range(2):
            hs_ = slice(dr * nh, (dr + 1) * nh)
            P.add("dve", lambda e, dr=dr, hs_=hs_: e.tensor_tensor(
                out=Q(0)[:, :, hs_], in0=TM(0)[:, :, hs_], in1=cums(dr, 0), op=ALU.subtract), [d_gtmp, d_ps[pb]], [d_g])
            P.add("dve", lambda e, dr=dr, hs_=hs_: e.tensor_tensor(
                out=Q(11)[:, :, hs_], in0=Q(0)[:, :, hs_], in1=tots(dr, 0), op=ALU.add), [d_g, d_ps[pb]], [d_g])
            P.add("dve", lambda e, dr=dr, hs_=hs_: e.tensor_copy(
                out=Q(12)[:, :, hs_], in_=tots(dr, 0)), [d_ps[pb]], [d_g])
            P.add("dve", lambda e, dr=dr, hs_=hs_: e.tensor_scalar(
                out=Q(6)[:, :, hs_], in0=cums(dr, 1), scalar1=-1.0, scalar2=None, op0=ALU.mult), [d_ps[pb]], [d_g])
            P.add("dve", lambda e, dr=dr, hs_=hs_: e.tensor_tensor(
                out=Q(7)[:, :, hs_], in0=Q(5)[:, :, hs_], in1=cums(dr, 1), op=ALU.add), [d_g, d_ps[pb]], [d_g])
            P.add("dve", lambda e, dr=dr, hs_=hs_: e.tensor_tensor(
                out=Q(10)[:, :, hs_], in0=Q(6)[:, :, hs_], in1=tots(dr, 1), op=ALU.add), [d_g, d_ps[pb]], [d_g])
        P.add("act", lambda e: e.activation(out=Q(2), in_=Q(11), func=AF.Exp), [d_g], [d_g])
        P.add("act", lambda e: e.activation(out=Q(8), in_=Q(5), func=AF.Exp), [d_g], [d_g])
        P.add("act", lambda e: e.activation(out=Q(9), in_=Q(7), func=AF.Exp), [d_g], [d_g])
        P.add("act", lambda e: e.activation(out=Q(10), in_=Q(10), func=AF.Exp), [d_g], [d_g])

        def col(tl_, q, j):
            return gt[:, tl_, q, j:j + 1]
        NEGns = {0: 3, 1: 5}
        NEGst = {0: 4, 1: 6}
        NEGts = {0: 6, 1: 4}
        bcnt = [0]

        def mlstm_unit(Q, dr, h, tl_):
            sn = dr
            T = TS[sn]
            rhs1, rhs2, rhs3, EB, DT, EG, FA, FB, FC = (T[k_] for k_ in ("rhs1", "rhs2", "rhs3", "EB", "DT", "EG", "FA", "FB", "FC"))
            swT, qbT, QKdT, wT, qgT, vnew = (T[k_] for k_ in ("swT", "qbT", "QKdT", "wT", "qgT", "vnew"))
            kw, wsc, Pm, PTm, X, kd, kdcc, dn = (T[k_] for k_ in ("kw", "wsc", "Pm", "PTm", "X", "kd", "kdcc", "dn"))
            j = dr * nh + h
            tsl = slice(tl_ * 128, (tl_ + 1) * 128)
            ba = 0
            A = ps[ba][:, dr * 256:(dr + 1) * 256]
            bb = 1 + dr
            Bk = ps[bb]
            Q.add("dve", lambda e: e.tensor_scalar(out=rhs1, in0=mk[:, dr, :], scalar1=col(tl_, 1, j), scalar2=None,
                                                   op0=ALU.mult), [d_msk, d_g], [dp("rhs1_%d" % sn)])
            Q.mm(A[:, 0:128], ones_f[:], rhs1, True, True, [d_ones, dp("rhs1_%d" % sn)], [d_ps[ba]])
            Q.mm(A[:, 128:256], ones_f[:], rhs1, True, False, [d_ones, dp("rhs1_%d" % sn)], [d_ps[ba]])
            Q.mm(A[:, 128:256], ident_f[:], mk[:, NEGns[dr], :], False, True, [d_id, d_msk], [d_ps[ba]])
            Q.mm(Bk[:, 0:128], kmT[:, h, tsl], qmT[:, h, tsl], True, True, [dp("kmT"), dp("qmT")], [d_ps[bb]])
            Q.add("act", lambda e: e.activation(out=EB, in_=A[:, 0:128], func=AF.Exp), [d_ps[ba]], [dp("EB_%d" % sn)])
            Q.add("act", lambda e: e.activation(out=DT, in_=A[:, 128:256], func=AF.Exp, bias=col(tl_, 0, j), scale=1.0),
                  [d_ps[ba], d_g], [dp("DT_%d" % sn)])
            Q.add("dve", lambda e: e.tensor_tensor(out=swT, in0=Bk[:, 0:128], in1=DT, op=ALU.mult),
                  [d_ps[bb], dp("DT_%d" % sn)], [dp("swT_%d" % sn)])
            Q.add("dve", lambda e: e.tensor_tensor(out=qbT, in0=qmT[:, h, tsl], in1=EB, op=ALU.mult),
                  [dp("qmT"), dp("EB_%d" % sn)], [dp("qbT_%d" % sn)])
            Q.add("dve", lambda e: e.tensor_scalar(out=wsc, in0=chind[:], scalar1=col(tl_, 2, j), scalar2=None,
                                                   op0=ALU.mult), [d_msk, d_g], [dp("wsc_%d" % sn)])
            for c in range(2):
                Q.add("dve", lambda e, c=c: e.tensor_scalar(out=kw[:, c, :], in0=kmtok[:, tl_ * nh + h, :],
                                                            scalar1=wsc[:, c:c + 1], scalar2=None, op0=ALU.mult),
                      [dp("kmtok"), dp("wsc_%d" % sn)], [dp("kw_%d" % sn)])

        def mlstm_chunk(Q, dr, h, tl_, c, first):
            sn = dr
            T = TS[sn]
            rhs1, rhs2, rhs3, EB, DT, EG, FA, FB, FC = (T[k_] for k_ in ("rhs1", "rhs2", "rhs3", "EB", "DT", "EG", "FA", "FB", "FC"))
            swT, qbT, QKdT, wT, qgT, vnew = (T[k_] for k_ in ("swT", "qbT", "QKdT", "wT", "qgT", "vnew"))
            kw, wsc, Pm, PTm, X, kd, kdcc, dn = (T[k_] for k_ in ("kw", "wsc", "Pm", "PTm", "X", "kd", "kdcc", "dn"))
            tc = slice(c * 64, (c + 1) * 64)
            tl1 = c * 64 + 63 if dr == 0 else c * 64
            chg = tl_ * 2 + c
            bb = 1 + dr
            B = ps[bb]
            Q.mm(B[0:64, 128:257], swT[:, tc], vaug[:, tl_ * nh + h, 0:129], True, False, [dp("swT_%d" % sn), dp("vaug")], [d_ps[bb]])
            Q.mm(B[0:64, 128:257], qbT[:, tc], Cbf[:, dr, h, 0:129], False, True, [dp("qbT_%d" % sn), dp("Cbf%d" % dr)], [d_ps[bb]])
            Q.mm(B[0:64, 320:449], kw[:, c, :], vaug[:, tl_ * nh + h, 0:129], True, True, [dp("kw_%d" % sn), dp("vaug")], [d_ps[bb]])
            Q.add("dve", lambda e: e.tensor_scalar(out=dn[0:64, 1:2], in0=B[0:64, 256:257], scalar1=-1.0, scalar2=1.0,
                                                   op0=ALU.mult, op1=ALU.max), [d_ps[bb]], [dp("dn_%d" % sn)])
            Q.add("dve", lambda e: e.scalar_tensor_tensor(out=dn[0:64, 0:1], in0=B[0:64, 256:257], scalar=1.0,
                                                          in1=dn[0:64, 1:2], op0=ALU.max, op1=ALU.max),
                  [d_ps[bb], dp("dn_%d" % sn)], [dp("dn_%d" % sn)])
            Q.add("dve", lambda e: e.reciprocal(out=dn[0:64, 1:2], in_=dn[0:64, 0:1]), [dp("dn_%d" % sn)], [dp("dn_%d" % sn)])
            if first:
                Q.add("dve", lambda e: e.tensor_scalar(out=hacc[:, chg * nh + h, :], in0=B[0:64, 128:256],
                                                       scalar1=dn[0:64, 1:2], scalar2=None, op0=ALU.mult),
                      [d_ps[bb], dp("dn_%d" % sn)], [dp("hacc")])
            else:
                Q.add("dve", lambda e: e.scalar_tensor_tensor(out=hacc[:, chg * nh + h, :], in0=B[0:64, 128:256],
                                                              scalar=dn[0:64, 1:2], in1=hacc[:, chg * nh + h, :],
                                                              op0=ALU.mult, op1=ALU.add),
                      [d_ps[bb], dp("dn_%d" % sn), dp("hacc")], [dp("hacc")])
            Q.add("dve", lambda e: e.scalar_tensor_tensor(out=Caug[0:64, dr, h, 0:129], in0=Caug[0:64, dr, h, 0:129],
                                                          scalar=EB[0:64, tl1:tl1 + 1], in1=B[0:64, 320:449],
                                                          op0=ALU.mult, op1=ALU.add),
                  [dp("Caug%d" % dr), dp("EB_%d" % sn), d_ps[bb]], [dp("Caug%d" % dr)])
            Q.add("act", lambda e: e.activation(out=Cbf[0:64, dr, h, 0:129], in_=Caug[0:64, dr, h, 0:129],
                                                func=AF.Identity), [dp("Caug%d" % dr)], [dp("Cbf%d" % dr)])

        def delta_unit(Q, dr, h, tl_):
            sn = dr
            T = TS[sn]
            rhs1, rhs2, rhs3, EB, DT, EG, FA, FB, FC = (T[k_] for k_ in ("rhsD", "rhs2", "rhs3", "EB", "DT", "EG", "FA", "FB", "FC"))
            swT, qbT, QKdT, wT, qgT, vnew = (T[k_] for k_ in ("swT", "qbT", "QKdT", "wT", "qgT", "vnew"))
            kw, wsc, Pm, PTm, X, kd, kdcc, dn = (T[k_] for k_ in ("kw", "wsc", "Pm", "PTm", "X", "kd", "kdcc", "dn"))
            j = dr * nh + h
            tsl = slice(tl_ * 128, (tl_ + 1) * 128)
            ba = 3 if dr == 0 else 5
            A = ps[ba]
            Q.add("dve", lambda e: e.tensor_scalar(out=rhs1, in0=mk[:, dr, :], scalar1=col(tl_, 3, j), scalar2=None,
                                                   op0=ALU.mult), [d_msk, d_g], [dp("rhsD_%d" % sn)])
            Q.add("dve", lambda e: e.scalar_tensor_tensor(out=rhs2, in0=ident_f[:], scalar=col(tl_, 5, j), in1=rhs1,
                                                          op0=ALU.mult, op1=ALU.add), [d_id, d_g, dp("rhsD_%d" % sn)], [dp("rhs2_%d" % sn)])
            Q.add("dve", lambda e: e.tensor_scalar(out=rhs3, in0=mk[:, dr, :], scalar1=col(tl_, 4, j), scalar2=None,
                                                   op0=ALU.mult), [d_msk, d_g], [dp("rhs3_%d" % sn)])
            Q.mm(A[:, 0:128], ones_f[:], rhs1, True, True, [d_ones, dp("rhsD_%d" % sn)], [d_ps[ba]])
            Q.mm(A[:, 128:256], ones_f[:], rhs2, True, False, [d_ones, dp("rhs2_%d" % sn)], [d_ps[ba]])
            Q.mm(A[:, 128:256], ident_f[:], mk[:, NEGst[dr], :], False, True, [d_id, d_msk], [d_ps[ba]])
            Q.mm(A[:, 256:384], ones_f[:], rhs3, True, False, [d_ones, dp("rhs3_%d" % sn)], [d_ps[ba]])
            Q.mm(A[:, 256:384], ident_f[:], mk[:, NEGts[dr], :], False, True, [d_id, d_msk], [d_ps[ba]])
            Q.mm(A[:, 384:512], ones_f[:], rhs1, True, False, [d_ones, dp("rhsD_%d" % sn)], [d_ps[ba]])
            Q.mm(A[:, 384:512], ident_f[:], mk[:, NEGns[dr], :], False, True, [d_id, d_msk], [d_ps[ba]])
            Q.add("act", lambda e: e.activation(out=EG, in_=A[:, 0:128], func=AF.Exp), [d_ps[ba]], [dp("EG_%d" % sn)])
            Q.add("act", lambda e: e.activation(out=FA, in_=A[:, 128:256], func=AF.Exp, bias=col(tl_, 6, j), scale=1.0),
                  [d_ps[ba], d_g], [dp("FA_%d" % sn)])
            Q.add("act", lambda e: e.activation(out=FB, in_=A[:, 256:384], func=AF.Exp, bias=col(tl_, 7, j), scale=1.0),
                  [d_ps[ba], d_g], [dp("FB_%d" % sn)])
            Q.add("act", lambda e: e.activation(out=FC, in_=A[:, 384:512], func=AF.Exp, bias=col(tl_, 6, j), scale=1.0),
                  [d_ps[ba], d_g], [dp("FC_%d" % sn)])
            cb_ = 4 if dr == 0 else 6
            C_ = ps[cb_]
            Q.mm(C_[:, 0:128], knT32[:, h, tsl], knT32[:, h, tsl], True, True, [d_sq], [d_ps[cb_]])
            Q.mm(C_[:, 128:256], knT[:, h, tsl], qnT[:, h, tsl], True, True, [dp("knT"), dp("qnT")], [d_ps[cb_]])
            Q.mm(C_[:, 256:384], knT32[:, h, tsl], ident_f[:], True, True, [d_sq, d_id], [d_ps[cb_]])
            Q.mm(C_[:, 384:512], vdT[:, h, tsl], ident_b[:], True, True, [dp("vdT"), d_ones], [d_ps[cb_]])
            Q.add("dve", lambda e: e.tensor_tensor(out=PTm[0], in0=C_[:, 0:128], in1=FA, op=ALU.mult),
                  [d_ps[cb_], dp("FA_%d" % sn)], [dp("PT0_%d" % sn)])
            Q.add("dve", lambda e: e.tensor_tensor(out=Pm[0], in0=C_[:, 0:128], in1=FB, op=ALU.mult),
                  [d_ps[cb_], dp("FB_%d" % sn)], [dp("P0_%d" % sn)])
            Q.add("dve", lambda e: e.tensor_tensor(out=QKdT, in0=C_[:, 128:256], in1=FC, op=ALU.mult),
                  [d_ps[cb_], dp("FC_%d" % sn)], [dp("QKdT_%d" % sn)])
            Q.add("dve", lambda e: e.tensor_scalar(out=X[:, 128:256], in0=C_[:, 256:384], scalar1=col(tl_, 9, j),
                                                   scalar2=None, op0=ALU.mult), [d_ps[cb_], d_g], [dp("X_%d" % sn)])
            Q.add("dve", lambda e: e.tensor_scalar(out=X[:, 0:128], in0=C_[:, 384:512], scalar1=col(tl_, 8, j),
                                                   scalar2=None, op0=ALU.mult), [d_ps[cb_], d_g], [dp("X_%d" % sn)])
            Q.add("dve", lambda e: e.tensor_scalar(out=kdcc, in0=chind[:], scalar1=col(tl_, 10, j), scalar2=None,
                                                   op0=ALU.mult), [d_msk, d_g], [dp("kdcc_%d" % sn)])
            for c in range(2):
                Q.add("dve", lambda e, c=c: e.tensor_scalar(out=kd[:, c, :], in0=C_[:, 256:384],
                                                            scalar1=kdcc[:, c:c + 1], scalar2=None, op0=ALU.mult),
                      [d_ps[cb_], dp("kdcc_%d" % sn)], [dp("kd_%d" % sn)])
            Q.add("dve", lambda e: e.tensor_tensor(out=qgT, in0=qnT[:, h, tsl], in1=EG, op=ALU.mult),
                  [dp("qnT"), dp("EG_%d" % sn)], [dp("qgT_%d" % sn)])
            Y = ps[ba]
            cur = 0
            for lev in range(6):
                Q.mm(Y[:, 0:256], PTm[cur], X, True, True, [dp(f"PT{cur}_{sn}"), dp("X_%d" % sn)], [d_ps[ba]])
                Q.add("dve", lambda e, lev=lev: e.tensor_tensor(out=X, in0=X, in1=Y[:, 0:256],
                                                                op=ALU.subtract if lev == 0 else ALU.add),
                      [dp("X_%d" % sn), d_ps[ba]], [dp("X_%d" % sn)])
                if lev < 5:
                    nx = 1 - cur
                    Q.mm(Y[:, 256:384], PTm[cur], Pm[cur], True, True, [dp(f"PT{cur}_{sn}"), dp(f"P{cur}_{sn}")], [d_ps[ba]])
                    Q.mm(Y[:, 384:512], Pm[cur], PTm[cur], True, True, [dp(f"PT{cur}_{sn}"), dp(f"P{cur}_{sn}")], [d_ps[ba]])
                    Q.add("act", lambda e, nx=nx: e.activation(out=Pm[nx], in_=Y[:, 256:384], func=AF.Identity),
                          [d_ps[ba]], [dp(f"P{nx}_{sn}")])
                    Q.add("dve", lambda e, nx=nx: e.tensor_copy(out=PTm[nx], in_=Y[:, 384:512]),
                          [d_ps[ba]], [dp(f"PT{nx}_{sn}")])
                    cur = nx
            Q.mm(Y[:, 0:128], X[:, 128:256], ident_f[:], True, True, [dp("X_%d" % sn), d_id], [d_ps[ba]])
            Q.add("act", lambda e: e.activation(out=wT, in_=Y[:, 0:128], func=AF.Identity), [d_ps[ba]], [dp("wT_%d" % sn)])

        def delta_chunk(Q, dr, h, tl_, c, first):
            sn = dr
            T = TS[sn]
            rhs1, rhs2, rhs3, EB, DT, EG, FA, FB, FC = (T[k_] for k_ in ("rhs1", "rhs2", "rhs3", "EB", "DT", "EG", "FA", "FB", "FC"))
            swT, qbT, QKdT, wT, qgT, vnew = (T[k_] for k_ in ("swT", "qbT", "QKdT", "wT", "qgT", "vnew"))
            kw, wsc, Pm, PTm, X, kd, kdcc, dn = (T[k_] for k_ in ("kw", "wsc", "Pm", "PTm", "X", "kd", "kdcc", "dn"))
            tc = slice(c * 64, (c + 1) * 64)
            tl1 = c * 64 + 63 if dr == 0 else c * 64
            chg = tl_ * 2 + c
            eb_ = 4 if dr == 0 else 6
            E_ = ps[eb_]
            Q.mm(E_[:, 0:128], wT, Sbf[:, dr, h, :], True, True, [dp("wT_%d" % sn), dp("Sbf%d" % dr)], [d_ps[eb_]])
            Q.add("dve", lambda e: e.tensor_tensor(out=vnew, in0=X[:, 0:128], in1=E_[:, 0:128], op=ALU.subtract),
                  [dp("X_%d" % sn), d_ps[eb_]], [dp("vnew_%d" % sn)])
            Q.mm(E_[0:64, 128:256], qgT[:, tc], Sbf[:, dr, h, :], True, False, [dp("qgT_%d" % sn), dp("Sbf%d" % dr)], [d_ps[eb_]])
            Q.mm(E_[0:64, 128:256], QKdT[:, tc], vnew, False, True, [dp("QKdT_%d" % sn), dp("vnew_%d" % sn)], [d_ps[eb_]])
            Q.mm(E_[:, 256:384], kd[:, c, :], vnew, True, True, [dp("kd_%d" % sn), dp("vnew_%d" % sn)], [d_ps[eb_]])
            if first:
                Q.add("act", lambda e: e.activation(out=oacc[:, chg * nh + h, :], in_=E_[0:64, 128:256], func=AF.Identity),
                      [d_ps[eb_]], [dp("oacc")])
            else:
                Q.add("dve", lambda e: e.tensor_tensor(out=oacc[:, chg * nh + h, :], in0=oacc[:, chg * nh + h, :],
                                                       in1=E_[0:64, 128:256], op=ALU.add),
                      [d_ps[eb_], dp("oacc")], [dp("oacc")])
            Q.add("dve", lambda e: e.scalar_tensor_tensor(out=Sst[:, dr, h, :], in0=Sst[:, dr, h, :],
                                                          scalar=EG[:, tl1:tl1 + 1], in1=E_[:, 256:384],
                                                          op0=ALU.mult, op1=ALU.add),
                  [dp("Sst%d" % dr), dp("EG_%d" % sn), d_ps[eb_]], [dp("Sst%d" % dr)])
            Q.add("act", lambda e: e.activation(out=Sbf[:, dr, h, :], in_=Sst[:, dr, h, :], func=AF.Identity),
                  [dp("Sst%d" % dr)], [dp("Sbf%d" % dr)])

        class Rec:
            def __init__(self):
                self.ops = []

            def add(self, *a_, **k_):
                self.ops.append((0, a_, k_))

            def mm(self, *a_, **k_):
                self.ops.append((1, a_, k_))

        for si, (t0, nt) in enumerate(seqs):
            cfg["init_state"](si, Caug, Cbf, Sst, Sbf, dp)
            chains = []
            for dr in range(2):
                qm_, qd_ = Rec(), Rec()
                for step in range(nt):
                    tl_ = t0 + step if dr == 0 else t0 + nt - 1 - step
                    other_step = nt - 1 - step
                    first = step < other_step
                    for h in range(nh):
                        mlstm_unit(qm_, dr, h, tl_)
                        for c in ((0, 1) if dr == 0 else (1, 0)):
                            mlstm_chunk(qm_, dr, h, tl_, c, first)
                        delta_unit(qd_, dr, h, tl_)
                        for c in ((0, 1) if dr == 0 else (1, 0)):
                            delta_chunk(qd_, dr, h, tl_, c, first)
                chains += [qm_.ops, qd_.ops]
            pos = [0] * len(chains)
            live = True
            while live:
                live = False
                for ci_, ch_ in enumerate(chains):
                    while pos[ci_] < len(ch_):
                        kind_, a_, k_ = ch_[pos[ci_]]
                        pos[ci_] += 1
                        (P.mm if kind_ else P.add)(*a_, **k_)
                        live = True
                        if not (kind_ == 1 and a_[4] is False):
                            break
            cfg["final_state"](si, t0, nt, Caug, Sst, gt, dp, d_g)

        mixg = cfg["mix_group"]
        FB_ = []
        NFB = 3
        for i_ in range(NFB):
            o6 = 5.25 * i_
            FB_.append(dict(f1=AVH(o6, [64, 4, 128], F32, 64), f2=AVH(o6 + 2, [64, 4, 128], F32, 64),
                            f3=AVH(o6 + 4, [64, 4, 128], BF16, 64), fc=AVH(o6 + 5, [64, 3, 4], F32, 64)))
        NI = NCHK * nh
        gi = 0
        for kind in range(2):
            acc = hacc if kind == 0 else oacc
            d_acc = dp("hacc") if kind == 0 else dp("oacc")
            for g in range(NI // 4):
                bf_ = FB_[gi % NFB]
                sfx = "_%d_1" % (gi % NFB)
                f1, f2, f3, fc = bf_["f1"], bf_["f2"], bf_["f3"], bf_["fc"]
                d1, d2, d3, dc = dp("f1" + sfx), dp("f2" + sfx), dp("f3" + sfx), dp("fc" + sfx)
                pbk = 5 + (gi % NFB)
                gi += 1
                i0 = 4 * g
                P.add("act", lambda e, acc=acc, i0=i0, f1=f1: e.activation(out=f1, in_=acc[:, i0:i0 + 4, :], func=AF.Square),
                      [d_acc], [d1])
                P.add("dve", lambda e, f1=f1, fc=fc: e.reduce_sum(out=fc[:, 0, :], in_=f1, axis=AX.X), [d1], [dc])
                P.add("act", lambda e, fc=fc: e.activation(out=fc[:, 1, :], in_=fc[:, 0, :], func=AF.Ln,
                                                           bias=epsb[0:64, 3:4], scale=1.0 / 128.0), [dc, d_ones], [dc])
                P.add("act", lambda e, fc=fc: e.activation(out=fc[:, 2, :], in_=fc[:, 1, :], func=AF.Exp, scale=-0.5),
                      [dc], [dc])
                if kind == 0:
                    P.add("act", lambda e, i0=i0, f2=f2: e.activation(out=f2, in_=om[:, i0:i0 + 4, :], func=AF.Exp, scale=-1.0),
                          [dp("om")], [d2])
                    P.add("dve", lambda e, f2=f2: e.tensor_scalar(out=f2, in0=f2, scalar1=1.0, scalar2=None, op0=ALU.add),
                          [d2], [d2])
                    P.add("dve", lambda e, f2=f2: e.reciprocal(out=f2, in_=f2), [d2], [d2])
                else:
                    P.add("act", lambda e, i0=i0, f2=f2: e.activation(out=f2, in_=zd[:, i0:i0 + 4, :], func=AF.Exp, scale=-1.0),
                          [dp("zd")], [d2])
                    P.add("dve", lambda e, f2=f2: e.tensor_scalar(out=f2, in0=f2, scalar1=1.0, scalar2=None, op0=ALU.add),
                          [d2], [d2])
                    P.add("dve", lambda e, f2=f2: e.reciprocal(out=f2, in_=f2), [d2], [d2])
                    P.add("dve", lambda e, i0=i0, f2=f2: e.tensor_tensor(out=f2, in0=f2, in1=zd[:, i0:i0 + 4, :], op=ALU.mult),
                          [d2, dp("zd")], [d2])
                for k in range(4):
                    h = (i0 + k) % nh
                    gsl = ngm[:, h * 128:(h + 1) * 128] if kind == 0 else ngd[:, :]
                    P.add("dve", lambda e, acc=acc, i0=i0, k=k, gsl=gsl, f1=f1, fc=fc: e.scalar_tensor_tensor(
                        out=f1[:, k, :], in0=acc[:, i0 + k, :], scalar=fc[:, 2, k:k + 1], in1=gsl,
                        op0=ALU.mult, op1=ALU.mult), [d_acc, dc, d_ngm, d_ngd, d1], [d1])
                P.add("dve", lambda e, f1=f1, f2=f2, f3=f3: e.tensor_tensor(out=f3, in0=f1, in1=f2, op=ALU.mult),
                      [d1, d2], [d3])
                for k in range(4):
                    P.mm(ps[pbk][:, k * 64:(k + 1) * 64], f3[:, k, :], ident_b[0:64, 0:64], True, True, [d3, d_ones],
                         [d_ps[pbk]])
                P.add("act", lambda e, kind=kind, g=g, pbk=pbk: e.activation(
                    out=mixg(kind, g), in_=ps[pbk][:, 0:256].rearrange("p (k t) -> p k t", k=4), func=AF.Identity),
                    [d_ps[pbk]], d_mix)
        return D_

    def ab_layer(l):
        for b in range(2):
            modulate(l, 1, b)
        d_ag1i, d_ag1o = Dep("ab_ag1i"), Dep("ab_ag1o")
        allgather(hT[:, :, BLK:2 * BLK], d_h[1], ag1_in.rearrange("p (k t) -> p k t", k=8), ag1_in, ag1_out,
                  d_ag1i, d_ag1o)

        def users_of(deplist):
            rs = []
            for o in deplist:
                rs += list(o.rs)
                if o.w is not None:
                    rs.append(o.w)
            return rs
        ffn_users = users_of(d_a)
        mixP = AV(61, [128, 8, BLK], BF16)
        d_mixP = Dep("ab_mixP")
        d_mixP.rs = list(ffn_users)
        cst = AV(69, [64, 2, 4, 130], F32, parts=64)
        mfin = P.sb([4, 2, 2], F32, "ab_mfin")
        mrun = P.sb([4, 4], F32, "ab_mrun")
        gmx = P.sb([4, 2, 2], F32, "ab_gmx")
        btt = P.sb([4, 2, 2], F32, "ab_btt")
        dg4 = P.sb([4, 4], F32, "ab_dg4")
        esc = P.sb([64, 4], F32, "ab_esc")
        d_fs = Dep("ab_fs")
        d_cst = Dep("ab_cst")
        d_mfin = Dep("ab_mfin")

        def p_init(si, Caug, Cbf, Sst, Sbf, dp):
            P.add("dve", lambda e: e.memset(Caug.rearrange("p a b c -> p (a b c)"), 0.0), [], [dp("Caug0"), dp("Caug1")])
            P.add("dve", lambda e: e.memset(Cbf.rearrange("p a b c -> p (a b c)"), 0.0), [], [dp("Cbf0"), dp("Cbf1")])
            P.add("dve", lambda e: e.memset(Sst.rearrange("p a b c -> p (a b c)"), 0.0), [], [dp("Sst0"), dp("Sst1")])
            P.add("dve", lambda e: e.memset(Sbf.rearrange("p a b c -> p (a b c)"), 0.0), [], [dp("Sbf0"), dp("Sbf1")])

        def p_final(si, t0, nt, Caug, Sst, gt, dp, d_g):
            nh = 4
            F_ = ps[7]
            for dr in range(2):
                for ti in range(nt):
                    tl_ = t0 + ti
                    P.mm(F_[0:nh, 0:128], gt[:, tl_, 11, dr * nh:(dr + 1) * nh], ident_f[:], True, True,
                         [d_g, d_id], [d_ps[7]])
                    P.mm(F_[0:nh, 128:256], gt[:, tl_, 12, dr * nh:(dr + 1) * nh], ident_f[:], True, True,
                         [d_g, d_id], [d_ps[7]])
                    P.add("dve", lambda e, ti=ti: e.tensor_reduce(
                        out=gmx[:, ti, :], in_=F_[0:nh, 0:128].rearrange("p (c s) -> p c s", c=2), axis=AX.X, op=ALU.max),
                        [d_ps[7]], [d_fs])
                    P.add("dve", lambda e, ti=ti: e.tensor_copy(
                        out=btt[:, ti, :], in_=F_[0:nh, 128:256].rearrange("p (c s) -> p c s", c=2)[:, :, 0]),
                        [d_ps[7]], [d_fs])
                P.add("dve", lambda e: e.memset(mrun[:, 0:1], 0.0), [d_fs], [d_fs])
                order = [(ti, c) for ti in range(nt) for c in range(2)]
                if dr == 1:
                    order = order[::-1]
                for (ti, c) in order:
                    P.add("dve", lambda e, ti=ti, c=c: e.tensor_tensor(out=mrun[:, 1:2], in0=mrun[:, 0:1],
                                                                       in1=btt[:, ti, c:c + 1], op=ALU.add), [d_fs], [d_fs])
                    P.add("dve", lambda e, ti=ti, c=c: e.tensor_tensor(out=mrun[:, 0:1], in0=mrun[:, 1:2],
                                                                       in1=gmx[:, ti, c:c + 1], op=ALU.max), [d_fs], [d_fs])
                P.add("dve", lambda e, dr=dr: e.tensor_copy(out=mfin[:, si, dr:dr + 1], in_=mrun[:, 0:1]), [d_fs], [d_mfin])
                P.add("dve", lambda e: e.tensor_scalar(out=dg4[:], in0=ident_f[0:4, 0:4], scalar1=mrun[:, 0:1],
                                                       scalar2=None, op0=ALU.mult), [d_fs, d_id], [d_fs])
                P.mm(F_[0:64, 256:260], ones_f[0:4, 0:64], dg4[:], True, True, [d_ones, d_fs], [d_ps[7]])
                P.add("act", lambda e: e.activation(out=esc[:], in_=F_[0:64, 256:260], func=AF.Exp, scale=-1.0),
                      [d_ps[7]], [d_fs])
                for h in range(nh):
                    P.add("dve", lambda e, dr=dr, h=h: e.tensor_scalar(
                        out=cst[:, dr, h, :], in0=Caug[0:64, dr, h, :], scalar1=esc[:, h:h + 1], scalar2=None,
                        op0=ALU.mult), [dp("Caug0"), dp("Caug1"), d_fs], [d_cst])
            P.add("sp", lambda e: e.dma_start(out=c_out[si], in_=cst.rearrange("p a b c -> p (a b c)")),
                  [d_cst], [d_y], dma=True)
            P.add("sp", lambda e: e.dma_start(out=s_out[si], in_=Sst.rearrange("p a b c -> p (a b c)")),
                  [dp("Sst0"), dp("Sst1")], [d_y], dma=True)

        cfgp = dict(nh=4, T=BLK, seqs=[(0, 2), (2, 2)], src=lambda k, a, b: hT[:, k, a:b], d_src=d_h[0],
                    wf=abp_wf, wtA=abp_wtA, wtB=abp_wtB, gp=abp_gp, cw=abp_cw, ngm=abp_ngm, ngd=ab_ngd,
                    mix_group=lambda kind, g: mixP[:, kind * 4:(kind + 1) * 4, g * 64:(g + 1) * 64], d_mix=[d_mixP],
                    prev_users=ffn_users, extra_users=lambda: [], init_state=p_init, final_state=p_final)
        Dp = ab_stage(cfgp)
        P.add("sp", lambda e: e.dma_start(out=m_out, in_=mfin[:].rearrange("p a b -> p (a b)")), [d_mfin], [d_y], dma=True)
        out_proj(l, ab_wo, mixP, [d_mixP], 0)
        prompt_users = users_of(list(Dp.values()) + [d_mixP]) + ffn_users
        hfull = AV(72, [128, 8, 2048], BF16)
        d_hf = Dep("ab_hfull")
        d_hf.rs = list(prompt_users)
        for m in range(4):
            P.add("sp", lambda e, m=m: e.dma_start(
                out=hfull[:, :, m * BLK:(m + 1) * BLK],
                in_=ag1_out[m * 128:(m + 1) * 128, :].rearrange("p (k t) -> p k t", k=8)),
                [d_ag1o], [d_hf], dma=True)
        headout = AV(61, [128, 2, 2048], BF16)
        d_ho = Dep("ab_headout")
        d_ho.rs = list(prompt_users)
        m0t = P.sb([128, 2], F32, "ab_m0t")
        d_c0 = Dep("ab_c0t")

        def s_init(si, Caug, Cbf, Sst, Sbf, dp):
            P.add("sp", lambda e: e.dma_start(out=Caug.rearrange("p a b c -> p (a b c)"),
                                              in_=ab_c0.rearrange("p a c -> p (a c)")), [], [dp("Caug0"), dp("Caug1")], dma=True)
            P.add("sp", lambda e: e.dma_start(out=m0t[:], in_=ab_m0), [], [d_c0], dma=True)
            P.add("act", lambda e: e.activation(out=m0t[:], in_=m0t[:], func=AF.Exp), [d_c0], [d_c0])
            for dr in range(2):
                P.add("dve", lambda e, dr=dr: e.tensor_scalar(out=Caug[:, dr, 0, :], in0=Caug[:, dr, 0, :],
                                                              scalar1=m0t[:, dr:dr + 1], scalar2=None, op0=ALU.mult),
                      [dp("Caug0"), dp("Caug1"), d_c0], [dp("Caug0"), dp("Caug1")])
            P.add("act", lambda e: e.activation(out=Cbf.rearrange("p a b c -> p (a b c)"),
                                                in_=Caug.rearrange("p a b c -> p (a b c)"), func=AF.Identity),
                  [dp("Caug0"), dp("Caug1")], [dp("Cbf0"), dp("Cbf1")])
            P.add("sp", lambda e: e.dma_start(out=Sst.rearrange("p a b c -> p (a b c)"),
                                              in_=ab_s0.rearrange("p a c -> p (a c)")), [], [dp("Sst0"), dp("Sst1")], dma=True)
            P.add("act", lambda e: e.activation(out=Sbf.rearrange("p a b c -> p (a b c)"),
                                                in_=Sst.rearrange("p a b c -> p (a b c)"), func=AF.Identity),
                  [dp("Sst0"), dp("Sst1")], [dp("Sbf0"), dp("Sbf1")])

        cfgs = dict(nh=1, T=2048, seqs=[(0, 16)], src=lambda k, a, b: hfull[:, k, a:b], d_src=[d_hf],
                    wf=abs_wf, wtA=abs_wtA, wtB=abs_wtB, gp=abs_gp, cw=abs_cw, ngm=abs_ngm, ngd=ab_ngd,
                    mix_group=lambda kind, g: headout[:, kind, g * 256:(g + 1) * 256].rearrange("p (k t) -> p k t", k=4),
                    d_mix=[d_ho],
                    prev_users=prompt_users, extra_users=lambda: users_of([d_hf]),
                    init_state=s_init, final_state=lambda *a: None)
        Ds = ab_stage(cfgs)
        d_ag2i, d_ag2o = Dep("ab_ag2i"), Dep("ab_ag2o")
        allgather(headout, [d_ho], ag2_in.rearrange("p (c t) -> p c t", c=2), ag2_in, ag2_out, d_ag2i, d_ag2o)
        cand = AV(0, [128, 4, 8, BLK], BF16)
        mixS = AV(32, [128, 8, BLK], BF16)
        d_cand, d_mixs = Dep("ab_cand"), Dep("ab_mixS")
        sample_users = users_of(list(Ds.values()) + [d_hf, d_ho])
        d_cand.rs = list(sample_users)
        d_mixs.rs = list(sample_users)
        for m in range(4):
            for kind in range(2):
                P.add("sp", lambda e, m=m, kind=kind: e.dma_start(
                    out=cand[:, :, kind * 4 + m, :],
                    in_=ag2_out[m * 128:(m + 1) * 128, kind * 2048:(kind + 1) * 2048].rearrange("p (r t) -> p r t", r=4)),
                    [d_ag2o], [d_cand], dma=True)
        for rr in range(4):
            if rr == 0:
                P.add("dve", lambda e: e.tensor_scalar(
                    out=mixS.rearrange("p a b -> p (a b)"), in0=cand[:, 0].rearrange("p a b -> p (a b)"),
                    scalar1=oh[:, 0:1], scalar2=None, op0=ALU.mult), [d_cand, d_oh], [d_mixs])
            else:
                P.add("dve", lambda e, rr=rr: e.scalar_tensor_tensor(
                    out=mixS.rearrange("p a b -> p (a b)"), in0=cand[:, rr].rearrange("p a b -> p (a b)"),
                    scalar=oh[:, rr:rr + 1], in1=mixS.rearrange("p a b -> p (a b)"), op0=ALU.mult, op1=ALU.add),
                    [d_cand, d_oh, d_mixs], [d_mixs])
        out_proj(l, ab_wo, mixS, [d_mixs], 1)
        tail = users_of([d_cand, d_mixs]) + sample_users
        for b in range(2):
            d_a[b].rs = list(d_a[b].rs) + tail
        hT_tmp_users = users_of([v_ for k_, v_ in list(Dp.items()) + list(Ds.items()) if k_.endswith("_1")])
        for b in range(2):
            for j in range(8):
                d_h[b][j].rs = list(d_h[b][j].rs) + hT_tmp_users


    def final_norm():
        for b in range(2):
            blk = slice(b * BLK, (b + 1) * BLK)
            P.add("act", lambda e, blk=blk: e.activation(out=sq[:], in_=xT[:, :, blk], func=AF.Square),
                  d_x[b], [d_sq])
            for k in range(8):
                P.mm(ps[6], ones_bf[:], sq[:, k, :], k == 0, k == 7, [d_ones, d_sq], [d_ps[6]])
            rstd_from_ps6()
            for j in range(8):
                P.add("dve", lambda e, j=j, blk=blk: e.scalar_tensor_tensor(
                    out=xT[:, j, blk], in0=xT[:, j, blk], scalar=ng[:, DEPTH * 3, j:j + 1], in1=rstd[:],
                    op0=ALU.mult, op1=ALU.mult), [d_x[b][j], d_ng, d_rstd], [d_x[b][j]])
                P.add("dve", lambda e, j=j, blk=blk: e.tensor_scalar(
                    out=xT[:, j, blk], in0=xT[:, j, blk], scalar1=32.0, scalar2=None, op0=ALU.mult),
                    [d_x[b][j]], [d_x[b][j]])
        for j in range(8):
            P.add("sp", lambda e, j=j: e.dma_start(out=yT_out[:, j, :], in_=xT[:, j, :]),
                  [d_x[0][j], d_x[1][j]], [d_y], dma=True)

    if stage.startswith("dbg"):
        na_layer(1, stop=stage[3:] or None)
    elif stage == "naA":
        ffn(1, 0)
        na_layer(1)
    elif stage == "naB":
        na_layer(1)
        ffn(1, 1)
    else:
        for l in range(DEPTH):
            ffn(l, 0)
            if l % 2 == 0 and stage in ("full", "ab"):
                ab_layer(l)
            if l % 2 == 1 and stage in ("full", "na"):
                na_layer(l)
            ffn(l, 1)
    final_norm()

    P.emit()
    return nc, es


NA_BIG = -30000.0


def na_bias_table(rel_bias_h):
    col = np.arange(64)
    cs = np.clip(col - 8, 0, 48)
    valid = (col[None, :] >= cs[:, None]) & (col[None, :] < cs[:, None] + 16)
    cidx = np.clip(col[None, :] - col[:, None], -15, 15) + 15
    t = rel_bias_h[:, :, cidx]
    t = np.where(valid[None, None], t, np.float32(NA_BIG))
    return np.ascontiguousarray(t.transpose(3, 1, 0, 2)).astype(np.float32)


def prep_inputs(inp):
    f32 = np.float32
    xp = np.asarray(inp["x_prompt"], f32)
    xs = np.asarray(inp["x_sample"], f32)
    c = np.asarray(inp["c"], f32)
    c_ctx = np.asarray(inp["c_ctx"], f32)
    shared = {}
    aw = np.asarray(inp["ada_w"], f32)
    aw6 = aw.reshape(DEPTH, 8, 128, 4, 18, 128)
    ab = np.asarray(inp["ada_b"], f32).reshape(DEPTH, 4, 18, 128)
    ngm = np.concatenate([np.asarray(inp["norm_g"], f32).reshape(DEPTH * 3, D),
                          np.asarray(inp["final_norm_g"], f32).reshape(1, D)], axis=0)
    shared["normg"] = np.ascontiguousarray(ngm.reshape(DEPTH * 3 + 1, 8, 128).transpose(2, 0, 1))
    wg = np.asarray(inp["ffn_wg"], f32).reshape(DEPTH * 2, 8, 128, NF, 128)
    wu = np.asarray(inp["ffn_wu"], f32).reshape(DEPTH * 2, 8, 128, NF, 128)
    gu = np.stack([wg, wu], axis=0)
    shared["ffn_gu"] = np.ascontiguousarray(gu.transpose(1, 4, 3, 0, 2, 5))
    wd = np.asarray(inp["ffn_wd"], f32).reshape(DEPTH * 2, NF, 128, 8, 128)
    shared["ffn_d"] = np.ascontiguousarray(wd.transpose(0, 3, 2, 1, 4))
    shared["ident"] = np.eye(128, dtype=f32)
    nw = np.asarray(inp["na_w_in"], f32)[0]
    wqk = nw[:, :2048].reshape(8, 128, 16, 128)
    shared["na_wqk"] = np.ascontiguousarray(wqk.transpose(2, 1, 0, 3))
    wv = nw[:, 2048:].reshape(8, 128, 4, 256)
    shared["na_wv"] = np.ascontiguousarray(wv.transpose(2, 1, 0, 3))
    wo = np.asarray(inp["na_w_out"], f32)[0].reshape(8, 128, 8, 128)
    shared["na_wo"] = np.ascontiguousarray(wo.transpose(2, 1, 0, 3))
    W = np.asarray(inp["ab_w_in"], f32)[0]
    Wx = np.concatenate([W, np.zeros((D, 1), f32)], axis=1)
    ZC = W.shape[1]
    O_QM, O_KM, O_VM, O_OM, O_I, O_F, O_QD, O_KD, O_VD, O_ZD, O_B, O_A = 0, 256, 512, 1024, 1536, 1544, 1552, 2064, 2576, 3088, 3600, 3608

    def ab_variant(heads):
        nh_ = len(heads)
        cf = []
        for h in heads:
            cf += [O_QM + h * 64 + d for d in range(64)] + [ZC] * 64
        for h in heads:
            cf += [O_KM + h * 64 + d for d in range(64)] + [ZC] * 64
        for base in (O_QD, O_KD, O_VD):
            for h in heads:
                cf += [base + h * 128 + d for d in range(128)]
        ca = []
        for h in heads:
            ca += [O_VM + h * 128 + d for d in range(128)]
        for h in heads:
            ca += [O_KM + h * 64 + d for d in range(64)]
        for base in (O_I, O_F, O_B, O_A):
            for dr in range(2):
                for h in heads:
                    ca.append(base + dr * 4 + h)
        ca += [ZC] * ((-len(ca)) % 256)
        cb = []
        for base in (O_OM, O_ZD):
            for h in heads:
                cb += [base + h * 128 + d for d in range(128)]
        wf_ = Wx[:, cf].reshape(8, 128, len(cf) // 128, 128).transpose(2, 1, 0, 3)
        wa_ = Wx[:, ca].reshape(8, 128, len(ca) // 256, 256).transpose(2, 1, 0, 3)
        wb_ = Wx[:, cb].reshape(8, 128, len(cb) // 256, 256).transpose(2, 1, 0, 3)

        def gsel(a):
            return np.asarray(a, f32)[:, heads].reshape(-1)
        gpv = np.stack([gsel(inp["mlstm_b_i"][0]), gsel(inp["mlstm_b_f"][0]), gsel(inp["delta_dt_bias"][0]),
                        gsel(inp["delta_a_log"][0])], axis=0)
        nt_ = 16 // nh_
        gp_ = np.broadcast_to(gpv[None, :, None, :], (128, 4, nt_, 2 * nh_))
        cwv = np.asarray(inp["delta_conv_w"], f32)[0]
        cw_ = np.stack([cwv[:, kind * 512 + h * 128: kind * 512 + (h + 1) * 128].T
                        for kind in range(3) for h in heads], axis=1)
        gm = np.asarray(inp["mlstm_norm_g"], f32)[0]
        ngm_ = np.broadcast_to(np.concatenate([gm[h * 128:(h + 1) * 128] for h in heads])[None], (64, 128 * nh_))
        return dict(wf=np.ascontiguousarray(wf_), wtA=np.ascontiguousarray(wa_), wtB=np.ascontiguousarray(wb_),
                    gp=np.ascontiguousarray(gp_), cw=np.ascontiguousarray(cw_), ngm=np.ascontiguousarray(ngm_))
    vp_ = ab_variant([0, 1, 2, 3])
    for k_, v_ in vp_.items():
        shared["abp_" + k_] = v_
    vs_ = [ab_variant([r_]) for r_ in range(4)]
    shared["ab_ngd"] = np.ascontiguousarray(np.broadcast_to(np.asarray(inp["delta_norm_g"], f32)[0][None], (64, 128)))
    awo = np.asarray(inp["ab_w_out"], f32)[0].reshape(8, 128, 8, 128)
    shared["ab_wo"] = np.ascontiguousarray(awo.transpose(2, 1, 0, 3))
    pp = np.arange(128)
    same = (pp[:, None] // 64) == (pp[None, :] // 64)
    le = same & (pp[:, None] <= pp[None, :])
    lt = same & (pp[:, None] < pp[None, :])
    ge = same & (pp[:, None] >= pp[None, :])
    gt_ = same & (pp[:, None] > pp[None, :])
    NEGV = np.float32(-30000.0)
    msk = np.stack([le.astype(f32), ge.astype(f32), same.astype(f32),
                    np.where(le, 0, NEGV), np.where(lt, 0, NEGV), np.where(ge, 0, NEGV), np.where(gt_, 0, NEGV)],
                   axis=1).astype(f32)
    shared["ab_masks"] = np.ascontiguousarray(msk)
    shared["ab_chind"] = np.ascontiguousarray(np.stack([(pp < 64), (pp >= 64)], axis=1).astype(f32))
    sC = np.asarray(inp["state_mlstm_C"], f32)[:, 0]
    sN = np.asarray(inp["state_mlstm_n"], f32)[:, 0]
    sM = np.asarray(inp["state_mlstm_m"], f32)[:, 0]
    sS = np.asarray(inp["state_delta_S"], f32)[:, 0]
    rel = np.asarray(inp["na_rel_bias"], f32)[0]
    ck = np.asarray(inp["cache_na_k"], f32)[:, 0]
    cvv_ = np.asarray(inp["cache_na_v"], f32)[:, 0]
    in_maps = []
    for core in range(NCORES):
        g = core // 4
        r = core % 4
        xtok = np.concatenate([xp[2 * core].reshape(256, D), xp[2 * core + 1].reshape(256, D),
                               xs[g, 512 * r:512 * (r + 1)]], axis=0)
        m = dict(shared)
        m["xT_in"] = np.ascontiguousarray(xtok.reshape(NTOK, 8, 128).transpose(2, 1, 0))
        m["ada_w"] = np.ascontiguousarray(aw6[:, :, :, r].transpose(0, 3, 2, 1, 4).reshape(36, 128, 8, 128))
        m["ada_b"] = np.ascontiguousarray(ab[:, r].transpose(2, 0, 1).reshape(128, 36))
        cvv = np.stack([c_ctx, c[g]], axis=-1)
        m["cvec3"] = np.ascontiguousarray(cvv.reshape(8, 128, 2).transpose(1, 0, 2))
        ohm = np.zeros((128, 4), f32)
        ohm[:, r] = 1.0
        m["onehot"] = ohm
        hs = slice(4 * r, 4 * r + 4)
        m["na_wqk_s"] = np.ascontiguousarray(np.stack(
            [shared["na_wqk"][2 * r], shared["na_wqk"][2 * r + 1],
             shared["na_wqk"][8 + 2 * r], shared["na_wqk"][8 + 2 * r + 1]], axis=0))
        wvs = nw[:, 2048 + 256 * r: 2048 + 256 * (r + 1)].reshape(8, 128, 256)
        m["na_wv_s"] = np.ascontiguousarray(wvs.transpose(1, 0, 2))
        kc = ck[g, hs]
        kc = kc.reshape(2, 2, 256, 64).transpose(1, 3, 0, 2)
        m["na_kc"] = np.ascontiguousarray(kc.reshape(128, 2, 256))
        vc = cvv_[g, hs]
        vc = vc.reshape(4, 2, 128, 64).transpose(2, 1, 0, 3)
        m["na_vc"] = np.ascontiguousarray(vc)
        m["na_bias"] = na_bias_table(rel[hs])
        for k_, v_ in vs_[r].items():
            m["abs_" + k_] = v_
        c0 = np.zeros((128, 2, 130), f32)
        c0[0:64, :, 0:128] = sC[g, :, r].transpose(1, 0, 2)
        c0[0:64, :, 128] = sN[g, :, r].T
        m["ab_c0"] = c0
        m["ab_m0"] = np.ascontiguousarray(np.broadcast_to(sM[g, :, r][None], (128, 2)))
        m["ab_s0"] = np.ascontiguousarray(sS[g, :, r].transpose(1, 0, 2))
        in_maps.append(m)
    return in_maps


def kernel(**inp):
    stage = inp.pop("_stage", "full")
    nc, es = build_program(stage)
    with es:
        in_maps = prep_inputs(inp)
        if stage.startswith("dbg"):
            for m in in_maps:
                for k_ in ("ada_w", "ffn_gu", "ffn_d"):
                    m.pop(k_, None)
        res = run_bass_kernel_spmd(nc, in_maps, core_ids=list(range(NCORES)))
    outs = res.results
    y_prompt = np.zeros((16, 256, D), np.float32)
    y_sample = np.zeros((2, 2048, D), np.float32)
    new_k = np.zeros((16, 1, 16, 256, 64), np.float32)
    new_v = np.zeros((16, 1, 16, 256, 64), np.float32)
    new_C = np.zeros((16, 1, 2, 4, 64, 128), np.float32)
    new_n = np.zeros((16, 1, 2, 4, 64), np.float32)
    new_m = np.zeros((16, 1, 2, 4), np.float32)
    new_S = np.zeros((16, 1, 2, 4, 128, 128), np.float32)
    for core in range(NCORES):
        g = core // 4
        r = core % 4
        yT = outs[core]["yT_out"]
        y = yT.transpose(2, 1, 0).reshape(NTOK, D)
        y_prompt[2 * core] = y[0:256]
        y_prompt[2 * core + 1] = y[256:512]
        y_sample[g, 512 * r:512 * (r + 1)] = y[512:1024]
        kT = outs[core]["kT_out"].reshape(2, 64, 8, 2, 256)
        new_k[2 * core:2 * core + 2, 0] = kT.transpose(3, 2, 0, 4, 1).reshape(2, 16, 256, 64)
        vv = outs[core]["v_out"].transpose(1, 0, 2).reshape(2, 256, 16, 64)
        new_v[2 * core:2 * core + 2, 0] = vv.transpose(0, 2, 1, 3)
        co = outs[core]["c_out"].reshape(2, 64, 2, 4, 130)
        so = outs[core]["s_out"].reshape(2, 128, 2, 4, 128)
        mo = outs[core]["m_out"].reshape(4, 2, 2)
        for si in range(2):
            bidx = 2 * core + si
            new_C[bidx, 0] = co[si, :, :, :, 0:128].transpose(1, 2, 0, 3)
            new_n[bidx, 0] = co[si, :, :, :, 128].transpose(1, 2, 0)
            new_m[bidx, 0] = mo[:, si, :].T
            new_S[bidx, 0] = so[si].transpose(1, 2, 0, 3)
    if stage in ("full", "ab"):
        return y_prompt, y_sample, new_C, new_n, new_m, new_S, new_k, new_v
    return y_prompt, y_sample, new_k, new_v
```

```python
import numpy as np
import ml_dtypes
from contextlib import ExitStack
import concourse.bass as bass
import concourse.mybir as mybir
from concourse.bass_utils import run_bass_kernel_spmd

F32 = mybir.dt.float32
BF16 = mybir.dt.bfloat16
AF = mybir.ActivationFunctionType
ALU = mybir.AluOpType
AX = mybir.AxisListType

D = 1024
DFF = 2816
NF = DFF // 128
DEPTH = 2
EPS = 1e-6
NTOK = 1024
BLK = 512
NCORES = 8


class Dep:
    __slots__ = ("name", "w", "rs", "sem", "hist", "excl")

    def __init__(self, name, excl=False):
        self.name = name
        self.excl = excl
        self.w = None
        self.rs = []
        self.sem = None
        self.hist = []


class Op:
    __slots__ = ("eng", "fn", "deps", "dma", "dst", "inc", "val", "idx", "incv")

    def __init__(self, eng, fn, deps, dma, dst, idx, incv=16):
        self.incv = incv
        self.eng = eng
        self.fn = fn
        self.deps = deps
        self.dma = dma
        self.dst = dst
        self.inc = False
        self.val = 0
        self.idx = idx


ENGS = ("pe", "act", "dve", "pool", "sp")


class Prog:
    def __init__(self, nc, es):
        self.nc = nc
        self.es = es
        self.ops = []
        self.ntile = 0

    def sb(self, shape, dt, name=None):
        self.ntile += 1
        return self.es.enter_context(self.nc.sbuf_tensor(name or f"t{self.ntile}", list(shape), dt))

    def psum(self, shape, dt, name=None):
        self.ntile += 1
        return self.es.enter_context(self.nc.psum_tensor(name or f"ps{self.ntile}", list(shape), dt))

    def add(self, eng, fn, reads=(), writes=(), dma=False, incv=16):
        idx = len(self.ops)
        deps = set()
        if any(d.excl for d in reads):
            writes = list(writes) + [d for d in reads if d.excl and d not in writes]
            reads = [d for d in reads if not d.excl]
        for d in reads:
            if d.w is not None:
                deps.add(d.w)
        for d in writes:
            if d.w is not None:
                deps.add(d.w)
            deps.update(d.rs)
        deps.discard(idx)
        last = {}
        keep = set()
        for di in deps:
            y = self.ops[di]
            if y.dma:
                keep.add(di)
            elif y.eng not in last or last[y.eng] < di:
                last[y.eng] = di
        keep.update(last.values())
        deps = keep
        for d in reads:
            d.rs.append(idx)
        for d in writes:
            d.w = idx
            d.rs = []
        op = Op(eng, fn, deps, dma, writes[0] if dma else None, idx, incv)
        self.ops.append(op)
        return op

    def mm(self, out, lhsT, rhs, start, stop, reads, writes):
        return self.add("pe", lambda e: e.matmul(out, lhsT=lhsT, rhs=rhs, start=start, stop=stop), reads, writes)

    def emit(self):
        nc = self.nc
        ops = self.ops
        for op in ops:
            for di in op.deps:
                y = ops[di]
                if y.dma:
                    continue
                if y.eng == "pe" and op.eng == "pe" and not op.dma:
                    continue
                y.inc = True
        engsem = {e: self.es.enter_context(nc.semaphore(f"sem_{e}")) for e in ENGS}
        cnt = {e: 0 for e in ENGS}
        for op in ops:
            if op.dma:
                d = op.dst
                if d.sem is None:
                    d.sem = self.es.enter_context(nc.semaphore(f"dsem_{d.name}"))
                c = (d.hist[-1][1] if d.hist else 0) + op.incv
                d.hist.append((op.idx, c))
                op.val = c
            elif op.inc:
                cnt[op.eng] += 1
                op.val = cnt[op.eng]
        per = {e: [] for e in ENGS}
        for op in ops:
            per[op.eng].append(op)

        def dma_wait_val(y, x):
            v = y.val
            for (i, c) in y.dst.hist:
                if i < x.idx:
                    v = max(v, c)
                else:
                    break
            return v

        def run_engine(ename, eobj):
            waited = {}
            for op in per[ename]:
                need = {}
                for di in op.deps:
                    y = ops[di]
                    if y.dma:
                        s = y.dst.sem
                        v = dma_wait_val(y, op)
                    else:
                        if y.eng == "pe" and op.eng == "pe" and not op.dma:
                            continue
                        s = engsem[y.eng]
                        v = y.val
                    k = id(s)
                    if k not in need or need[k][1] < v:
                        need[k] = (s, v)
                for k, (s, v) in need.items():
                    if waited.get(k, 0) >= v:
                        continue
                    waited[k] = v
                    eobj.wait_ge(s, v)
                ins = op.fn(eobj)
                if op.dma:
                    ins.then_inc(op.dst.sem, op.incv)
                elif op.inc:
                    ins.then_inc(engsem[op.eng], 1)
            last = {}
            for op in per[ename]:
                if op.dma:
                    last[id(op.dst.sem)] = (op.dst.sem, op.dst.hist[-1][1])
            for k, (s, v) in last.items():
                if waited.get(k, 0) < v:
                    eobj.wait_ge(s, v)

        with nc.Block() as block:
            @block.tensor
            def _(e):
                run_engine("pe", e)

            @block.scalar
            def _(e):
                run_engine("act", e)

            @block.vector
            def _(e):
                run_engine("dve", e)

            @block.gpsimd
            def _(e):
                run_engine("pool", e)

            @block.sync
            def _(e):
                run_engine("sp", e)


def build_program(stage="full"):
    nc = bass.Bass("TRN2", target_bir_lowering=False)
    es = ExitStack()
    P = Prog(nc, es)

    def din(name, shape, dt=F32):
        return nc.dram_tensor(name, list(shape), dt, kind="ExternalInput").ap()

    def dout(name, shape, dt=F32):
        return nc.dram_tensor(name, list(shape), dt, kind="ExternalOutput").ap()

    def dint(name, shape, dt):
        return nc.dram_tensor(name, list(shape), dt, kind="Internal").ap()

    xT_in = din("xT_in", [128, 8, NTOK])
    DBG = stage.startswith("dbg")
    ada_w = None if DBG else din("ada_w", [36, 128, 8, 128])
    ada_b = din("ada_b", [128, 36])
    cvec3 = din("cvec3", [128, 8, 2])
    agm_in = dint("agm_in", [128, 72], F32)
    agm_out = dint("agm_out", [4 * 128, 72], F32)
    normg = din("normg", [128, DEPTH * 3 + 1, 8])
    ffn_gu = None if DBG else din("ffn_gu", [DEPTH * 2, NF, 128, 2, 8, 128])
    ffn_d = None if DBG else din("ffn_d", [DEPTH * 2, 8, 128, NF, 128])
    ident_in = din("ident", [128, 128])
    oh_in = din("onehot", [128, 4])
    na_wqk = din("na_wqk", [16, 128, 8, 128])
    na_wv = din("na_wv", [4, 128, 8, 256])
    na_wo = din("na_wo", [8, 128, 8, 128])
    na_wqk_s = din("na_wqk_s", [4, 128, 8, 128])
    na_wv_s = din("na_wv_s", [128, 8, 256])
    na_kc = din("na_kc", [128, 2, 256])
    na_vc = din("na_vc", [128, 2, 4, 64])
    na_bias = din("na_bias", [64, 15, 4, 64])
    abp_wf = din("abp_wf", [20, 128, 8, 128])
    abp_wtA = din("abp_wtA", [4, 128, 8, 256])
    abp_wtB = din("abp_wtB", [4, 128, 8, 256])
    abp_gp = din("abp_gp", [128, 4, 4, 8])
    abp_cw = din("abp_cw", [128, 12, 5])
    abp_ngm = din("abp_ngm", [64, 512])
    abs_wf = din("abs_wf", [5, 128, 8, 128])
    abs_wtA = din("abs_wtA", [1, 128, 8, 256])
    abs_wtB = din("abs_wtB", [1, 128, 8, 256])
    abs_gp = din("abs_gp", [128, 4, 16, 2])
    abs_cw = din("abs_cw", [128, 3, 5])
    abs_ngm = din("abs_ngm", [64, 128])
    ab_ngd = din("ab_ngd", [64, 128])
    ab_wo = din("ab_wo", [8, 128, 8, 128])
    ab_c0 = din("ab_c0", [128, 2, 130])
    ab_m0 = din("ab_m0", [128, 2])
    ab_s0 = din("ab_s0", [128, 2, 128])
    ab_masks = din("ab_masks", [128, 7, 128])
    ab_chind = din("ab_chind", [128, 2])
    c_out = dout("c_out", [2, 64, 2 * 4 * 130])
    s_out = dout("s_out", [2, 128, 2 * 4 * 128])
    m_out = dout("m_out", [4, 4])
    yT_out = dout("yT_out", [128, 8, NTOK])
    kT_out = dout("kT_out", [128, 8, BLK])
    v_out = dout("v_out", [128, 4, 1024])
    ag1_in = dint("ag1_in", [128, 8 * BLK], BF16)
    ag1_out = dint("ag1_out", [4 * 128, 8 * BLK], BF16)
    ag2_in = dint("ag2_in", [128, 2 * 2048], BF16)
    ag2_out = dint("ag2_out", [4 * 128, 2 * 2048], BF16)
    GROUPS = [[0, 1, 2, 3], [4, 5, 6, 7]]

    xT = P.sb([128, 8, NTOK], F32, "xT")
    hT = P.sb([128, 8, NTOK], BF16, "hT")
    ARENA_KB = 104
    arena = P.sb([128, ARENA_KB * 256], F32, "arena")

    def AV(off_kb, shape, dt, parts=128):
        n = 1
        for s_ in shape[1:]:
            n *= s_
        nbytes = n * (2 if dt == BF16 else 4)
        w0 = int(off_kb * 256)
        assert w0 * 4 == int(off_kb * 1024)
        nw = (nbytes + 3) // 4
        assert w0 + nw <= ARENA_KB * 256, (off_kb, shape)
        v = arena[0:parts, w0:w0 + nw]
        if dt == BF16:
            v = v.bitcast(BF16)
        if len(shape) == 2:
            return v
        names = "abcdef"[:len(shape) - 1]
        pat = "p (" + " ".join(names) + ") -> p " + " ".join(names)
        kw = {names[i]: shape[1 + i] for i in range(len(shape) - 1)}
        return v.rearrange(pat, **kw)

    aT = AV(0, [128, NF, NTOK], BF16)
    sq = P.sb([128, 8, BLK], BF16, "sq")
    rstd = P.sb([128, BLK], F32, "rstd")
    tmp = [P.sb([128, BLK], F32, f"tmp{i}") for i in range(2)]
    sg = tmp
    NGU = 3
    gu_s = [P.sb([128, 2, 8, 128], BF16, f"gu{i}") for i in range(NGU)]
    NWD = 2
    wd_s = [P.sb([128, NF, 128], BF16, f"wd{i}") for i in range(NWD)]
    cv = P.sb([128, 8, 2], F32, "cv")
    scT = P.sb([128, 8, 2], BF16, "scT")
    adab = P.sb([128, 36], F32, "adab")
    modpart = P.sb([128, 36, 2], F32, "modpart")
    modall = P.sb([128, 4, 72], F32, "modall")
    ng = P.sb([128, DEPTH * 3 + 1, 8], F32, "ng")
    modT = P.sb([128, DEPTH, 72, 2], F32, "modT")
    Acoef = P.sb([128, DEPTH * 3, 8, 2], F32, "Acoef")
    Gcoef = P.sb([128, DEPTH * 3, 8, 2], F32, "Gcoef")
    ones_bf = P.sb([128, 128], BF16, "ones_bf")
    onespad = P.sb([128, 2, 128], BF16, "onespad")
    ident_f = P.sb([128, 128], F32, "ident_f")
    ident_b = P.sb([128, 128], BF16, "ident_b")
    oh = P.sb([128, 4], F32, "oh")
    epsb = P.sb([128, 4], F32, "epsb")
    mk = P.sb([128, 7, 128], F32, "mk")
    chind = P.sb([128, 2], F32, "chind")
    ones_f = P.sb([128, 128], F32, "ones_f")
    ps2 = [P.psum([128, 1024], F32, f"psb{i}") for i in range(4)]
    ps = [ps2[i // 2][:, (i % 2) * 512:(i % 2 + 1) * 512] for i in range(8)]

    d_x = [[Dep(f"x{b}_{j}") for j in range(8)] for b in range(2)]
    d_h = [[Dep(f"h{b}_{j}") for j in range(8)] for b in range(2)]
    d_a = [Dep(f"a{b}") for b in range(2)]
    d_sq = Dep("sq")
    d_rstd = Dep("rstd")
    d_tmp = [Dep("tmp0"), Dep("tmp1")]
    d_sg = d_tmp
    d_gu = [Dep(f"gu{i}") for i in range(NGU)]
    d_wd = [Dep(f"wd{i}") for i in range(NWD)]
    d_cv = Dep("cv")
    d_sc = Dep("scT")
    d_adab = Dep("adab")
    d_ng = Dep("ng")
    d_mod = Dep("modT")
    d_coef = Dep("coef")
    d_ones = Dep("ones")
    d_ps = [Dep(f"ps{i}", excl=True) for i in range(8)]
    d_y = Dep("yout")

    P.add("dve", lambda e: e.memset(ones_bf[:], 1.0), [], [d_ones])
    P.add("dve", lambda e: e.memset(epsb[:, 0:1], float(D * EPS)), [], [d_ones])
    P.add("dve", lambda e: e.memset(epsb[:, 1:2], 1e-6), [d_ones], [d_ones])
    P.add("dve", lambda e: e.memset(epsb[:, 2:3], 1.0), [d_ones], [d_ones])
    P.add("dve", lambda e: e.memset(epsb[:, 3:4], 1e-6), [d_ones], [d_ones])
    P.add("dve", lambda e: e.memset(ones_f[:], 1.0), [d_ones], [d_ones])
    d_msk = Dep("masks")
    P.add("sp", lambda e: e.dma_start(out=mk[:], in_=ab_masks), [], [d_msk], dma=True)
    P.add("sp", lambda e: e.dma_start(out=chind[:], in_=ab_chind), [d_msk], [d_msk], dma=True)
    P.add("dve", lambda e: e.memset(onespad[:], 0.0), [], [d_ones])
    P.add("dve", lambda e: e.memset(onespad[:, 0, 0:64], 1.0), [d_ones], [d_ones])
    P.add("dve", lambda e: e.memset(onespad[:, 1, 64:128], 1.0), [d_ones], [d_ones])
    d_id = Dep("ident")
    P.add("sp", lambda e: e.dma_start(out=ident_f[:], in_=ident_in), [], [d_id], dma=True)
    P.add("dve", lambda e: e.tensor_copy(out=ident_b[:], in_=ident_f[:]), [d_id], [d_ones])
    d_oh = Dep("oh")
    P.add("sp", lambda e: e.dma_start(out=oh[:], in_=oh_in), [], [d_oh], dma=True)
    for j in range(8):
        P.add("sp", lambda e, j=j: e.dma_start(out=xT[:, j, :], in_=xT_in[:, j, :]), [], [d_x[0][j]], dma=True)
        d_x[1][j].w = d_x[0][j].w
    P.add("sp", lambda e: e.dma_start(out=cv[:], in_=cvec3), [], [d_cv], dma=True)
    P.add("sp", lambda e: e.dma_start(out=adab[:], in_=ada_b), [], [d_adab], dma=True)
    P.add("sp", lambda e: e.dma_start(out=ng[:], in_=normg), [], [d_ng], dma=True)
    P.add("act", lambda e: e.activation(out=scT[:], in_=cv[:], func=AF.Silu), [d_cv], [d_sc])

    cnt_ada = 0
    if DBG:
        P.add("dve", lambda e: e.memset(modT[:].rearrange("p a b c -> p (a b c)"), 0.1), [], [d_mod])
    else:
        for q in range(36):
            s = cnt_ada % NGU
            cnt_ada += 1
            wv = gu_s[s][:, 0, :, :]
            P.add("pool", lambda e, q=q, wv=wv: e.dma_start(out=wv, in_=ada_w[q]), [], [d_gu[s]], dma=True)
            for k in range(8):
                P.mm(ps[7][:, 2 * q:2 * q + 2], wv[:, k, :], scT[:, k, :], k == 0, k == 7, [d_gu[s], d_sc], [d_ps[7]])
        d_mp = Dep("modpart")
        for v in range(2):
            P.add("dve", lambda e, v=v: e.tensor_tensor(
                out=modpart[:, :, v], in0=ps[7][:, 0:72].rearrange("p (q v) -> p q v", v=2)[:, :, v],
                in1=adab[:], op=ALU.add), [d_ps[7], d_adab], [d_mp])
        d_agmi, d_agmo, d_mall = Dep("agmi"), Dep("agmo"), Dep("modall")
        P.add("sp", lambda e: e.dma_start(out=agm_in, in_=modpart[:].rearrange("p q v -> p (q v)")), [d_mp], [d_agmi], dma=True)
        P.add("pool", lambda e: e.collective_compute("AllGather", ALU.bypass, replica_groups=GROUPS,
                                                     ins=[agm_in], outs=[agm_out]),
              [d_agmi], [d_agmo], dma=True, incv=1)
        P.add("sp", lambda e: e.dma_start(out=modall[:], in_=agm_out.rearrange("(r p) c -> p r c", p=128)),
              [d_agmo], [d_mall], dma=True)
        for l in range(DEPTH):
            srcv = modall[:, :, l * 36:(l + 1) * 36].rearrange("p r (j v) -> p r j v", v=2)
            dstv = modT[:, l].rearrange("p (r j) v -> p r j v", r=4)
            for v in range(2):
                P.add("dve", lambda e, srcv=srcv, dstv=dstv, v=v: e.tensor_copy(out=dstv[:, :, :, v], in_=srcv[:, :, :, v]),
                      [d_mall], [d_mod])
    for l in range(DEPTH):
        for s_ in range(3):
            ls = l * 3 + s_
            for v in range(2):
                sc_ap = modT[:, l, (3 * s_ + 1) * 8:(3 * s_ + 2) * 8, v]
                P.add("dve", lambda e, ls=ls, v=v, sc_ap=sc_ap: e.scalar_tensor_tensor(
                    out=Acoef[:, ls, :, v], in0=sc_ap, scalar=1.0, in1=ng[:, ls, :], op0=ALU.add, op1=ALU.mult),
                    [d_mod, d_ng], [d_coef])
                P.add("dve", lambda e, ls=ls, v=v: e.tensor_scalar(
                    out=Acoef[:, ls, :, v], in0=Acoef[:, ls, :, v], scalar1=32.0, scalar2=None, op0=ALU.mult),
                    [d_coef], [d_coef])
                g_ap = modT[:, l, (3 * s_ + 2) * 8:(3 * s_ + 3) * 8, v]
                fac = 1.0 if s_ == 1 else 0.5
                P.add("dve", lambda e, ls=ls, v=v, g_ap=g_ap, fac=fac: e.tensor_scalar(
                    out=Gcoef[:, ls, :, v], in0=g_ap, scalar1=fac, scalar2=None, op0=ALU.mult),
                    [d_mod], [d_coef])

    def rstd_from_ps6():
        P.add("act", lambda e: e.activation(out=rstd[:], in_=ps[6], func=AF.Ln, bias=epsb[:, 0:1], scale=1.0),
              [d_ps[6], d_ones], [d_rstd])
        P.add("act", lambda e: e.activation(out=rstd[:], in_=rstd[:], func=AF.Exp, scale=-0.5),
              [d_rstd], [d_rstd])

    def modulate(l, s_, b):
        ls = l * 3 + s_
        blk = slice(b * BLK, (b + 1) * BLK)
        P.add("act", lambda e: e.activation(out=sq[:], in_=xT[:, :, blk], func=AF.Square),
              d_x[b], [d_sq])
        for k in range(8):
            P.mm(ps[6], ones_bf[:], sq[:, k, :], k == 0, k == 7, [d_ones, d_sq], [d_ps[6]])
        rstd_from_ps6()
        for j in range(8):
            t = j % 2
            P.add("dve", lambda e, j=j, t=t: e.scalar_tensor_tensor(
                out=tmp[t][:], in0=xT[:, j, blk], scalar=Acoef[:, ls, j, b:b + 1], in1=rstd[:],
                op0=ALU.mult, op1=ALU.mult), [d_x[b][j], d_coef, d_rstd], [d_tmp[t]])
            P.add("act", lambda e, j=j, t=t: e.activation(
                out=hT[:, j, blk], in_=tmp[t][:], func=AF.Identity,
                bias=modT[:, l, 3 * s_ * 8 + j, b:b + 1], scale=1.0), [d_tmp[t], d_mod], [d_h[b][j]])

    state = {"gu": cnt_ada, "wd": 0, "psA": 0, "psB": 0, "sg": 0}

    def ffn(l, which):
        fi = l * 2 + which
        s_ = 0 if which == 0 else 2
        ls = l * 3 + s_
        for b in range(2):
            modulate(l, s_, b)
        for f in range(NF):
            s = state["gu"] % NGU
            state["gu"] += 1
            P.add("pool", lambda e, f=f, s=s: e.dma_start(out=gu_s[s][:], in_=ffn_gu[fi, f]), [], [d_gu[s]], dma=True)
            for b in range(2):
                blk = slice(b * BLK, (b + 1) * BLK)
                pa = (state["psA"] % 2) * 2
                state["psA"] += 1
                for w in range(2):
                    for k in range(8):
                        P.mm(ps[pa + w], gu_s[s][:, w, k, :], hT[:, k, blk], k == 0, k == 7,
                             [d_gu[s]] + d_h[b], [d_ps[pa + w]])
                t = state["sg"] % 2
                state["sg"] += 1
                P.add("act", lambda e, pa=pa, t=t: e.activation(out=sg[t][:], in_=ps[pa], func=AF.Silu),
                      [d_ps[pa]], [d_sg[t]])
                P.add("dve", lambda e, pa=pa, t=t, f=f, blk=blk: e.tensor_tensor(
                    out=aT[:, f, blk], in0=sg[t][:], in1=ps[pa + 1], op=ALU.mult),
                    [d_sg[t], d_ps[pa + 1]], [d_a[b]])
        for j in range(8):
            s = state["wd"] % NWD
            state["wd"] += 1
            P.add("pool", lambda e, j=j, s=s: e.dma_start(out=wd_s[s][:], in_=ffn_d[fi, j]), [], [d_wd[s]], dma=True)
            for b in range(2):
                blk = slice(b * BLK, (b + 1) * BLK)
                pb = 4 + (state["psB"] % 2)
                state["psB"] += 1
                for f in range(NF):
                    P.mm(ps[pb], wd_s[s][:, f, :], aT[:, f, blk], f == 0, f == NF - 1,
                         [d_wd[s], d_a[b]], [d_ps[pb]])
                P.add("dve", lambda e, j=j, b=b, blk=blk, pb=pb: e.scalar_tensor_tensor(
                    out=xT[:, j, blk], in0=ps[pb], scalar=Gcoef[:, ls, j, b:b + 1], in1=xT[:, j, blk],
                    op0=ALU.mult, op1=ALU.add), [d_ps[pb], d_coef, d_x[b][j]], [d_x[b][j]])

    def out_proj(l, wo_dram, mix, d_mix, b):
        ls = l * 3 + 1
        blk = slice(b * BLK, (b + 1) * BLK)
        for j in range(8):
            s = state["gu"] % NGU
            state["gu"] += 1
            wv = gu_s[s][:, 0, :, :]
            P.add("pool", lambda e, j=j, wv=wv: e.dma_start(out=wv, in_=wo_dram[j]), [], [d_gu[s]], dma=True)
            pb = 4 + (state["psB"] % 2)
            state["psB"] += 1
            for c in range(8):
                P.mm(ps[pb], wv[:, c, :], mix[:, c, :], c == 0, c == 7, [d_gu[s]] + d_mix, [d_ps[pb]])
            P.add("dve", lambda e, j=j, pb=pb: e.scalar_tensor_tensor(
                out=xT[:, j, blk], in0=ps[pb], scalar=Gcoef[:, ls, j, b:b + 1], in1=xT[:, j, blk],
                op0=ALU.mult, op1=ALU.add), [d_ps[pb], d_coef, d_x[b][j]], [d_x[b][j]])

    def allgather(src_ap_sb, d_src, ag_in_view, ag_in, ag_out, d_agin, d_agout):
        P.add("sp", lambda e: e.dma_start(out=ag_in_view, in_=src_ap_sb), d_src, [d_agin], dma=True)
        P.add("pool", lambda e: e.collective_compute("AllGather", ALU.bypass, replica_groups=GROUPS,
                                                     ins=[ag_in], outs=[ag_out]),
              [d_agin], [d_agout], dma=True, incv=1)

    def na_layer(l, stop=None):
        for b in range(2):
            modulate(l, 1, b)
        if stop == "P0":
            return
        d_ag1i, d_ag1o = Dep("ag1i"), Dep("ag1o")
        allgather(hT[:, :, BLK:2 * BLK], d_h[1], ag1_in.rearrange("p (k t) -> p k t", k=8), ag1_in, ag1_out, d_ag1i, d_ag1o)
        if stop == "P0b":
            return
        qT_p = AV(0, [128, 8, BLK], BF16)
        kT_p = AV(8, [128, 8, BLK], BF16)
        kT32 = AV(16, [128, 8, BLK], F32)
        v32 = AV(32, [128, 4, 1024], F32)
        Vpad = AV(48, [128, 4, 8, 2, 128], BF16)
        PTp = [AV(64 + 2 * i, [128, 4, 256], BF16) for i in range(2)]
        OT_p = AV(68, [128, 8, BLK], BF16)
        rcp = [AV(76 + i, [128, 256], F32) for i in range(2)]
        d_q, d_k, d_k32, d_v32, d_vpad = Dep("qTp"), Dep("kTp"), Dep("kT32"), Dep("v32"), Dep("Vpad")
        d_pt = [Dep("PTp0"), Dep("PTp1")]
        d_ot = Dep("OTp")
        d_rcp = [Dep("rcp0"), Dep("rcp1")]
        for dd in (d_q, d_k, d_k32, d_v32, d_vpad, d_pt[0], d_pt[1], d_ot, d_rcp[0], d_rcp[1]):
            dd.rs = list(d_a[0].rs) + list(d_a[1].rs)
            dd.w = d_a[1].w
        for i_ in range(4):
            P.add("dve", lambda e, i_=i_: e.memset(AV(48 + 4 * i_, [128, 2048], BF16), 0.0), [], [d_vpad])
        pj = 0
        for c in range(16):
            s = state["gu"] % NGU
            state["gu"] += 1
            wv = gu_s[s][:, 0, :, :]
            P.add("pool", lambda e, c=c, wv=wv: e.dma_start(out=wv, in_=na_wqk[c]), [], [d_gu[s]], dma=True)
            pb = 6 + (pj % 2)
            pj += 1
            for k in range(8):
                P.mm(ps[pb], wv[:, k, :], hT[:, k, 0:BLK], k == 0, k == 7, [d_gu[s]] + d_h[0], [d_ps[pb]])
            if c < 8:
                P.add("act", lambda e, c=c, pb=pb: e.activation(out=qT_p[:, c, :], in_=ps[pb], func=AF.Identity,
                                                              scale=0.125), [d_ps[pb]], [d_q])
            else:
                P.add("act", lambda e, c=c, pb=pb: e.activation(out=kT32[:, c - 8, :], in_=ps[pb], func=AF.Identity),
                      [d_ps[pb]], [d_k32])
                P.add("dve", lambda e, c=c, pb=pb: e.tensor_copy(out=kT_p[:, c - 8, :], in_=ps[pb]),
                      [d_ps[pb]], [d_k])
        if stop == "P0c":
            return
        P.add("sp", lambda e: e.dma_start(out=kT_out, in_=kT32), [d_k32], [d_y], dma=True)
        for qv in range(4):
            s = state["gu"] % NGU
            state["gu"] += 1
            wv = gu_s[s][:].rearrange("p a b c -> p (a b c)").rearrange("p (k n) -> p k n", k=8)
            P.add("pool", lambda e, qv=qv, wv=wv: e.dma_start(out=wv, in_=na_wv[qv]), [], [d_gu[s]], dma=True)
            for tt in range(4):
                pb = 6 + (pj % 2)
                pj += 1
                for k in range(8):
                    P.mm(ps[pb][:, 0:256], hT[:, k, tt * 128:(tt + 1) * 128], wv[:, k, :], k == 0, k == 7,
                         [d_gu[s]] + d_h[0], [d_ps[pb]])
                P.add("act", lambda e, tt=tt, qv=qv, pb=pb: e.activation(
                    out=v32[:, tt, qv * 256:(qv + 1) * 256], in_=ps[pb][:, 0:256], func=AF.Identity), [d_ps[pb]], [d_v32])
                for e_ in range(2):
                    src = ps[pb][:, 0:256].rearrange("p (c e d) -> p c e d", c=2, e=2)[:, :, e_, :]
                    P.add("dve", lambda e, tt=tt, qv=qv, e_=e_, src=src: e.tensor_copy(
                        out=Vpad[:, tt, qv * 2:(qv + 1) * 2, e_, e_ * 64:(e_ + 1) * 64], in_=src),
                        [d_ps[pb]], [d_vpad])
        P.add("sp", lambda e: e.dma_start(out=v_out, in_=v32), [d_v32], [d_y], dma=True)
        if stop == "P1":
            return
        it = 0
        for sq_ in range(2):
            t0 = sq_ * 256
            for c in range(8):
                S2 = ps2[it % 2]
                dS = [d_ps[(it % 2) * 2], d_ps[(it % 2) * 2 + 1]]
                for hh in range(2):
                    for sb in range(2):
                        i4 = hh * 2 + sb
                        P.mm(S2[:, i4 * 256:(i4 + 1) * 256],
                             kT_p[hh * 64:(hh + 1) * 64, c, t0 + sb * 128:t0 + (sb + 1) * 128],
                             qT_p[hh * 64:(hh + 1) * 64, c, t0:t0 + 256], True, True,
                             [d_k, d_q], [dS[i4 // 2]])
                pt = PTp[it % 2]
                P.add("act", lambda e, S2=S2, pt=pt: e.activation(
                    out=pt.rearrange("p a b -> p (a b)"), in_=S2[:, :], func=AF.Exp), dS, [d_pt[it % 2]])
                pb = 4 + (it % 2)
                n = 0
                for hh in range(2):
                    for sb in range(2):
                        P.mm(ps[pb][:, 0:256], Vpad[:, sq_ * 2 + sb, c, hh, :], pt[:, hh * 2 + sb, :], n == 0, n == 3,
                             [d_vpad, d_pt[it % 2]], [d_ps[pb]])
                        n += 1
                n = 0
                for hh in range(2):
                    for sb in range(2):
                        P.mm(ps[pb][:, 256:512], onespad[:, hh, :], pt[:, hh * 2 + sb, :], n == 0, n == 3,
                             [d_ones, d_pt[it % 2]], [d_ps[pb]])
                        n += 1
                rc = rcp[it % 2]
                P.add("dve", lambda e, rc=rc, pb=pb: e.reciprocal(out=rc, in_=ps[pb][:, 256:512]),
                      [d_ps[pb]], [d_rcp[it % 2]])
                P.add("dve", lambda e, rc=rc, pb=pb, c=c, t0=t0: e.tensor_tensor(
                    out=OT_p[:, c, t0:t0 + 256], in0=ps[pb][:, 0:256], in1=rc, op=ALU.mult),
                    [d_ps[pb], d_rcp[it % 2]], [d_ot])
                it += 1
        if stop == "P2":
            return
        out_proj(l, na_wo, OT_p, [d_ot], 0)
        if stop == "P3":
            return

        hfull = AV(0, [128, 8, 2048], BF16)
        qT_s = AV(32, [128, 2, 2048], BF16)
        kTpad = AV(40, [128, 2, 2, 2048], BF16)
        v64 = AV(56, [128, 32, 2, 2, 128], BF16)
        biasT = AV(0, [128, 15, 4, 64], BF16)
        kcpad = AV(8, [128, 2, 2, 256], BF16)
        vcb = AV(10, [128, 2, 4, 64], BF16)
        Vcpad = AV(11, [128, 2, 2, 2, 128], BF16)
        PTl = [AV(13 + 2 * i, [128, 2, 8, 64], BF16) for i in range(2)]
        PTc = [AV(17 + 0.5 * i, [128, 2, 2, 64], BF16) for i in range(2)]
        rcs = [AV(18 + 0.25 * i, [128, 64], F32) for i in range(2)]
        OT_s = AV(20, [128, 2, 2048], BF16)
        d_hf = Dep("hfull")
        d_qs, d_ks, d_v64 = Dep("qTs"), Dep("kTs"), Dep("v64")
        d_bias, d_kc, d_vcb, d_vcp = Dep("biasT"), Dep("kcT"), Dep("vcb"), Dep("Vcpad")
        d_ptl = [Dep("PTl0"), Dep("PTl1")]
        d_ptc = [Dep("PTc0"), Dep("PTc1")]
        d_rcs = [Dep("rcs0"), Dep("rcs1")]
        d_ots = Dep("OTs")
        last_prompt = [d_q, d_k, d_k32, d_v32, d_vpad, d_pt[0], d_pt[1], d_ot, d_rcp[0], d_rcp[1]]

        def users_of(deplist):
            rs = []
            for o in deplist:
                rs += list(o.rs)
                if o.w is not None:
                    rs.append(o.w)
            return rs
        for dd in (d_hf, d_qs, d_ks, d_v64):
            dd.rs = users_of(last_prompt) + users_of(d_a)
            dd.w = None
        for m in range(4):
            P.add("sp", lambda e, m=m: e.dma_start(
                out=hfull[:, :, m * BLK:(m + 1) * BLK],
                in_=ag1_out[m * 128:(m + 1) * 128, :].rearrange("p (k t) -> p k t", k=8)),
                [d_ag1o], [d_hf], dma=True)
        for i_ in range(4):
            P.add("dve", lambda e, i_=i_: e.memset(AV(40 + 4 * i_, [128, 2048], BF16), 0.0), [], [d_ks])
        for i_ in range(16):
            P.add("pool", lambda e, i_=i_: e.memset(AV(56 + 2 * i_, [128, 1024], BF16), 0.0), [], [d_v64])
        if stop == "S1":
            return
        for c4 in range(4):
            s = state["gu"] % NGU
            state["gu"] += 1
            wv = gu_s[s][:, 0, :, :]
            P.add("pool", lambda e, c4=c4, wv=wv: e.dma_start(out=wv, in_=na_wqk_s[c4]), [], [d_gu[s]], dma=True)
            for tb in range(4):
                pb = 6 + (pj % 2)
                pj += 1
                for k in range(8):
                    P.mm(ps[pb], wv[:, k, :], hfull[:, k, tb * BLK:(tb + 1) * BLK], k == 0, k == 7,
                         [d_gu[s], d_hf], [d_ps[pb]])
                if c4 < 2:
                    P.add("act", lambda e, c4=c4, tb=tb, pb=pb: e.activation(
                        out=qT_s[:, c4, tb * BLK:(tb + 1) * BLK], in_=ps[pb], func=AF.Identity, scale=0.125),
                        [d_ps[pb]], [d_qs])
                else:
                    P.add("act", lambda e, c4=c4, tb=tb, pb=pb: e.activation(
                        out=kTpad[0:64, c4 - 2, 0, tb * BLK:(tb + 1) * BLK], in_=ps[pb][0:64, :], func=AF.Identity),
                        [d_ps[pb]], [d_ks])
                    P.add("act", lambda e, c4=c4, tb=tb, pb=pb: e.activation(
                        out=kTpad[64:128, c4 - 2, 1, tb * BLK:(tb + 1) * BLK], in_=ps[pb][64:128, :],
                        func=AF.Identity), [d_ps[pb]], [d_ks])
        s = state["gu"] % NGU
        state["gu"] += 1
        wvs = gu_s[s][:].rearrange("p a b c -> p (a b c)").rearrange("p (k n) -> p k n", k=8)
        P.add("pool", lambda e: e.dma_start(out=wvs, in_=na_wv_s), [], [d_gu[s]], dma=True)
        for rp in range(16):
            pb = 6 + (pj % 2)
            pj += 1
            for r2 in range(2):
                rho = rp * 2 + r2
                for k in range(8):
                    P.mm(ps[pb][0:64, r2 * 256:(r2 + 1) * 256], hfull[:, k, rho * 64:(rho + 1) * 64], wvs[:, k, :],
                         k == 0, k == 7, [d_gu[s], d_hf], [d_ps[pb]])
            for e_ in range(2):
                src = ps[pb][0:64, :].rearrange("p (r c e d) -> p r c e d", r=2, c=2, e=2)[:, :, :, e_, :]
                P.add("act", lambda e, rp=rp, e_=e_, src=src: e.activation(
                    out=v64[0:64, rp * 2:rp * 2 + 2, :, e_, e_ * 64:(e_ + 1) * 64], in_=src, func=AF.Identity),
                    [d_ps[pb]], [d_v64])
        if stop == "S2":
            return
        for dd in (d_bias, d_kc, d_vcb, d_vcp, d_ptl[0], d_ptl[1], d_ptc[0], d_ptc[1], d_rcs[0], d_rcs[1], d_ots):
            dd.rs = users_of([d_hf])
            dd.w = None
        P.add("dve", lambda e: e.memset(AV(0, [128, 2048], BF16), 0.0), [], [d_bias])
        P.add("dve", lambda e: e.memset(AV(4, [128, 2048], BF16), 0.0), [], [d_bias])
        P.add("pool", lambda e: e.dma_start(out=biasT[0:64], in_=na_bias), [d_bias], [d_bias], dma=True)
        P.add("dve", lambda e: e.memset(AV(8, [128, 1024], BF16), 0.0), [], [d_kc])
        for hh in range(2):
            P.add("pool", lambda e, hh=hh: e.dma_start(out=kcpad[hh * 64:(hh + 1) * 64, :, hh, :],
                                                       in_=na_kc[hh * 64:(hh + 1) * 64]),
                  [d_kc], [d_kc], dma=True)
        P.add("pool", lambda e: e.dma_start(out=vcb, in_=na_vc), [], [d_vcb], dma=True)
        P.add("dve", lambda e: e.memset(AV(11, [128, 1024], BF16), 0.0), [], [d_vcp])
        for e_ in range(2):
            for c in range(2):
                P.add("dve", lambda e, e_=e_, c=c: e.tensor_copy(
                    out=Vcpad[:, :, c, e_, e_ * 64:(e_ + 1) * 64], in_=vcb[:, :, 2 * c + e_, :]),
                    [d_vcb, d_vcp], [d_vcp])
        for i in range(2):
            P.add("dve", lambda e, i=i: e.memset(AV(13 + 2 * i, [128, 1024], BF16), 0.0), [], [d_ptl[i]])
        it = 0
        for c in range(2):
            for R in range(32):
                k0 = min(max(R - 4, 0), 24)
                bi = it % 2
                S2 = ps2[bi]
                dS = [d_ps[bi * 2], d_ps[bi * 2 + 1]]
                for hh in range(2):
                    for i in range(8):
                        rho = k0 + i
                        dr = rho - R
                        o_ap = S2[0:64, hh * 512 + i * 64: hh * 512 + (i + 1) * 64]
                        P.mm(o_ap, kTpad[:, c, hh, rho * 64:(rho + 1) * 64],
                             qT_s[:, c, R * 64:(R + 1) * 64], True, False,
                             [d_ks, d_qs], [dS[hh]])
                        P.mm(o_ap, ident_b[:, 0:64], biasT[:, dr + 7, 2 * c + hh, :], False, True,
                             [d_ones, d_bias], [dS[hh]])
                pc = 6 + bi
                for hh in range(2):
                    for sb in range(2):
                        i4 = hh * 2 + sb
                        P.mm(ps[pc][:, i4 * 64:(i4 + 1) * 64],
                             kcpad[:, c, hh, sb * 128:(sb + 1) * 128],
                             qT_s[:, c, R * 64:(R + 1) * 64], True, True,
                             [d_kc, d_qs], [d_ps[pc]])
                ptl, ptc = PTl[bi], PTc[bi]
                P.add("act", lambda e, S2=S2, ptl=ptl: e.activation(
                    out=ptl[0:64].rearrange("p a b c -> p (a b c)"), in_=S2[0:64, :], func=AF.Exp), dS, [d_ptl[bi]])
                P.add("act", lambda e, pc=pc, ptc=ptc: e.activation(
                    out=ptc.rearrange("p a b c -> p (a b c)"), in_=ps[pc][:, 0:256], func=AF.Exp),
                    [d_ps[pc]], [d_ptc[bi]])
                pb = 4 + bi
                for which in range(2):
                    n = 0
                    o_ap = ps[pb][:, which * 64:(which + 1) * 64]
                    for hh in range(2):
                        for i in range(8):
                            rho = k0 + i
                            lhs = v64[:, rho, c, hh, :] if which == 0 else onespad[:, hh, :]
                            P.mm(o_ap, lhs, ptl[:, hh, i, :], n == 0, False,
                                 [d_v64 if which == 0 else d_ones, d_ptl[bi]], [d_ps[pb]])
                            n += 1
                    for hh in range(2):
                        for sb in range(2):
                            lhs = Vcpad[:, sb, c, hh, :] if which == 0 else onespad[:, hh, :]
                            P.mm(o_ap, lhs, ptc[:, hh, sb, :], False, (hh == 1 and sb == 1),
                                 [d_vcp if which == 0 else d_ones, d_ptc[bi]], [d_ps[pb]])
                rc = rcs[bi]
                P.add("dve", lambda e, rc=rc, pb=pb: e.reciprocal(out=rc, in_=ps[pb][:, 64:128]),
                      [d_ps[pb]], [d_rcs[bi]])
                P.add("dve", lambda e, rc=rc, pb=pb, c=c, R=R: e.tensor_tensor(
                    out=OT_s[:, c, R * 64:(R + 1) * 64], in0=ps[pb][:, 0:64], in1=rc, op=ALU.mult),
                    [d_ps[pb], d_rcs[bi]], [d_ots])
                it += 1
        if stop == "S3":
            return
        d_ag2i, d_ag2o = Dep("ag2i"), Dep("ag2o")
        allgather(OT_s, [d_ots], ag2_in.rearrange("p (c t) -> p c t", c=2), ag2_in, ag2_out, d_ag2i, d_ag2o)
        cand = AV(28, [128, 4, 8, BLK], BF16)
        mixS = AV(60, [128, 8, BLK], BF16)
        d_cand, d_mixs = Dep("cand"), Dep("mixS")
        d_cand.rs = users_of([d_qs, d_ks, d_v64])
        d_mixs.rs = users_of([d_v64])
        for m in range(4):
            for c in range(2):
                P.add("sp", lambda e, m=m, c=c: e.dma_start(
                    out=cand[:, :, 2 * m + c, :],
                    in_=ag2_out[m * 128:(m + 1) * 128, c * 2048:(c + 1) * 2048].rearrange("p (r t) -> p r t", r=4)),
                    [d_ag2o], [d_cand], dma=True)
        for rr in range(4):
            if rr == 0:
                P.add("dve", lambda e: e.tensor_scalar(
                    out=mixS.rearrange("p a b -> p (a b)"), in0=cand[:, 0].rearrange("p a b -> p (a b)"),
                    scalar1=oh[:, 0:1], scalar2=None, op0=ALU.mult), [d_cand, d_oh], [d_mixs])
            else:
                P.add("dve", lambda e, rr=rr: e.scalar_tensor_tensor(
                    out=mixS.rearrange("p a b -> p (a b)"), in0=cand[:, rr].rearrange("p a b -> p (a b)"),
                    scalar=oh[:, rr:rr + 1], in1=mixS.rearrange("p a b -> p (a b)"), op0=ALU.mult, op1=ALU.add),
                    [d_cand, d_oh, d_mixs], [d_mixs])
        out_proj(l, na_wo, mixS, [d_mixs], 1)
        tail = []
        for o in (d_cand, d_mixs, d_ots, d_v64, d_ks, d_qs, d_bias, d_kc, d_vcb, d_vcp, d_ptl[0], d_ptl[1],
                  d_ptc[0], d_ptc[1], d_rcs[0], d_rcs[1]):
            tail += list(o.rs)
            if o.w is not None:
                tail.append(o.w)
        for b in range(2):
            d_a[b].rs = list(d_a[b].rs) + tail

    def ab_stage(cfg):
        nh, T, seqs, src, d_src = cfg["nh"], cfg["T"], cfg["seqs"], cfg["src"], cfg["d_src"]
        wf, wtA, wtB, gp_d, cw_d, ngm_d, ngd_d = cfg["wf"], cfg["wtA"], cfg["wtB"], cfg["gp"], cfg["cw"], cfg["ngm"], cfg["ngd"]
        d_mix = cfg["d_mix"]
        NT, NCHK, n2 = T // 128, T // 64, 2 * nh
        qmT = AV(0, [128, nh, T], BF16)
        kmT = AV(4, [128, nh, T], BF16)
        qnT = AV(8, [128, nh, T], BF16)
        knT = AV(12, [128, nh, T], BF16)
        vdT = AV(16, [128, nh, T], BF16)
        vaug = AV(20, [128, NT * nh, 130], BF16)
        kmtok = AV(24.5, [128, NT * nh, 64], BF16)
        gates = AV(26.5, [128, NT, 4, n2], F32)
        NQ = 13
        gt = AV(27, [128, NT, NQ, n2], F32)
        om = AV(29, [64, NCHK * nh, 128], BF16, parts=64)
        zd = AV(37, [64, NCHK * nh, 128], BF16, parts=64)
        hacc = AV(45, [64, NCHK * nh, 128], BF16, parts=64)
        oacc = AV(53, [64, NCHK * nh, 128], BF16, parts=64)
        XL = T + 4 * len(seqs)
        xin = AV(45, [128, XL], BF16)
        caccs = [AV(49.5, [128, 512], F32), AV(54, [128, 512], F32)]
        csqs = [AV(49.5, [128, 512], BF16), AV(54, [128, 512], BF16)]
        knT32 = sq[:].rearrange("p a b -> p (a b)").bitcast(F32).rearrange("p (h t) -> p h t", h=nh)
        ctmps = [AV(51.5, [128, 512], F32), AV(56, [128, 512], F32)]
        Caug = AV(78, [128, 2, nh, 130], F32)
        Cbf = AV(82.5, [128, 2, nh, 130], BF16)
        Sst = AV(85, [128, 2, nh, 128], F32)
        Sbf = AV(89, [128, 2, nh, 128], BF16)
        hflat = hT[:].rearrange("p a b -> p (a b)")

        def AVH(off_kb, shape, dt, parts=128):
            n = 1
            for s_ in shape[1:]:
                n *= s_
            nbytes = n * (2 if dt == BF16 else 4)
            e0 = int(off_kb * 512)
            ne = (nbytes + 1) // 2
            assert e0 + ne <= 8 * NTOK
            v = hflat[0:parts, e0:e0 + ne]
            if dt == F32:
                v = v.bitcast(F32)
            if len(shape) == 2:
                return v
            names = "abcdef"[:len(shape) - 1]
            pat = "p (" + " ".join(names) + ") -> p " + " ".join(names)
            kw_ = {names[i]: shape[1 + i] for i in range(len(shape) - 1)}
            return v.rearrange(pat, **kw_)

        def alloc_set(sn):
            o_ = [91.0 if sn == 0 else 0.0]
            AVx = AV if sn == 0 else AVH

            def TMP(kb, shape, dt, parts=128):
                v = AVx(o_[0], shape, dt, parts)
                o_[0] += kb
                return v
            T = {}
            for nm in ("rhs1", "rhs2", "rhs3", "rhsD", "EB", "DT", "EG", "FA", "FB", "FC"):
                T[nm] = TMP(0.5, [128, 128], F32)
            for nm in ("swT", "qbT", "QKdT", "wT", "qgT", "vnew"):
                T[nm] = TMP(0.25, [128, 128], BF16)
            T["kw"] = TMP(0.25, [128, 2, 64], BF16)
            T["wsc"] = TMP(0.25, [128, 2], F32)
            T["Pm"] = [TMP(0.5, [128, 128], F32) for _ in range(2)]
            T["PTm"] = [TMP(0.5, [128, 128], F32) for _ in range(2)]
            T["X"] = TMP(1.0, [128, 256], F32)
            T["kd"] = TMP(0.5, [128, 2, 128], BF16)
            T["kdcc"] = TMP(0.25, [128, 2], F32)
            T["dn"] = TMP(0.25, [128, 2], F32)
            T["_end"] = o_[0]
            return T
        TS = [alloc_set(0), alloc_set(1)]
        o_ = [TS[0]["_end"]]

        def TMP(kb, shape, dt, parts=128):
            v = AV(o_[0], shape, dt, parts)
            o_[0] += kb
            return v
        assert o_[0] <= ARENA_KB, o_[0]
        assert TS[1]["_end"] <= 16.0
        gpt = P.sb([128, 4, NT, n2], F32, f"gpt{nh}")
        expA = P.sb([128, NT, n2], F32, f"expA{nh}")
        cwt = P.sb([128, 3 * nh, 5], F32, f"cwt{nh}")
        ngm = P.sb([64, nh * 128], F32, f"ngm{nh}")
        ngd = P.sb([64, 128], F32, f"ngd{nh}")
        D_ = {}

        ALIAS = {"hacc": ["xin", "cacc0", "ctmp0", "cacc1", "ctmp1"], "oacc": ["xin", "cacc0", "ctmp0", "cacc1", "ctmp1"]}

        def dp(name):
            if name not in D_:
                D_[name] = Dep(f"ab{nh}_{name}")
                rs = list(cfg["prev_users"]) + list(cfg["extra_users"]())
                if name.endswith("_1"):
                    for b_ in range(2):
                        for j_ in range(8):
                            rs += list(d_h[b_][j_].rs)
                            if d_h[b_][j_].w is not None:
                                rs.append(d_h[b_][j_].w)
                for o in ALIAS.get(name, []):
                    if o in D_:
                        rs += list(D_[o].rs)
                        if D_[o].w is not None:
                            rs.append(D_[o].w)
                D_[name].rs = rs
            return D_[name]
        d_par = dp("par")
        d_par.rs = []
        P.add("sp", lambda e: e.dma_start(out=gpt[:], in_=gp_d), [], [d_par], dma=True)
        d_cw = Dep(f"cw{nh}")
        P.add("sp", lambda e: e.dma_start(out=cwt[:], in_=cw_d), [], [d_cw], dma=True)
        d_ngm = Dep(f"ngm{nh}")
        P.add("sp", lambda e: e.dma_start(out=ngm[:], in_=ngm_d), [], [d_ngm], dma=True)
        d_ngd = Dep(f"ngd{nh}")
        P.add("sp", lambda e: e.dma_start(out=ngd[:], in_=ngd_d), [], [d_ngd], dma=True)
        d_expA = Dep(f"expA{nh}")
        P.add("act", lambda e: e.activation(out=expA[:], in_=gpt[:, 3], func=AF.Exp), [d_par], [d_expA])

        P.add("dve", lambda e: e.memset(xin, 0.0), [], [dp("xin")])
        segs = []
        xo = 0
        for (t0, nt) in seqs:
            L = nt * 128
            for a in range(0, L, 512):
                ln = min(512, L - a)
                segs.append((xo + a, t0 * 128 + a, ln))
            xo += L + 4
        seq_x0 = {}
        xo = 0
        for (t0, nt) in seqs:
            seq_x0[t0] = xo
            xo += nt * 128 + 4
        pj = [0]
        segc = [0]

        def xin_pos(tok):
            for (t0, nt) in seqs:
                if t0 * 128 <= tok < (t0 + nt) * 128:
                    return seq_x0[t0] + 2 + (tok - t0 * 128)
            raise ValueError
        for ci in range(5 * nh):
            kind, h = ci // nh, ci % nh
            s = state["gu"] % NGU
            state["gu"] += 1
            wv = gu_s[s][:, 0, :, :]
            P.add("pool", lambda e, ci=ci, wv=wv: e.dma_start(out=wv, in_=wf[ci]), [], [d_gu[s]], dma=True)
            for p0 in range(0, T, 512):
                pb = 6 + (pj[0] % 2)
                pj[0] += 1
                for k in range(8):
                    P.mm(ps[pb], wv[:, k, :], src(k, p0, p0 + 512), k == 0, k == 7, [d_gu[s]] + d_src, [d_ps[pb]])
                if kind == 0:
                    P.add("act", lambda e, h=h, p0=p0, pb=pb: e.activation(
                        out=qmT[:, h, p0:p0 + 512], in_=ps[pb], func=AF.Identity, scale=0.125), [d_ps[pb]], [dp("qmT")])
                elif kind == 1:
                    P.add("act", lambda e, h=h, p0=p0, pb=pb: e.activation(
                        out=kmT[:, h, p0:p0 + 512], in_=ps[pb], func=AF.Identity), [d_ps[pb]], [dp("kmT")])
                else:
                    a = p0
                    while a < p0 + 512:
                        for (t0, nt) in seqs:
                            if t0 * 128 <= a < (t0 + nt) * 128:
                                b_ = min(p0 + 512, (t0 + nt) * 128)
                        xp = xin_pos(a)
                        P.add("act", lambda e, a=a, b_=b_, xp=xp, pb=pb, p0=p0: e.activation(
                            out=xin[:, xp:xp + (b_ - a)], in_=ps[pb][:, a - p0:b_ - p0], func=AF.Identity),
                            [d_ps[pb]], [dp("xin")])
                        a = b_
            if kind >= 2:
                cidx = (kind - 2) * nh + h
                for (xo_, tok0, ln) in segs:
                    bsel = segc[0] % 2
                    segc[0] += 1
                    cacc, csq, ctmp = caccs[bsel], csqs[bsel], ctmps[bsel]
                    CA, CT = "cacc%d" % bsel, "ctmp%d" % bsel
                    for j in range(5):
                        sl = xin[:, xo_ + j:xo_ + j + ln]
                        if j == 0:
                            P.add("dve", lambda e, cacc=cacc, csq=csq, ctmp=ctmp, sl=sl, ln=ln, cidx=cidx: e.tensor_scalar(
                                out=cacc[:, 0:ln], in0=sl, scalar1=cwt[:, cidx, 0:1], scalar2=None, op0=ALU.mult),
                                [dp("xin"), d_cw], [dp(CA)])
                        else:
                            P.add("dve", lambda e, cacc=cacc, csq=csq, ctmp=ctmp, sl=sl, ln=ln, cidx=cidx, j=j: e.scalar_tensor_tensor(
                                out=cacc[:, 0:ln], in0=sl, scalar=cwt[:, cidx, j:j + 1], in1=cacc[:, 0:ln],
                                op0=ALU.mult, op1=ALU.add), [dp("xin"), d_cw, dp(CA)], [dp(CA)])
                    if kind == 4:
                        P.add("act", lambda e, cacc=cacc, csq=csq, ctmp=ctmp, h=h, tok0=tok0, ln=ln: e.activation(
                            out=vdT[:, h, tok0:tok0 + ln], in_=cacc[:, 0:ln], func=AF.Silu), [dp(CA)], [dp("vdT")])
                    else:
                        P.add("act", lambda e, cacc=cacc, csq=csq, ctmp=ctmp, ln=ln: e.activation(out=ctmp[:, 0:ln], in_=cacc[:, 0:ln], func=AF.Silu),
                              [dp(CA)], [dp(CT)])
                        P.add("act", lambda e, cacc=cacc, csq=csq, ctmp=ctmp, ln=ln: e.activation(out=csq[:, 0:ln], in_=ctmp[:, 0:ln], func=AF.Square),
                              [dp(CT), dp(CA)], [dp(CA)])
                        P.mm(ps[6][:, 0:ln], ones_bf[:], csq[:, 0:ln], True, True, [d_ones, dp(CA)], [d_ps[6]])
                        P.add("act", lambda e, cacc=cacc, csq=csq, ctmp=ctmp, ln=ln: e.activation(out=rstd[:, 0:ln], in_=ps[6][:, 0:ln], func=AF.Ln,
                                                                   bias=epsb[:, 1:2], scale=1.0), [d_ps[6], d_ones], [d_rstd])
                        P.add("act", lambda e, cacc=cacc, csq=csq, ctmp=ctmp, ln=ln: e.activation(out=rstd[:, 0:ln], in_=rstd[:, 0:ln], func=AF.Exp,
                                                                   scale=-0.5), [d_rstd], [d_rstd])
                        dst = qnT if kind == 2 else knT
                        scl = float(128 ** -0.5) if kind == 2 else 1.0
                        P.add("dve", lambda e, cacc=cacc, csq=csq, ctmp=ctmp, dst=dst, h=h, tok0=tok0, ln=ln, scl=scl: e.scalar_tensor_tensor(
                            out=dst[:, h, tok0:tok0 + ln], in0=ctmp[:, 0:ln], scalar=scl, in1=rstd[:, 0:ln],
                            op0=ALU.mult, op1=ALU.mult), [dp(CT), d_rstd], [dp("qnT" if kind == 2 else "knT")])
                        if kind == 3:
                            P.add("dve", lambda e, cacc=cacc, csq=csq, ctmp=ctmp, h=h, tok0=tok0, ln=ln: e.tensor_tensor(
                                out=knT32[:, h, tok0:tok0 + ln], in0=ctmp[:, 0:ln], in1=rstd[:, 0:ln], op=ALU.mult),
                                [dp(CT), d_rstd], [d_sq])

        P.add("dve", lambda e: e.memset(vaug[:, :, 128:130], 1.0), [], [dp("vaug")])
        nA = (128 * nh + 64 * nh + 4 * n2 + 255) // 256
        for blk in range(nA):
            s = state["gu"] % NGU
            state["gu"] += 1
            wv = gu_s[s][:].rearrange("p a b c -> p (a b c)").rearrange("p (k n) -> p k n", k=8)
            P.add("pool", lambda e, blk=blk, wv=wv: e.dma_start(out=wv, in_=wtA[blk]), [], [d_gu[s]], dma=True)
            for tl_ in range(NT):
                pb = 6 + (pj[0] % 2)
                pj[0] += 1
                for k in range(8):
                    P.mm(ps[pb][:, 0:256], src(k, tl_ * 128, (tl_ + 1) * 128), wv[:, k, :], k == 0, k == 7,
                         [d_gu[s]] + d_src, [d_ps[pb]])
                c0 = blk * 256
                for h in range(nh):
                    a = h * 128
                    if c0 <= a < c0 + 256:
                        P.add("act", lambda e, tl_=tl_, h=h, a=a, c0=c0, pb=pb: e.activation(
                            out=vaug[:, tl_ * nh + h, 0:128], in_=ps[pb][:, a - c0:a - c0 + 128], func=AF.Identity),
                            [d_ps[pb]], [dp("vaug")])
                for h in range(nh):
                    a = 128 * nh + h * 64
                    if c0 <= a < c0 + 256:
                        P.add("dve", lambda e, tl_=tl_, h=h, a=a, c0=c0, pb=pb: e.tensor_copy(
                            out=kmtok[:, tl_ * nh + h, :], in_=ps[pb][:, a - c0:a - c0 + 64]),
                            [d_ps[pb]], [dp("kmtok")])
                a = 128 * nh + 64 * nh
                if c0 <= a < c0 + 256:
                    P.add("dve", lambda e, tl_=tl_, a=a, c0=c0, pb=pb: e.tensor_copy(
                        out=gates[:, tl_].rearrange("p a b -> p (a b)"), in_=ps[pb][:, a - c0:a - c0 + 4 * n2]),
                        [d_ps[pb]], [dp("gates")])
        nB = (256 * nh) // 256
        for blk in range(nB):
            s = state["gu"] % NGU
            state["gu"] += 1
            wv = gu_s[s][:].rearrange("p a b c -> p (a b c)").rearrange("p (k n) -> p k n", k=8)
            P.add("pool", lambda e, blk=blk, wv=wv: e.dma_start(out=wv, in_=wtB[blk]), [], [d_gu[s]], dma=True)
            for ch in range(NCHK):
                pb = 6 + (pj[0] % 2)
                pj[0] += 1
                for k in range(8):
                    P.mm(ps[pb][0:64, 0:256], src(k, ch * 64, (ch + 1) * 64), wv[:, k, :], k == 0, k == 7,
                         [d_gu[s]] + d_src, [d_ps[pb]])
                for half in range(2):
                    gcol = blk * 256 + half * 128
                    isz = gcol >= nh * 128
                    h = (gcol - (nh * 128 if isz else 0)) // 128
                    dst = zd if isz else om
                    P.add("act", lambda e, ch=ch, h=h, half=half, dst=dst, pb=pb: e.activation(
                        out=dst[:, ch * nh + h, :], in_=ps[pb][0:64, half * 128:(half + 1) * 128], func=AF.Identity),
                        [d_ps[pb]], [dp("zd" if isz else "om")])

        gtmp = P.sb([128, 6, NT, n2], F32, f"gtmp{nh}")
        cin = P.sb([128, 2, NT, 2, nh], F32, f"cin{nh}")
        d_g = dp("gt")
        d_gtmp = Dep(f"gtmp{nh}")
        G4 = lambda w: gates[:, :, w, :]
        Q = lambda q: gt[:, :, q, :]
        TM = lambda i: gtmp[:, i]
        ONE = epsb[:, 2:3]
        P.add("dve", lambda e: e.tensor_tensor(out=TM(0), in0=G4(0), in1=gpt[:, 0], op=ALU.add),
              [dp("gates"), d_par], [d_gtmp])
        P.add("dve", lambda e: e.tensor_tensor(out=TM(1), in0=G4(1), in1=gpt[:, 1], op=ALU.add),
              [dp("gates"), d_par], [d_gtmp])
        P.add("act", lambda e: e.activation(out=TM(1), in_=TM(1), func=AF.Exp, scale=-1.0), [d_gtmp], [d_gtmp])
        P.add("act", lambda e: e.activation(out=TM(1), in_=TM(1), func=AF.Ln, bias=ONE, scale=1.0),
              [d_gtmp, d_ones], [d_gtmp])
        P.add("dve", lambda e: e.tensor_scalar(out=Q(1), in0=TM(1), scalar1=-1.0, scalar2=None, op0=ALU.mult),
              [d_gtmp], [d_g])
        P.add("act", lambda e: e.activation(out=TM(2), in_=G4(2), func=AF.Exp, scale=-1.0), [dp("gates")], [d_gtmp])
        P.add("act", lambda e: e.activation(out=TM(2), in_=TM(2), func=AF.Ln, bias=ONE, scale=1.0),
              [d_gtmp, d_ones], [d_gtmp])
        P.add("dve", lambda e: e.tensor_scalar(out=Q(5), in0=TM(2), scalar1=-1.0, scalar2=None, op0=ALU.mult),
              [d_gtmp], [d_g])
        P.add("dve", lambda e: e.tensor_tensor(out=TM(3), in0=G4(3), in1=gpt[:, 2], op=ALU.add),
              [dp("gates"), d_par], [d_gtmp])
        P.add("act", lambda e: e.activation(out=TM(3), in_=TM(3), func=AF.Exp), [d_gtmp], [d_gtmp])
        P.add("act", lambda e: e.activation(out=TM(3), in_=TM(3), func=AF.Ln, bias=ONE, scale=1.0),
              [d_gtmp, d_ones], [d_gtmp])
        P.add("dve", lambda e: e.tensor_tensor(out=Q(4), in0=TM(3), in1=expA[:], op=ALU.mult),
              [d_gtmp, d_expA], [d_g])
        P.add("dve", lambda e: e.tensor_scalar(out=Q(3), in0=Q(4), scalar1=-1.0, scalar2=None, op0=ALU.mult),
              [d_g], [d_g])
        for dr in range(2):
            P.add("dve", lambda e, dr=dr: e.tensor_copy(out=cin[:, dr, :, 0, :], in_=Q(1)[:, :, dr * nh:(dr + 1) * nh]),
                  [d_g, d_gtmp], [d_gtmp])
            P.add("dve", lambda e, dr=dr: e.tensor_copy(out=cin[:, dr, :, 1, :], in_=Q(3)[:, :, dr * nh:(dr + 1) * nh]),
                  [d_g, d_gtmp], [d_gtmp])
        pb = 6 + (pj[0] % 2)
        pj[0] += 1
        W_ = NT * n2
        for dr in range(2):
            P.mm(ps[pb][:, dr * W_:(dr + 1) * W_], mk[:, dr, :], cin[:, dr].rearrange("p t a b -> p (t a b)"), True, True,
                 [d_msk, d_gtmp], [d_ps[pb]])
        for dr in range(2):
            P.mm(ps[pb][:, 256 + dr * W_:256 + (dr + 1) * W_], mk[:, 2, :], cin[:, dr].rearrange("p t a b -> p (t a b)"),
                 True, True, [d_msk, d_gtmp], [d_ps[pb]])
        cums = lambda dr, w: ps[pb][:, dr * W_:(dr + 1) * W_].rearrange("p (t a b) -> p t a b", t=NT, a=2)[:, :, w, :]
        tots = lambda dr, w: ps[pb][:, 256 + dr * W_:256 + (dr + 1) * W_].rearrange("p (t a b) -> p t a b", t=NT, a=2)[:, :, w, :]
        for dr in range(2):
            hs_ = slice(dr * nh, (dr + 1) * nh)
            P.add("dve", lambda e, dr=dr, hs_=hs_: e.tensor_tensor(
                out=Q(0)[:, :, hs_], in0=TM(0)[:, :, hs_], in1=cums(dr, 0), op=ALU.subtract), [d_gtmp, d_ps[pb]], [d_g])
            P.add("dve", lambda e, dr=dr, hs_=hs_: e.tensor_tensor(
                out=Q(11)[:, :, hs_], in0=Q(0)[:, :, hs_], in1=tots(dr, 0), op=ALU.add), [d_g, d_ps[pb]], [d_g])
            P.add("dve", lambda e, dr=dr, hs_=hs_: e.tensor_copy(
                out=Q(12)[:, :, hs_], in_=tots(dr, 0)), [d_ps[pb]], [d_g])
            P.add("dve", lambda e, dr=dr, hs_=hs_: e.tensor_scalar(
                out=Q(6)[:, :, hs_], in0=cums(dr, 1), scalar1=-1.0, scalar2=None, op0=ALU.mult), [d_ps[pb]], [d_g])
            P.add("dve", lambda e, dr=dr, hs_=hs_: e.tensor_tensor(
                out=Q(7)[:, :, hs_], in0=Q(5)[:, :, hs_], in1=cums(dr, 1), op=ALU.add), [d_g, d_ps[pb]], [d_g])
            P.add("dve", lambda e, dr=dr, hs_=hs_: e.tensor_tensor(
                out=Q(10)[:, :, hs_], in0=Q(6)[:, :, hs_], in1=tots(dr, 1), op=ALU.add), [d_g, d_ps[pb]], [d_g])
        P.add("act", lambda e: e.activation(out=Q(2), in_=Q(11), func=AF.Exp), [d_g], [d_g])
        P.add("act", lambda e: e.activation(out=Q(8), in_=Q(5), func=AF.Exp), [d_g], [d_g])
        P.add("act", lambda e: e.activation(out=Q(9), in_=Q(7), func=AF.Exp), [d_g], [d_g])
        P.add("act", lambda e: e.activation(out=Q(10), in_=Q(10), func=AF.Exp), [d_g], [d_g])

        def col(tl_, q, j):
            return gt[:, tl_, q, j:j + 1]
        NEGns = {0: 3, 1: 5}
        NEGst = {0: 4, 1: 6}
        NEGts = {0: 6, 1: 4}
        bcnt = [0]

        def mlstm_unit(Q, dr, h, tl_):
            sn = dr
            T = TS[sn]
            rhs1, rhs2, rhs3, EB, DT, EG, FA, FB, FC = (T[k_] for k_ in ("rhs1", "rhs2", "rhs3", "EB", "DT", "EG", "FA", "FB", "FC"))
            swT, qbT, QKdT, wT, qgT, vnew = (T[k_] for k_ in ("swT", "qbT", "QKdT", "wT", "qgT", "vnew"))
            kw, wsc, Pm, PTm, X, kd, kdcc, dn = (T[k_] for k_ in ("kw", "wsc", "Pm", "PTm", "X", "kd", "kdcc", "dn"))
            j = dr * nh + h
            tsl = slice(tl_ * 128, (tl_ + 1) * 128)
            ba = 0
            A = ps[ba][:, dr * 256:(dr + 1) * 256]
            bb = 1 + dr
            Bk = ps[bb]
            Q.add("dve", lambda e: e.tensor_scalar(out=rhs1, in0=mk[:, dr, :], scalar1=col(tl_, 1, j), scalar2=None,
                                                   op0=ALU.mult), [d_msk, d_g], [dp("rhs1_%d" % sn)])
            Q.mm(A[:, 0:128], ones_f[:], rhs1, True, True, [d_ones, dp("rhs1_%d" % sn)], [d_ps[ba]])
            Q.mm(A[:, 128:256], ones_f[:], rhs1, True, False, [d_ones, dp("rhs1_%d" % sn)], [d_ps[ba]])
            Q.mm(A[:, 128:256], ident_f[:], mk[:, NEGns[dr], :], False, True, [d_id, d_msk], [d_ps[ba]])
            Q.mm(Bk[:, 0:128], kmT[:, h, tsl], qmT[:, h, tsl], True, True, [dp("kmT"), dp("qmT")], [d_ps[bb]])
            Q.add("act", lambda e: e.activation(out=EB, in_=A[:, 0:128], func=AF.Exp), [d_ps[ba]], [dp("EB_%d" % sn)])
            Q.add("act", lambda e: e.activation(out=DT, in_=A[:, 128:256], func=AF.Exp, bias=col(tl_, 0, j), scale=1.0),
                  [d_ps[ba], d_g], [dp("DT_%d" % sn)])
            Q.add("dve", lambda e: e.tensor_tensor(out=swT, in0=Bk[:, 0:128], in1=DT, op=ALU.mult),
                  [d_ps[bb], dp("DT_%d" % sn)], [dp("swT_%d" % sn)])
            Q.add("dve", lambda e: e.tensor_tensor(out=qbT, in0=qmT[:, h, tsl], in1=EB, op=ALU.mult),
                  [dp("qmT"), dp("EB_%d" % sn)], [dp("qbT_%d" % sn)])
            Q.add("dve", lambda e: e.tensor_scalar(out=wsc, in0=chind[:], scalar1=col(tl_, 2, j), scalar2=None,
                                                   op0=ALU.mult), [d_msk, d_g], [dp("wsc_%d" % sn)])
            for c in range(2):
                Q.add("dve", lambda e, c=c: e.tensor_scalar(out=kw[:, c, :], in0=kmtok[:, tl_ * nh + h, :],
                                                            scalar1=wsc[:, c:c + 1], scalar2=None, op0=ALU.mult),
                      [dp("kmtok"), dp("wsc_%d" % sn)], [dp("kw_%d" % sn)])

        def mlstm_chunk(Q, dr, h, tl_, c, first):
            sn = dr
            T = TS[sn]
            rhs1, rhs2, rhs3, EB, DT, EG, FA, FB, FC = (T[k_] for k_ in ("rhs1", "rhs2", "rhs3", "EB", "DT", "EG", "FA", "FB", "FC"))
            swT, qbT, QKdT, wT, qgT, vnew = (T[k_] for k_ in ("swT", "qbT", "QKdT", "wT", "qgT", "vnew"))
            kw, wsc, Pm, PTm, X, kd, kdcc, dn = (T[k_] for k_ in ("kw", "wsc", "Pm", "PTm", "X", "kd", "kdcc", "dn"))
            tc = slice(c * 64, (c + 1) * 64)
            tl1 = c * 64 + 63 if dr == 0 else c * 64
            chg = tl_ * 2 + c
            bb = 1 + dr
            B = ps[bb]
            Q.mm(B[0:64, 128:257], swT[:, tc], vaug[:, tl_ * nh + h, 0:129], True, False, [dp("swT_%d" % sn), dp("vaug")], [d_ps[bb]])
            Q.mm(B[0:64, 128:257], qbT[:, tc], Cbf[:, dr, h, 0:129], False, True, [dp("qbT_%d" % sn), dp("Cbf%d" % dr)], [d_ps[bb]])
            Q.mm(B[0:64, 320:449], kw[:, c, :], vaug[:, tl_ * nh + h, 0:129], True, True, [dp("kw_%d" % sn), dp("vaug")], [d_ps[bb]])
            Q.add("dve", lambda e: e.tensor_scalar(out=dn[0:64, 1:2], in0=B[0:64, 256:257], scalar1=-1.0, scalar2=1.0,
                                                   op0=ALU.mult, op1=ALU.max), [d_ps[bb]], [dp("dn_%d" % sn)])
            Q.add("dve", lambda e: e.scalar_tensor_tensor(out=dn[0:64, 0:1], in0=B[0:64, 256:257], scalar=1.0,
                                                          in1=dn[0:64, 1:2], op0=ALU.max, op1=ALU.max),
                  [d_ps[bb], dp("dn_%d" % sn)], [dp("dn_%d" % sn)])
            Q.add("dve", lambda e: e.reciprocal(out=dn[0:64, 1:2], in_=dn[0:64, 0:1]), [dp("dn_%d" % sn)], [dp("dn_%d" % sn)])
            if first:
                Q.add("dve", lambda e: e.tensor_scalar(out=hacc[:, chg * nh + h, :], in0=B[0:64, 128:256],
                                                       scalar1=dn[0:64, 1:2], scalar2=None, op0=ALU.mult),
                      [d_ps[bb], dp("dn_%d" % sn)], [dp("hacc")])
            else:
                Q.add("dve", lambda e: e.scalar_tensor_tensor(out=hacc[:, chg * nh + h, :], in0=B[0:64, 128:256],
                                                              scalar=dn[0:64, 1:2], in1=hacc[:, chg * nh + h, :],
                                                              op0=ALU.mult, op1=ALU.add),
                      [d_ps[bb], dp("dn_%d" % sn), dp("hacc")], [dp("hacc")])
            Q.add("dve", lambda e: e.scalar_tensor_tensor(out=Caug[0:64, dr, h, 0:129], in0=Caug[0:64, dr, h, 0:129],
                                                          scalar=EB[0:64, tl1:tl1 + 1], in1=B[0:64, 320:449],
                                                          op0=ALU.mult, op1=ALU.add),
                  [dp("Caug%d" % dr), dp("EB_%d" % sn), d_ps[bb]], [dp("Caug%d" % dr)])
            Q.add("act", lambda e: e.activation(out=Cbf[0:64, dr, h, 0:129], in_=Caug[0:64, dr, h, 0:129],
                                                func=AF.Identity), [dp("Caug%d" % dr)], [dp("Cbf%d" % dr)])

        def delta_unit(Q, dr, h, tl_):
            sn = dr
            T = TS[sn]
            rhs1, rhs2, rhs3, EB, DT, EG, FA, FB, FC = (T[k_] for k_ in ("rhsD", "rhs2", "rhs3", "EB", "DT", "EG", "FA", "FB", "FC"))
            swT, qbT, QKdT, wT, qgT, vnew = (T[k_] for k_ in ("swT", "qbT", "QKdT", "wT", "qgT", "vnew"))
            kw, wsc, Pm, PTm, X, kd, kdcc, dn = (T[k_] for k_ in ("kw", "wsc", "Pm", "PTm", "X", "kd", "kdcc", "dn"))
            j = dr * nh + h
            tsl = slice(tl_ * 128, (tl_ + 1) * 128)
            ba = 3 if dr == 0 else 5
            A = ps[ba]
            Q.add("dve", lambda e: e.tensor_scalar(out=rhs1, in0=mk[:, dr, :], scalar1=col(tl_, 3, j), scalar2=None,
                                                   op0=ALU.mult), [d_msk, d_g], [dp("rhsD_%d" % sn)])
            Q.add("dve", lambda e: e.scalar_tensor_tensor(out=rhs2, in0=ident_f[:], scalar=col(tl_, 5, j), in1=rhs1,
                                                          op0=ALU.mult, op1=ALU.add), [d_id, d_g, dp("rhsD_%d" % sn)], [dp("rhs2_%d" % sn)])
            Q.add("dve", lambda e: e.tensor_scalar(out=rhs3, in0=mk[:, dr, :], scalar1=col(tl_, 4, j), scalar2=None,
                                                   op0=ALU.mult), [d_msk, d_g], [dp("rhs3_%d" % sn)])
            Q.mm(A[:, 0:128], ones_f[:], rhs1, True, True, [d_ones, dp("rhsD_%d" % sn)], [d_ps[ba]])
            Q.mm(A[:, 128:256], ones_f[:], rhs2, True, False, [d_ones, dp("rhs2_%d" % sn)], [d_ps[ba]])
            Q.mm(A[:, 128:256], ident_f[:], mk[:, NEGst[dr], :], False, True, [d_id, d_msk], [d_ps[ba]])
            Q.mm(A[:, 256:384], ones_f[:], rhs3, True, False, [d_ones, dp("rhs3_%d" % sn)], [d_ps[ba]])
            Q.mm(A[:, 256:384], ident_f[:], mk[:, NEGts[dr], :], False, True, [d_id, d_msk], [d_ps[ba]])
            Q.mm(A[:, 384:512], ones_f[:], rhs1, True, False, [d_ones, dp("rhsD_%d" % sn)], [d_ps[ba]])
            Q.mm(A[:, 384:512], ident_f[:], mk[:, NEGns[dr], :], False, True, [d_id, d_msk], [d_ps[ba]])
            Q.add("act", lambda e: e.activation(out=EG, in_=A[:, 0:128], func=AF.Exp), [d_ps[ba]], [dp("EG_%d" % sn)])
            Q.add("act", lambda e: e.activation(out=FA, in_=A[:, 128:256], func=AF.Exp, bias=col(tl_, 6, j), scale=1.0),
                  [d_ps[ba], d_g], [dp("FA_%d" % sn)])
            Q.add("act", lambda e: e.activation(out=FB, in_=A[:, 256:384], func=AF.Exp, bias=col(tl_, 7, j), scale=1.0),
                  [d_ps[ba], d_g], [dp("FB_%d" % sn)])
            Q.add("act", lambda e: e.activation(out=FC, in_=A[:, 384:512], func=AF.Exp, bias=col(tl_, 6, j), scale=1.0),
                  [d_ps[ba], d_g], [dp("FC_%d" % sn)])
            cb_ = 4 if dr == 0 else 6
            C_ = ps[cb_]
            Q.mm(C_[:, 0:128], knT32[:, h, tsl], knT32[:, h, tsl], True, True, [d_sq], [d_ps[cb_]])
            Q.mm(C_[:, 128:256], knT[:, h, tsl], qnT[:, h, tsl], True, True, [dp("knT"), dp("qnT")], [d_ps[cb_]])
            Q.mm(C_[:, 256:384], knT32[:, h, tsl], ident_f[:], True, True, [d_sq, d_id], [d_ps[cb_]])
            Q.mm(C_[:, 384:512], vdT[:, h, tsl], ident_b[:], True, True, [dp("vdT"), d_ones], [d_ps[cb_]])
            Q.add("dve", lambda e: e.tensor_tensor(out=PTm[0], in0=C_[:, 0:128], in1=FA, op=ALU.mult),
                  [d_ps[cb_], dp("FA_%d" % sn)], [dp("PT0_%d" % sn)])
            Q.add("dve", lambda e: e.tensor_tensor(out=Pm[0], in0=C_[:, 0:128], in1=FB, op=ALU.mult),
                  [d_ps[cb_], dp("FB_%d" % sn)], [dp("P0_%d" % sn)])
            Q.add("dve", lambda e: e.tensor_tensor(out=QKdT, in0=C_[:, 128:256], in1=FC, op=ALU.mult),
                  [d_ps[cb_], dp("FC_%d" % sn)], [dp("QKdT_%d" % sn)])
            Q.add("dve", lambda e: e.tensor_scalar(out=X[:, 128:256], in0=C_[:, 256:384], scalar1=col(tl_, 9, j),
                                                   scalar2=None, op0=ALU.mult), [d_ps[cb_], d_g], [dp("X_%d" % sn)])
            Q.add("dve", lambda e: e.tensor_scalar(out=X[:, 0:128], in0=C_[:, 384:512], scalar1=col(tl_, 8, j),
                                                   scalar2=None, op0=ALU.mult), [d_ps[cb_], d_g], [dp("X_%d" % sn)])
            Q.add("dve", lambda e: e.tensor_scalar(out=kdcc, in0=chind[:], scalar1=col(tl_, 10, j), scalar2=None,
                                                   op0=ALU.mult), [d_msk, d_g], [dp("kdcc_%d" % sn)])
            for c in range(2):
                Q.add("dve", lambda e, c=c: e.tensor_scalar(out=kd[:, c, :], in0=C_[:, 256:384],
                                                            scalar1=kdcc[:, c:c + 1], scalar2=None, op0=ALU.mult),
                      [d_ps[cb_], dp("kdcc_%d" % sn)], [dp("kd_%d" % sn)])
            Q.add("dve", lambda e: e.tensor_tensor(out=qgT, in0=qnT[:, h, tsl], in1=EG, op=ALU.mult),
                  [dp("qnT"), dp("EG_%d" % sn)], [dp("qgT_%d" % sn)])
            Y = ps[ba]
            cur = 0
            for lev in range(6):
                Q.mm(Y[:, 0:256], PTm[cur], X, True, True, [dp(f"PT{cur}_{sn}"), dp("X_%d" % sn)], [d_ps[ba]])
                Q.add("dve", lambda e, lev=lev: e.tensor_tensor(out=X, in0=X, in1=Y[:, 0:256],
                                                                op=ALU.subtract if lev == 0 else ALU.add),
                      [dp("X_%d" % sn), d_ps[ba]], [dp("X_%d" % sn)])
                if lev < 5:
                    nx = 1 - cur
                    Q.mm(Y[:, 256:384], PTm[cur], Pm[cur], True, True, [dp(f"PT{cur}_{sn}"), dp(f"P{cur}_{sn}")], [d_ps[ba]])
                    Q.mm(Y[:, 384:512], Pm[cur], PTm[cur], True, True, [dp(f"PT{cur}_{sn}"), dp(f"P{cur}_{sn}")], [d_ps[ba]])
                    Q.add("act", lambda e, nx=nx: e.activation(out=Pm[nx], in_=Y[:, 256:384], func=AF.Identity),
                          [d_ps[ba]], [dp(f"P{nx}_{sn}")])
                    Q.add("dve", lambda e, nx=nx: e.tensor_copy(out=PTm[nx], in_=Y[:, 384:512]),
                          [d_ps[ba]], [dp(f"PT{nx}_{sn}")])
                    cur = nx
            Q.mm(Y[:, 0:128], X[:, 128:256], ident_f[:], True, True, [dp("X_%d" % sn), d_id], [d_ps[ba]])
            Q.add("act", lambda e: e.activation(out=wT, in_=Y[:, 0:128], func=AF.Identity), [d_ps[ba]], [dp("wT_%d" % sn)])

        def delta_chunk(Q, dr, h, tl_, c, first):
            sn = dr
            T = TS[sn]
            rhs1, rhs2, rhs3, EB, DT, EG, FA, FB, FC = (T[k_] for k_ in ("rhs1", "rhs2", "rhs3", "EB", "DT", "EG", "FA", "FB", "FC"))
            swT, qbT, QKdT, wT, qgT, vnew = (T[k_] for k_ in ("swT", "qbT", "QKdT", "wT", "qgT", "vnew"))
            kw, wsc, Pm, PTm, X, kd, kdcc, dn = (T[k_] for k_ in ("kw", "wsc", "Pm", "PTm", "X", "kd", "kdcc", "dn"))
            tc = slice(c * 64, (c + 1) * 64)
            tl1 = c * 64 + 63 if dr == 0 else c * 64
            chg = tl_ * 2 + c
            eb_ = 4 if dr == 0 else 6
            E_ = ps[eb_]
            Q.mm(E_[:, 0:128], wT, Sbf[:, dr, h, :], True, True, [dp("wT_%d" % sn), dp("Sbf%d" % dr)], [d_ps[eb_]])
            Q.add("dve", lambda e: e.tensor_tensor(out=vnew, in0=X[:, 0:128], in1=E_[:, 0:128], op=ALU.subtract),
                  [dp("X_%d" % sn), d_ps[eb_]], [dp("vnew_%d" % sn)])
            Q.mm(E_[0:64, 128:256], qgT[:, tc], Sbf[:, dr, h, :], True, False, [dp("qgT_%d" % sn), dp("Sbf%d" % dr)], [d_ps[eb_]])
            Q.mm(E_[0:64, 128:256], QKdT[:, tc], vnew, False, True, [dp("QKdT_%d" % sn), dp("vnew_%d" % sn)], [d_ps[eb_]])
            Q.mm(E_[:, 256:384], kd[:, c, :], vnew, True, True, [dp("kd_%d" % sn), dp("vnew_%d" % sn)], [d_ps[eb_]])
            if first:
                Q.add("act", lambda e: e.activation(out=oacc[:, chg * nh + h, :], in_=E_[0:64, 128:256], func=AF.Identity),
                      [d_ps[eb_]], [dp("oacc")])
            else:
                Q.add("dve", lambda e: e.tensor_tensor(out=oacc[:, chg * nh + h, :], in0=oacc[:, chg * nh + h, :],
                                                       in1=E_[0:64, 128:256], op=ALU.add),
                      [d_ps[eb_], dp("oacc")], [dp("oacc")])
            Q.add("dve", lambda e: e.scalar_tensor_tensor(out=Sst[:, dr, h, :], in0=Sst[:, dr, h, :],
                                                          scalar=EG[:, tl1:tl1 + 1], in1=E_[:, 256:384],
                                                          op0=ALU.mult, op1=ALU.add),
                  [dp("Sst%d" % dr), dp("EG_%d" % sn), d_ps[eb_]], [dp("Sst%d" % dr)])
            Q.add("act", lambda e: e.activation(out=Sbf[:, dr, h, :], in_=Sst[:, dr, h, :], func=AF.Identity),
                  [dp("Sst%d" % dr)], [dp("Sbf%d" % dr)])

        class Rec:
            def __init__(self):
                self.ops = []

            def add(self, *a_, **k_):
                self.ops.append((0, a_, k_))

            def mm(self, *a_, **k_):
                self.ops.append((1, a_, k_))

        for si, (t0, nt) in enumerate(seqs):
            cfg["init_state"](si, Caug, Cbf, Sst, Sbf, dp)
            chains = []
            for dr in range(2):
                qm_, qd_ = Rec(), Rec()
                for step in range(nt):
                    tl_ = t0 + step if dr == 0 else t0 + nt - 1 - step
                    other_step = nt - 1 - step
                    first = step < other_step
                    for h in range(nh):
                        mlstm_unit(qm_, dr, h, tl_)
                        for c in ((0, 1) if dr == 0 else (1, 0)):
                            mlstm_chunk(qm_, dr, h, tl_, c, first)
                        delta_unit(qd_, dr, h, tl_)
                        for c in ((0, 1) if dr == 0 else (1, 0)):
                            delta_chunk(qd_, dr, h, tl_, c, first)
                chains += [qm_.ops, qd_.ops]
            pos = [0] * len(chains)
            live = True
            while live:
                live = False
                for ci_, ch_ in enumerate(chains):
                    while pos[ci_] < len(ch_):
                        kind_, a_, k_ = ch_[pos[ci_]]
                        pos[ci_] += 1
                        (P.mm if kind_ else P.add)(*a_, **k_)
                        live = True
                        if not (kind_ == 1 and a_[4] is False):
                            break
            cfg["final_state"](si, t0, nt, Caug, Sst, gt, dp, d_g)

        mixg = cfg["mix_group"]
        FB_ = []
        NFB = 3
        for i_ in range(NFB):
            o6 = 5.25 * i_
            FB_.append(dict(f1=AVH(o6, [64, 4, 128], F32, 64), f2=AVH(o6 + 2, [64, 4, 128], F32, 64),
                            f3=AVH(o6 + 4, [64, 4, 128], BF16, 64), fc=AVH(o6 + 5, [64, 3, 4], F32, 64)))
        NI = NCHK * nh
        gi = 0
        for kind in range(2):
            acc = hacc if kind == 0 else oacc
            d_acc = dp("hacc") if kind == 0 else dp("oacc")
            for g in range(NI // 4):
                bf_ = FB_[gi % NFB]
                sfx = "_%d_1" % (gi % NFB)
                f1, f2, f3, fc = bf_["f1"], bf_["f2"], bf_["f3"], bf_["fc"]
                d1, d2, d3, dc = dp("f1" + sfx), dp("f2" + sfx), dp("f3" + sfx), dp("fc" + sfx)
                pbk = 5 + (gi % NFB)
                gi += 1
                i0 = 4 * g
                P.add("act", lambda e, acc=acc, i0=i0, f1=f1: e.activation(out=f1, in_=acc[:, i0:i0 + 4, :], func=AF.Square),
                      [d_acc], [d1])
                P.add("dve", lambda e, f1=f1, fc=fc: e.reduce_sum(out=fc[:, 0, :], in_=f1, axis=AX.X), [d1], [dc])
                P.add("act", lambda e, fc=fc: e.activation(out=fc[:, 1, :], in_=fc[:, 0, :], func=AF.Ln,
                                                           bias=epsb[0:64, 3:4], scale=1.0 / 128.0), [dc, d_ones], [dc])
                P.add("act", lambda e, fc=fc: e.activation(out=fc[:, 2, :], in_=fc[:, 1, :], func=AF.Exp, scale=-0.5),
                      [dc], [dc])
                if kind == 0:
                    P.add("act", lambda e, i0=i0, f2=f2: e.activation(out=f2, in_=om[:, i0:i0 + 4, :], func=AF.Sigmoid),
                          [dp("om")], [d2])
                else:
                    P.add("act", lambda e, i0=i0, f2=f2: e.activation(out=f2, in_=zd[:, i0:i0 + 4, :], func=AF.Silu),
                          [dp("zd")], [d2])
                for k in range(4):
                    h = (i0 + k) % nh
                    gsl = ngm[:, h * 128:(h + 1) * 128] if kind == 0 else ngd[:, :]
                    P.add("dve", lambda e, acc=acc, i0=i0, k=k, gsl=gsl, f1=f1, fc=fc: e.scalar_tensor_tensor(
                        out=f1[:, k, :], in0=acc[:, i0 + k, :], scalar=fc[:, 2, k:k + 1], in1=gsl,
                        op0=ALU.mult, op1=ALU.mult), [d_acc, dc, d_ngm, d_ngd, d1], [d1])
                P.add("dve", lambda e, f1=f1, f2=f2, f3=f3: e.tensor_tensor(out=f3, in0=f1, in1=f2, op=ALU.mult),
                      [d1, d2], [d3])
                for k in range(4):
                    P.mm(ps[pbk][:, k * 64:(k + 1) * 64], f3[:, k, :], ident_b[0:64, 0:64], True, True, [d3, d_ones],
                         [d_ps[pbk]])
                P.add("act", lambda e, kind=kind, g=g, pbk=pbk: e.activation(
                    out=mixg(kind, g), in_=ps[pbk][:, 0:256].rearrange("p (k t) -> p k t", k=4), func=AF.Identity),
                    [d_ps[pbk]], d_mix)
        return D_

    def ab_layer(l):
        for b in range(2):
            modulate(l, 1, b)
        d_ag1i, d_ag1o = Dep("ab_ag1i"), Dep("ab_ag1o")
        allgather(hT[:, :, BLK:2 * BLK], d_h[1], ag1_in.rearrange("p (k t) -> p k t", k=8), ag1_in, ag1_out,
                  d_ag1i, d_ag1o)

        def users_of(deplist):
            rs = []
            for o in deplist:
                rs += list(o.rs)
                if o.w is not None:
                    rs.append(o.w)
            return rs
        ffn_users = users_of(d_a)
        mixP = AV(61, [128, 8, BLK], BF16)
        d_mixP = Dep("ab_mixP")
        d_mixP.rs = list(ffn_users)
        cst = AV(69, [64, 2, 4, 130], F32, parts=64)
        mfin = P.sb([4, 2, 2], F32, "ab_mfin")
        mrun = P.sb([4, 4], F32, "ab_mrun")
        gmx = P.sb([4, 2, 2], F32, "ab_gmx")
        btt = P.sb([4, 2, 2], F32, "ab_btt")
        dg4 = P.sb([4, 4], F32, "ab_dg4")
        esc = P.sb([64, 4], F32, "ab_esc")
        d_fs = Dep("ab_fs")
        d_cst = Dep("ab_cst")
        d_mfin = Dep("ab_mfin")

        def p_init(si, Caug, Cbf, Sst, Sbf, dp):
            P.add("dve", lambda e: e.memset(Caug.rearrange("p a b c -> p (a b c)"), 0.0), [], [dp("Caug0"), dp("Caug1")])
            P.add("dve", lambda e: e.memset(Cbf.rearrange("p a b c -> p (a b c)"), 0.0), [], [dp("Cbf0"), dp("Cbf1")])
            P.add("dve", lambda e: e.memset(Sst.rearrange("p a b c -> p (a b c)"), 0.0), [], [dp("Sst0"), dp("Sst1")])
            P.add("dve", lambda e: e.memset(Sbf.rearrange("p a b c -> p (a b c)"), 0.0), [], [dp("Sbf0"), dp("Sbf1")])

        def p_final(si, t0, nt, Caug, Sst, gt, dp, d_g):
            nh = 4
            F_ = ps[7]
            for dr in range(2):
                for ti in range(nt):
                    tl_ = t0 + ti
                    P.mm(F_[0:nh, 0:128], gt[:, tl_, 11, dr * nh:(dr + 1) * nh], ident_f[:], True, True,
                         [d_g, d_id], [d_ps[7]])
                    P.mm(F_[0:nh, 128:256], gt[:, tl_, 12, dr * nh:(dr + 1) * nh], ident_f[:], True, True,
                         [d_g, d_id], [d_ps[7]])
                    P.add("dve", lambda e, ti=ti: e.tensor_reduce(
                        out=gmx[:, ti, :], in_=F_[0:nh, 0:128].rearrange("p (c s) -> p c s", c=2), axis=AX.X, op=ALU.max),
                        [d_ps[7]], [d_fs])
                    P.add("dve", lambda e, ti=ti: e.tensor_copy(
                        out=btt[:, ti, :], in_=F_[0:nh, 128:256].rearrange("p (c s) -> p c s", c=2)[:, :, 0]),
                        [d_ps[7]], [d_fs])
                P.add("dve", lambda e: e.memset(mrun[:, 0:1], 0.0), [d_fs], [d_fs])
                order = [(ti, c) for ti in range(nt) for c in range(2)]
                if dr == 1:
                    order = order[::-1]
                for (ti, c) in order:
                    P.add("dve", lambda e, ti=ti, c=c: e.tensor_tensor(out=mrun[:, 1:2], in0=mrun[:, 0:1],
                                                                       in1=btt[:, ti, c:c + 1], op=ALU.add), [d_fs], [d_fs])
                    P.add("dve", lambda e, ti=ti, c=c: e.tensor_tensor(out=mrun[:, 0:1], in0=mrun[:, 1:2],
                                                                       in1=gmx[:, ti, c:c + 1], op=ALU.max), [d_fs], [d_fs])
                P.add("dve", lambda e, dr=dr: e.tensor_copy(out=mfin[:, si, dr:dr + 1], in_=mrun[:, 0:1]), [d_fs], [d_mfin])
                P.add("dve", lambda e: e.tensor_scalar(out=dg4[:], in0=ident_f[0:4, 0:4], scalar1=mrun[:, 0:1],
                                                       scalar2=None, op0=ALU.mult), [d_fs, d_id], [d_fs])
                P.mm(F_[0:64, 256:260], ones_f[0:4, 0:64], dg4[:], True, True, [d_ones, d_fs], [d_ps[7]])
                P.add("act", lambda e: e.activation(out=esc[:], in_=F_[0:64, 256:260], func=AF.Exp, scale=-1.0),
                      [d_ps[7]], [d_fs])
                for h in range(nh):
                    P.add("dve", lambda e, dr=dr, h=h: e.tensor_scalar(
                        out=cst[:, dr, h, :], in0=Caug[0:64, dr, h, :], scalar1=esc[:, h:h + 1], scalar2=None,
                        op0=ALU.mult), [dp("Caug0"), dp("Caug1"), d_fs], [d_cst])
            P.add("sp", lambda e: e.dma_start(out=c_out[si], in_=cst.rearrange("p a b c -> p (a b c)")),
                  [d_cst], [d_y], dma=True)
            P.add("sp", lambda e: e.dma_start(out=s_out[si], in_=Sst.rearrange("p a b c -> p (a b c)")),
                  [dp("Sst0"), dp("Sst1")], [d_y], dma=True)

        cfgp = dict(nh=4, T=BLK, seqs=[(0, 2), (2, 2)], src=lambda k, a, b: hT[:, k, a:b], d_src=d_h[0],
                    wf=abp_wf, wtA=abp_wtA, wtB=abp_wtB, gp=abp_gp, cw=abp_cw, ngm=abp_ngm, ngd=ab_ngd,
                    mix_group=lambda kind, g: mixP[:, kind * 4:(kind + 1) * 4, g * 64:(g + 1) * 64], d_mix=[d_mixP],
                    prev_users=ffn_users, extra_users=lambda: [], init_state=p_init, final_state=p_final)
        Dp = ab_stage(cfgp)
        P.add("sp", lambda e: e.dma_start(out=m_out, in_=mfin[:].rearrange("p a b -> p (a b)")), [d_mfin], [d_y], dma=True)
        out_proj(l, ab_wo, mixP, [d_mixP], 0)
        prompt_users = users_of(list(Dp.values()) + [d_mixP]) + ffn_users
        hfull = AV(72, [128, 8, 2048], BF16)
        d_hf = Dep("ab_hfull")
        d_hf.rs = list(prompt_users)
        for m in range(4):
            P.add("sp", lambda e, m=m: e.dma_start(
                out=hfull[:, :, m * BLK:(m + 1) * BLK],
                in_=ag1_out[m * 128:(m + 1) * 128, :].rearrange("p (k t) -> p k t", k=8)),
                [d_ag1o], [d_hf], dma=True)
        headout = AV(61, [128, 2, 2048], BF16)
        d_ho = Dep("ab_headout")
        d_ho.rs = list(prompt_users)
        m0t = P.sb([128, 2], F32, "ab_m0t")
        d_c0 = Dep("ab_c0t")

        def s_init(si, Caug, Cbf, Sst, Sbf, dp):
            P.add("sp", lambda e: e.dma_start(out=Caug.rearrange("p a b c -> p (a b c)"),
                                              in_=ab_c0.rearrange("p a c -> p (a c)")), [], [dp("Caug0"), dp("Caug1")], dma=True)
            P.add("sp", lambda e: e.dma_start(out=m0t[:], in_=ab_m0), [], [d_c0], dma=True)
            P.add("act", lambda e: e.activation(out=m0t[:], in_=m0t[:], func=AF.Exp), [d_c0], [d_c0])
            for dr in range(2):
                P.add("dve", lambda e, dr=dr: e.tensor_scalar(out=Caug[:, dr, 0, :], in0=Caug[:, dr, 0, :],
                                                              scalar1=m0t[:, dr:dr + 1], scalar2=None, op0=ALU.mult),
                      [dp("Caug0"), dp("Caug1"), d_c0], [dp("Caug0"), dp("Caug1")])
            P.add("act", lambda e: e.activation(out=Cbf.rearrange("p a b c -> p (a b c)"),
                                                in_=Caug.rearrange("p a b c -> p (a b c)"), func=AF.Identity),
                  [dp("Caug0"), dp("Caug1")], [dp("Cbf0"), dp("Cbf1")])
            P.add("sp", lambda e: e.dma_start(out=Sst.rearrange("p a b c -> p (a b c)"),
                                              in_=ab_s0.rearrange("p a c -> p (a c)")), [], [dp("Sst0"), dp("Sst1")], dma=True)
            P.add("act", lambda e: e.activation(out=Sbf.rearrange("p a b c -> p (a b c)"),
                                                in_=Sst.rearrange("p a b c -> p (a b c)"), func=AF.Identity),
                  [dp("Sst0"), dp("Sst1")], [dp("Sbf0"), dp("Sbf1")])

        cfgs = dict(nh=1, T=2048, seqs=[(0, 16)], src=lambda k, a, b: hfull[:, k, a:b], d_src=[d_hf],
                    wf=abs_wf, wtA=abs_wtA, wtB=abs_wtB, gp=abs_gp, cw=abs_cw, ngm=abs_ngm, ngd=ab_ngd,
                    mix_group=lambda kind, g: headout[:, kind, g * 256:(g + 1) * 256].rearrange("p (k t) -> p k t", k=4),
                    d_mix=[d_ho],
                    prev_users=prompt_users, extra_users=lambda: users_of([d_hf]),
                    init_state=s_init, final_state=lambda *a: None)
        Ds = ab_stage(cfgs)
        d_ag2i, d_ag2o = Dep("ab_ag2i"), Dep("ab_ag2o")
        allgather(headout, [d_ho], ag2_in.rearrange("p (c t) -> p c t", c=2), ag2_in, ag2_out, d_ag2i, d_ag2o)
        cand = AV(0, [128, 4, 8, BLK], BF16)
        mixS = AV(32, [128, 8, BLK], BF16)
        d_cand, d_mixs = Dep("ab_cand"), Dep("ab_mixS")
        sample_users = users_of(list(Ds.values()) + [d_hf, d_ho])
        d_cand.rs = list(sample_users)
        d_mixs.rs = list(sample_users)
        for m in range(4):
            for kind in range(2):
                P.add("sp", lambda e, m=m, kind=kind: e.dma_start(
                    out=cand[:, :, kind * 4 + m, :],
                    in_=ag2_out[m * 128:(m + 1) * 128, kind * 2048:(kind + 1) * 2048].rearrange("p (r t) -> p r t", r=4)),
                    [d_ag2o], [d_cand], dma=True)
        for rr in range(4):
            if rr == 0:
                P.add("dve", lambda e: e.tensor_scalar(
                    out=mixS.rearrange("p a b -> p (a b)"), in0=cand[:, 0].rearrange("p a b -> p (a b)"),
                    scalar1=oh[:, 0:1], scalar2=None, op0=ALU.mult), [d_cand, d_oh], [d_mixs])
            else:
                P.add("dve", lambda e, rr=rr: e.scalar_tensor_tensor(
                    out=mixS.rearrange("p a b -> p (a b)"), in0=cand[:, rr].rearrange("p a b -> p (a b)"),
                    scalar=oh[:, rr:rr + 1], in1=mixS.rearrange("p a b -> p (a b)"), op0=ALU.mult, op1=ALU.add),
                    [d_cand, d_oh, d_mixs], [d_mixs])
        out_proj(l, ab_wo, mixS, [d_mixs], 1)
        tail = users_of([d_cand, d_mixs]) + sample_users
        for b in range(2):
            d_a[b].rs = list(d_a[b].rs) + tail
        hT_tmp_users = users_of([v_ for k_, v_ in list(Dp.items()) + list(Ds.items()) if k_.endswith("_1")])
        for b in range(2):
            for j in range(8):
                d_h[b][j].rs = list(d_h[b][j].rs) + hT_tmp_users


    def final_norm():
        for b in range(2):
            blk = slice(b * BLK, (b + 1) * BLK)
            P.add("act", lambda e, blk=blk: e.activation(out=sq[:], in_=xT[:, :, blk], func=AF.Square),
                  d_x[b], [d_sq])
            for k in range(8):
                P.mm(ps[6], ones_bf[:], sq[:, k, :], k == 0, k == 7, [d_ones, d_sq], [d_ps[6]])
            rstd_from_ps6()
            for j in range(8):
                P.add("dve", lambda e, j=j, blk=blk: e.scalar_tensor_tensor(
                    out=xT[:, j, blk], in0=xT[:, j, blk], scalar=ng[:, DEPTH * 3, j:j + 1], in1=rstd[:],
                    op0=ALU.mult, op1=ALU.mult), [d_x[b][j], d_ng, d_rstd], [d_x[b][j]])
                P.add("dve", lambda e, j=j, blk=blk: e.tensor_scalar(
                    out=xT[:, j, blk], in0=xT[:, j, blk], scalar1=32.0, scalar2=None, op0=ALU.mult),
                    [d_x[b][j]], [d_x[b][j]])
        for j in range(8):
            P.add("sp", lambda e, j=j: e.dma_start(out=yT_out[:, j, :], in_=xT[:, j, :]),
                  [d_x[0][j], d_x[1][j]], [d_y], dma=True)

    if stage.startswith("dbg"):
        na_layer(1, stop=stage[3:] or None)
    elif stage == "naA":
        ffn(1, 0)
        na_layer(1)
    elif stage == "naB":
        na_layer(1)
        ffn(1, 1)
    else:
        for l in range(DEPTH):
            ffn(l, 0)
            if l % 2 == 0 and stage in ("full", "ab"):
                ab_layer(l)
            if l % 2 == 1 and stage in ("full", "na"):
                na_layer(l)
            ffn(l, 1)
    final_norm()

    P.emit()
    return nc, es


NA_BIG = -30000.0


def na_bias_table(rel_bias_h):
    col = np.arange(64)
    cs = np.clip(col - 8, 0, 48)
    valid = (col[None, :] >= cs[:, None]) & (col[None, :] < cs[:, None] + 16)
    cidx = np.clip(col[None, :] - col[:, None], -15, 15) + 15
    t = rel_bias_h[:, :, cidx]
    t = np.where(valid[None, None], t, np.float32(NA_BIG))
    return np.ascontiguousarray(t.transpose(3, 1, 0, 2)).astype(np.float32)


def prep_inputs(inp):
    f32 = np.float32
    xp = np.asarray(inp["x_prompt"], f32)
    xs = np.asarray(inp["x_sample"], f32)
    c = np.asarray(inp["c"], f32)
    c_ctx = np.asarray(inp["c_ctx"], f32)
    shared = {}
    aw = np.asarray(inp["ada_w"], f32)
    aw6 = aw.reshape(DEPTH, 8, 128, 4, 18, 128)
    ab = np.asarray(inp["ada_b"], f32).reshape(DEPTH, 4, 18, 128)
    ngm = np.concatenate([np.asarray(inp["norm_g"], f32).reshape(DEPTH * 3, D),
                          np.asarray(inp["final_norm_g"], f32).reshape(1, D)], axis=0)
    shared["normg"] = np.ascontiguousarray(ngm.reshape(DEPTH * 3 + 1, 8, 128).transpose(2, 0, 1))
    wg = np.asarray(inp["ffn_wg"], f32).reshape(DEPTH * 2, 8, 128, NF, 128)
    wu = np.asarray(inp["ffn_wu"], f32).reshape(DEPTH * 2, 8, 128, NF, 128)
    gu = np.stack([wg, wu], axis=0)
    shared["ffn_gu"] = np.ascontiguousarray(gu.transpose(1, 4, 3, 0, 2, 5))
    wd = np.asarray(inp["ffn_wd"], f32).reshape(DEPTH * 2, NF, 128, 8, 128)
    shared["ffn_d"] = np.ascontiguousarray(wd.transpose(0, 3, 2, 1, 4))
    shared["ident"] = np.eye(128, dtype=f32)
    nw = np.asarray(inp["na_w_in"], f32)[0]
    wqk = nw[:, :2048].reshape(8, 128, 16, 128)
    shared["na_wqk"] = np.ascontiguousarray(wqk.transpose(2, 1, 0, 3))
    wv = nw[:, 2048:].reshape(8, 128, 4, 256)
    shared["na_wv"] = np.ascontiguousarray(wv.transpose(2, 1, 0, 3))
    wo = np.asarray(inp["na_w_out"], f32)[0].reshape(8, 128, 8, 128)
    shared["na_wo"] = np.ascontiguousarray(wo.transpose(2, 1, 0, 3))
    W = np.asarray(inp["ab_w_in"], f32)[0]
    Wx = np.concatenate([W, np.zeros((D, 1), f32)], axis=1)
    ZC = W.shape[1]
    O_QM, O_KM, O_VM, O_OM, O_I, O_F, O_QD, O_KD, O_VD, O_ZD, O_B, O_A = 0, 256, 512, 1024, 1536, 1544, 1552, 2064, 2576, 3088, 3600, 3608

    def ab_variant(heads):
        nh_ = len(heads)
        cf = []
        for h in heads:
            cf += [O_QM + h * 64 + d for d in range(64)] + [ZC] * 64
        for h in heads:
            cf += [O_KM + h * 64 + d for d in range(64)] + [ZC] * 64
        for base in (O_QD, O_KD, O_VD):
            for h in heads:
                cf += [base + h * 128 + d for d in range(128)]
        ca = []
        for h in heads:
            ca += [O_VM + h * 128 + d for d in range(128)]
        for h in heads:
            ca += [O_KM + h * 64 + d for d in range(64)]
        for base in (O_I, O_F, O_B, O_A):
            for dr in range(2):
                for h in heads:
                    ca.append(base + dr * 4 + h)
        ca += [ZC] * ((-len(ca)) % 256)
        cb = []
        for base in (O_OM, O_ZD):
            for h in heads:
                cb += [base + h * 128 + d for d in range(128)]
        wf_ = Wx[:, cf].reshape(8, 128, len(cf) // 128, 128).transpose(2, 1, 0, 3)
        wa_ = Wx[:, ca].reshape(8, 128, len(ca) // 256, 256).transpose(2, 1, 0, 3)
        wb_ = Wx[:, cb].reshape(8, 128, len(cb) // 256, 256).transpose(2, 1, 0, 3)

        def gsel(a):
            return np.asarray(a, f32)[:, heads].reshape(-1)
        gpv = np.stack([gsel(inp["mlstm_b_i"][0]), gsel(inp["mlstm_b_f"][0]), gsel(inp["delta_dt_bias"][0]),
                        gsel(inp["delta_a_log"][0])], axis=0)
        nt_ = 16 // nh_
        gp_ = np.broadcast_to(gpv[None, :, None, :], (128, 4, nt_, 2 * nh_))
        cwv = np.asarray(inp["delta_conv_w"], f32)[0]
        cw_ = np.stack([cwv[:, kind * 512 + h * 128: kind * 512 + (h + 1) * 128].T
                        for kind in range(3) for h in heads], axis=1)
        gm = np.asarray(inp["mlstm_norm_g"], f32)[0]
        ngm_ = np.broadcast_to(np.concatenate([gm[h * 128:(h + 1) * 128] for h in heads])[None], (64, 128 * nh_))
        return dict(wf=np.ascontiguousarray(wf_), wtA=np.ascontiguousarray(wa_), wtB=np.ascontiguousarray(wb_),
                    gp=np.ascontiguousarray(gp_), cw=np.ascontiguousarray(cw_), ngm=np.ascontiguousarray(ngm_))
    vp_ = ab_variant([0, 1, 2, 3])
    for k_, v_ in vp_.items():
        shared["abp_" + k_] = v_
    vs_ = [ab_variant([r_]) for r_ in range(4)]
    shared["ab_ngd"] = np.ascontiguousarray(np.broadcast_to(np.asarray(inp["delta_norm_g"], f32)[0][None], (64, 128)))
    awo = np.asarray(inp["ab_w_out"], f32)[0].reshape(8, 128, 8, 128)
    shared["ab_wo"] = np.ascontiguousarray(awo.transpose(2, 1, 0, 3))
    pp = np.arange(128)
    same = (pp[:, None] // 64) == (pp[None, :] // 64)
    le = same & (pp[:, None] <= pp[None, :])
    lt = same & (pp[:, None] < pp[None, :])
    ge = same & (pp[:, None] >= pp[None, :])
    gt_ = same & (pp[:, None] > pp[None, :])
    NEGV = np.float32(-30000.0)
    msk = np.stack([le.astype(f32), ge.astype(f32), same.astype(f32),
                    np.where(le, 0, NEGV), np.where(lt, 0, NEGV), np.where(ge, 0, NEGV), np.where(gt_, 0, NEGV)],
                   axis=1).astype(f32)
    shared["ab_masks"] = np.ascontiguousarray(msk)
    shared["ab_chind"] = np.ascontiguousarray(np.stack([(pp < 64), (pp >= 64)], axis=1).astype(f32))
    sC = np.asarray(inp["state_mlstm_C"], f32)[:, 0]
    sN = np.asarray(inp["state_mlstm_n"], f32)[:, 0]
    sM = np.asarray(inp["state_mlstm_m"], f32)[:, 0]
    sS = np.asarray(inp["state_delta_S"], f32)[:, 0]
    rel = np.asarray(inp["na_rel_bias"], f32)[0]
    ck = np.asarray(inp["cache_na_k"], f32)[:, 0]
    cvv_ = np.asarray(inp["cache_na_v"], f32)[:, 0]
    in_maps = []
    for core in range(NCORES):
        g = core // 4
        r = core % 4
        xtok = np.concatenate([xp[2 * core].reshape(256, D), xp[2 * core + 1].reshape(256, D),
                               xs[g, 512 * r:512 * (r + 1)]], axis=0)
        m = dict(shared)
        m["xT_in"] = np.ascontiguousarray(xtok.reshape(NTOK, 8, 128).transpose(2, 1, 0))
        m["ada_w"] = np.ascontiguousarray(aw6[:, :, :, r].transpose(0, 3, 2, 1, 4).reshape(36, 128, 8, 128))
        m["ada_b"] = np.ascontiguousarray(ab[:, r].transpose(2, 0, 1).reshape(128, 36))
        cvv = np.stack([c_ctx, c[g]], axis=-1)
        m["cvec3"] = np.ascontiguousarray(cvv.reshape(8, 128, 2).transpose(1, 0, 2))
        ohm = np.zeros((128, 4), f32)
        ohm[:, r] = 1.0
        m["onehot"] = ohm
        hs = slice(4 * r, 4 * r + 4)
        m["na_wqk_s"] = np.ascontiguousarray(np.stack(
            [shared["na_wqk"][2 * r], shared["na_wqk"][2 * r + 1],
             shared["na_wqk"][8 + 2 * r], shared["na_wqk"][8 + 2 * r + 1]], axis=0))
        wvs = nw[:, 2048 + 256 * r: 2048 + 256 * (r + 1)].reshape(8, 128, 256)
        m["na_wv_s"] = np.ascontiguousarray(wvs.transpose(1, 0, 2))
        kc = ck[g, hs]
        kc = kc.reshape(2, 2, 256, 64).transpose(1, 3, 0, 2)
        m["na_kc"] = np.ascontiguousarray(kc.reshape(128, 2, 256))
        vc = cvv_[g, hs]
        vc = vc.reshape(4, 2, 128, 64).transpose(2, 1, 0, 3)
        m["na_vc"] = np.ascontiguousarray(vc)
        m["na_bias"] = na_bias_table(rel[hs])
        for k_, v_ in vs_[r].items():
            m["abs_" + k_] = v_
        c0 = np.zeros((128, 2, 130), f32)
        c0[0:64, :, 0:128] = sC[g, :, r].transpose(1, 0, 2)
        c0[0:64, :, 128] = sN[g, :, r].T
        m["ab_c0"] = c0
        m["ab_m0"] = np.ascontiguousarray(np.broadcast_to(sM[g, :, r][None], (128, 2)))
        m["ab_s0"] = np.ascontiguousarray(sS[g, :, r].transpose(1, 0, 2))
        in_maps.append(m)
    return in_maps


def kernel(**inp):
    stage = inp.pop("_stage", "full")
    nc, es = build_program(stage)
    with es:
        in_maps = prep_inputs(inp)
        if stage.startswith("dbg"):
            for m in in_maps:
                for k_ in ("ada_w", "ffn_gu", "ffn_d"):
                    m.pop(k_, None)
        res = run_bass_kernel_spmd(nc, in_maps, core_ids=list(range(NCORES)))
    outs = res.results
    y_prompt = np.zeros((16, 256, D), np.float32)
    y_sample = np.zeros((2, 2048, D), np.float32)
    new_k = np.zeros((16, 1, 16, 256, 64), np.float32)
    new_v = np.zeros((16, 1, 16, 256, 64), np.float32)
    new_C = np.zeros((16, 1, 2, 4, 64, 128), np.float32)
    new_n = np.zeros((16, 1, 2, 4, 64), np.float32)
    new_m = np.zeros((16, 1, 2, 4), np.float32)
    new_S = np.zeros((16, 1, 2, 4, 128, 128), np.float32)
    for core in range(NCORES):
        g = core // 4
        r = core % 4
        yT = outs[core]["yT_out"]
        y = yT.transpose(2, 1, 0).reshape(NTOK, D)
        y_prompt[2 * core] = y[0:256]
        y_prompt[2 * core + 1] = y[256:512]
        y_sample[g, 512 * r:512 * (r + 1)] = y[512:1024]
        kT = outs[core]["kT_out"].reshape(2, 64, 8, 2, 256)
        new_k[2 * core:2 * core + 2, 0] = kT.transpose(3, 2, 0, 4, 1).reshape(2, 16, 256, 64)
        vv = outs[core]["v_out"].transpose(1, 0, 2).reshape(2, 256, 16, 64)
        new_v[2 * core:2 * core + 2, 0] = vv.transpose(0, 2, 1, 3)
        co = outs[core]["c_out"].reshape(2, 64, 2, 4, 130)
        so = outs[core]["s_out"].reshape(2, 128, 2, 4, 128)
        mo = outs[core]["m_out"].reshape(4, 2, 2)
        for si in range(2):
            bidx = 2 * core + si
            new_C[bidx, 0] = co[si, :, :, :, 0:128].transpose(1, 2, 0, 3)
            new_n[bidx, 0] = co[si, :, :, :, 128].transpose(1, 2, 0)
            new_m[bidx, 0] = mo[:, si, :].T
            new_S[bidx, 0] = so[si].transpose(1, 2, 0, 3)
    if stage in ("full", "ab"):
        return y_prompt, y_sample, new_C, new_n, new_m, new_S, new_k, new_v
    return y_prompt, y_sample, new_k, new_v
```

```python
import numpy as np
import ml_dtypes
from contextlib import ExitStack
import concourse.bass as bass
import concourse.mybir as mybir
from concourse.bass_utils import run_bass_kernel_spmd

F32 = mybir.dt.float32
BF16 = mybir.dt.bfloat16
AF = mybir.ActivationFunctionType
ALU = mybir.AluOpType
AX = mybir.AxisListType

D = 1024
DFF = 2816
NF = DFF // 128
DEPTH = 2
EPS = 1e-6
NTOK = 1024
BLK = 512
NCORES = 8


class Dep:
    __slots__ = ("name", "w", "rs", "sem", "hist", "excl")

    def __init__(self, name, excl=False):
        self.name = name
        self.excl = excl
        self.w = None
        self.rs = []
        self.sem = None
        self.hist = []


class Op:
    __slots__ = ("eng", "fn", "deps", "dma", "dst", "inc", "val", "idx", "incv")

    def __init__(self, eng, fn, deps, dma, dst, idx, incv=16):
        self.incv = incv
        self.eng = eng
        self.fn = fn
        self.deps = deps
        self.dma = dma
        self.dst = dst
        self.inc = False
        self.val = 0
        self.idx = idx


ENGS = ("pe", "act", "dve", "pool", "sp")


class Prog:
    def __init__(self, nc, es):
        self.nc = nc
        self.es = es
        self.ops = []
        self.ntile = 0

    def sb(self, shape, dt, name=None):
        self.ntile += 1
        return self.es.enter_context(self.nc.sbuf_tensor(name or f"t{self.ntile}", list(shape), dt))

    def psum(self, shape, dt, name=None):
        self.ntile += 1
        return self.es.enter_context(self.nc.psum_tensor(name or f"ps{self.ntile}", list(shape), dt))

    def add(self, eng, fn, reads=(), writes=(), dma=False, incv=16):
        idx = len(self.ops)
        deps = set()
        if any(d.excl for d in reads):
            writes = list(writes) + [d for d in reads if d.excl and d not in writes]
            reads = [d for d in reads if not d.excl]
        for d in reads:
            if d.w is not None:
                deps.add(d.w)
        for d in writes:
            if d.w is not None:
                deps.add(d.w)
            deps.update(d.rs)
        deps.discard(idx)
        last = {}
        keep = set()
        for di in deps:
            y = self.ops[di]
            if y.dma:
                keep.add(di)
            elif y.eng not in last or last[y.eng] < di:
                last[y.eng] = di
        keep.update(last.values())
        deps = keep
        for d in reads:
            d.rs.append(idx)
        for d in writes:
            d.w = idx
            d.rs = []
        op = Op(eng, fn, deps, dma, writes[0] if dma else None, idx, incv)
        self.ops.append(op)
        return op

    def mm(self, out, lhsT, rhs, start, stop, reads, writes):
        return self.add("pe", lambda e: e.matmul(out, lhsT=lhsT, rhs=rhs, start=start, stop=stop), reads, writes)

    def emit(self):
        nc = self.nc
        ops = self.ops
        for op in ops:
            for di in op.deps:
                y = ops[di]
                if y.dma:
                    continue
                if y.eng == "pe" and op.eng == "pe" and not op.dma:
                    continue
                y.inc = True
        engsem = {e: self.es.enter_context(nc.semaphore(f"sem_{e}")) for e in ENGS}
        cnt = {e: 0 for e in ENGS}
        for op in ops:
            if op.dma:
                d = op.dst
                if d.sem is None:
                    d.sem = self.es.enter_context(nc.semaphore(f"dsem_{d.name}"))
                c = (d.hist[-1][1] if d.hist else 0) + op.incv
                d.hist.append((op.idx, c))
                op.val = c
            elif op.inc:
                cnt[op.eng] += 1
                op.val = cnt[op.eng]
        per = {e: [] for e in ENGS}
        for op in ops:
            per[op.eng].append(op)

        def dma_wait_val(y, x):
            v = y.val
            for (i, c) in y.dst.hist:
                if i < x.idx:
                    v = max(v, c)
                else:
                    break
            return v

        def run_engine(ename, eobj):
            waited = {}
            for op in per[ename]:
                need = {}
                for di in op.deps:
                    y = ops[di]
                    if y.dma:
                        s = y.dst.sem
                        v = dma_wait_val(y, op)
                    else:
                        if y.eng == "pe" and op.eng == "pe" and not op.dma:
                            continue
                        s = engsem[y.eng]
                        v = y.val
                    k = id(s)
                    if k not in need or need[k][1] < v:
                        need[k] = (s, v)
                for k, (s, v) in need.items():
                    if waited.get(k, 0) >= v:
                        continue
                    waited[k] = v
                    eobj.wait_ge(s, v)
                ins = op.fn(eobj)
                if op.dma:
                    ins.then_inc(op.dst.sem, op.incv)
                elif op.inc:
                    ins.then_inc(engsem[op.eng], 1)
            last = {}
            for op in per[ename]:
                if op.dma:
                    last[id(op.dst.sem)] = (op.dst.sem, op.dst.hist[-1][1])
            for k, (s, v) in last.items():
                if waited.get(k, 0) < v:
                    eobj.wait_ge(s, v)

        with nc.Block() as block:
            @block.tensor
            def _(e):
                run_engine("pe", e)

            @block.scalar
            def _(e):
                run_engine("act", e)

            @block.vector
            def _(e):
                run_engine("dve", e)

            @block.gpsimd
            def _(e):
                run_engine("pool", e)

            @block.sync
            def _(e):
                run_engine("sp", e)


def build_program(stage="full"):
    nc = bass.Bass("TRN2", target_bir_lowering=False)
    es = ExitStack()
    P = Prog(nc, es)

    def din(name, shape, dt=F32):
        return nc.dram_tensor(name, list(shape), dt, kind="ExternalInput").ap()

    def dout(name, shape, dt=F32):
        return nc.dram_tensor(name, list(shape), dt, kind="ExternalOutput").ap()

    def dint(name, shape, dt):
        return nc.dram_tensor(name, list(shape), dt, kind="Internal").ap()

    xT_in = din("xT_in", [128, 8, NTOK])
    DBG = stage.startswith("dbg")
    ada_w = None if DBG else din("ada_w", [36, 128, 8, 128])
    ada_b = din("ada_b", [128, 36])
    cvec3 = din("cvec3", [128, 8, 2])
    agm_in = dint("agm_in", [128, 72], F32)
    agm_out = dint("agm_out", [4 * 128, 72], F32)
    normg = din("normg", [128, DEPTH * 3 + 1, 8])
    ffn_gu = None if DBG else din("ffn_gu", [DEPTH * 2, NF, 128, 2, 8, 128])
    ffn_d = None if DBG else din("ffn_d", [DEPTH * 2, 8, 128, NF, 128])
    ident_in = din("ident", [128, 128])
    oh_in = din("onehot", [128, 4])
    na_wqk = din("na_wqk", [16, 128, 8, 128])
    na_wv = din("na_wv", [4, 128, 8, 256])
    na_wo = din("na_wo", [8, 128, 8, 128])
    na_wqk_s = din("na_wqk_s", [4, 128, 8, 128])
    na_wv_s = din("na_wv_s", [128, 8, 256])
    na_kc = din("na_kc", [128, 2, 256])
    na_vc = din("na_vc", [128, 2, 4, 64])
    na_bias = din("na_bias", [64, 15, 4, 64])
    abp_wf = din("abp_wf", [20, 128, 8, 128])
    abp_wtA = din("abp_wtA", [4, 128, 8, 256])
    abp_wtB = din("abp_wtB", [4, 128, 8, 256])
    abp_gp = din("abp_gp", [128, 4, 4, 8])
    abp_cw = din("abp_cw", [128, 12, 5])
    abp_ngm = din("abp_ngm", [64, 512])
    abs_wf = din("abs_wf", [5, 128, 8, 128])
    abs_wtA = din("abs_wtA", [1, 128, 8, 256])
    abs_wtB = din("abs_wtB", [1, 128, 8, 256])
    abs_gp = din("abs_gp", [128, 4, 16, 2])
    abs_cw = din("abs_cw", [128, 3, 5])
    abs_ngm = din("abs_ngm", [64, 128])
    ab_ngd = din("ab_ngd", [64, 128])
    ab_wo = din("ab_wo", [8, 128, 8, 128])
    ab_c0 = din("ab_c0", [128, 2, 130])
    ab_m0 = din("ab_m0", [128, 2])
    ab_s0 = din("ab_s0", [128, 2, 128])
    ab_masks = din("ab_masks", [128, 7, 128])
    ab_chind = din("ab_chind", [128, 2])
    c_out = dout("c_out", [2, 64, 2 * 4 * 130])
    s_out = dout("s_out", [2, 128, 2 * 4 * 128])
    m_out = dout("m_out", [4, 4])
    yT_out = dout("yT_out", [128, 8, NTOK])
    kT_out = dout("kT_out", [128, 8, BLK])
    v_out = dout("v_out", [128, 4, 1024])
    ag1_in = dint("ag1_in", [128, 8 * BLK], BF16)
    ag1_out = dint("ag1_out", [4 * 128, 8 * BLK], BF16)
    ag2_in = dint("ag2_in", [128, 2 * 2048], BF16)
    ag2_out = dint("ag2_out", [4 * 128, 2 * 2048], BF16)
    GROUPS = [[0, 1, 2, 3], [4, 5, 6, 7]]

    xT = P.sb([128, 8, NTOK], F32, "xT")
    hT = P.sb([128, 8, NTOK], BF16, "hT")
    ARENA_KB = 104
    arena = P.sb([128, ARENA_KB * 256], F32, "arena")

    def AV(off_kb, shape, dt, parts=128):
        n = 1
        for s_ in shape[1:]:
            n *= s_
        nbytes = n * (2 if dt == BF16 else 4)
        w0 = int(off_kb * 256)
        assert w0 * 4 == int(off_kb * 1024)
        nw = (nbytes + 3) // 4
        assert w0 + nw <= ARENA_KB * 256, (off_kb, shape)
        v = arena[0:parts, w0:w0 + nw]
        if dt == BF16:
            v = v.bitcast(BF16)
        if len(shape) == 2:
            return v
        names = "abcdef"[:len(shape) - 1]
        pat = "p (" + " ".join(names) + ") -> p " + " ".join(names)
        kw = {names[i]: shape[1 + i] for i in range(len(shape) - 1)}
        return v.rearrange(pat, **kw)

    aT = AV(0, [128, NF, NTOK], BF16)
    sq = P.sb([128, 8, BLK], BF16, "sq")
    rstd = P.sb([128, BLK], F32, "rstd")
    tmp = [P.sb([128, BLK], F32, f"tmp{i}") for i in range(2)]
    sg = tmp
    NGU = 3
    gu_s = [P.sb([128, 2, 8, 128], BF16, f"gu{i}") for i in range(NGU)]
    NWD = 2
    wd_s = [P.sb([128, NF, 128], BF16, f"wd{i}") for i in range(NWD)]
    cv = P.sb([128, 8, 2], F32, "cv")
    scT = P.sb([128, 8, 2], BF16, "scT")
    adab = P.sb([128, 36], F32, "adab")
    modpart = P.sb([128, 36, 2], F32, "modpart")
    modall = P.sb([128, 4, 72], F32, "modall")
    ng = P.sb([128, DEPTH * 3 + 1, 8], F32, "ng")
    modT = P.sb([128, DEPTH, 72, 2], F32, "modT")
    Acoef = P.sb([128, DEPTH * 3, 8, 2], F32, "Acoef")
    Gcoef = P.sb([128, DEPTH * 3, 8, 2], F32, "Gcoef")
    ones_bf = P.sb([128, 128], BF16, "ones_bf")
    onespad = P.sb([128, 2, 128], BF16, "onespad")
    ident_f = P.sb([128, 128], F32, "ident_f")
    ident_b = P.sb([128, 128], BF16, "ident_b")
    oh = P.sb([128, 4], F32, "oh")
    epsb = P.sb([128, 4], F32, "epsb")
    mk = P.sb([128, 7, 128], F32, "mk")
    chind = P.sb([128, 2], F32, "chind")
    ones_f = P.sb([128, 128], F32, "ones_f")
    ps2 = [P.psum([128, 1024], F32, f"psb{i}") for i in range(4)]
    ps = [ps2[i // 2][:, (i % 2) * 512:(i % 2 + 1) * 512] for i in range(8)]

    d_x = [[Dep(f"x{b}_{j}") for j in range(8)] for b in range(2)]
    d_h = [[Dep(f"h{b}_{j}") for j in range(8)] for b in range(2)]
    d_a = [Dep(f"a{b}") for b in range(2)]
    d_sq = Dep("sq")
    d_rstd = Dep("rstd")
    d_tmp = [Dep("tmp0"), Dep("tmp1")]
    d_sg = d_tmp
    d_gu = [Dep(f"gu{i}") for i in range(NGU)]
    d_wd = [Dep(f"wd{i}") for i in range(NWD)]
    d_cv = Dep("cv")
    d_sc = Dep("scT")
    d_adab = Dep("adab")
    d_ng = Dep("ng")
    d_mod = Dep("modT")
    d_coef = Dep("coef")
    d_ones = Dep("ones")
    d_ps = [Dep(f"ps{i}", excl=True) for i in range(8)]
    d_y = Dep("yout")

    P.add("dve", lambda e: e.memset(ones_bf[:], 1.0), [], [d_ones])
    P.add("dve", lambda e: e.memset(epsb[:, 0:1], float(D * EPS)), [], [d_ones])
    P.add("dve", lambda e: e.memset(epsb[:, 1:2], 1e-6), [d_ones], [d_ones])
    P.add("dve", lambda e: e.memset(epsb[:, 2:3], 1.0), [d_ones], [d_ones])
    P.add("dve", lambda e: e.memset(epsb[:, 3:4], 1e-6), [d_ones], [d_ones])
    P.add("dve", lambda e: e.memset(ones_f[:], 1.0), [d_ones], [d_ones])
    d_msk = Dep("masks")
    P.add("sp", lambda e: e.dma_start(out=mk[:], in_=ab_masks), [], [d_msk], dma=True)
    P.add("sp", lambda e: e.dma_start(out=chind[:], in_=ab_chind), [d_msk], [d_msk], dma=True)
    P.add("dve", lambda e: e.memset(onespad[:], 0.0), [], [d_ones])
    P.add("dve", lambda e: e.memset(onespad[:, 0, 0:64], 1.0), [d_ones], [d_ones])
    P.add("dve", lambda e: e.memset(onespad[:, 1, 64:128], 1.0), [d_ones], [d_ones])
    d_id = Dep("ident")
    P.add("sp", lambda e: e.dma_start(out=ident_f[:], in_=ident_in), [], [d_id], dma=True)
    P.add("dve", lambda e: e.tensor_copy(out=ident_b[:], in_=ident_f[:]), [d_id], [d_ones])
    d_oh = Dep("oh")
    P.add("sp", lambda e: e.dma_start(out=oh[:], in_=oh_in), [], [d_oh], dma=True)
    P.add("sp", lambda e: e.dma_start(out=xT[:], in_=xT_in), [], [d_x[0][0]], dma=True)
    for j in range(8):
        d_x[0][j].w = d_x[0][0].w
        d_x[0][j].sem = None
        d_x[1][j].w = d_x[0][0].w
    P.add("sp", lambda e: e.dma_start(out=cv[:], in_=cvec3), [], [d_cv], dma=True)
    P.add("sp", lambda e: e.dma_start(out=adab[:], in_=ada_b), [], [d_adab], dma=True)
    P.add("sp", lambda e: e.dma_start(out=ng[:], in_=normg), [], [d_ng], dma=True)
    P.add("act", lambda e: e.activation(out=scT[:], in_=cv[:], func=AF.Silu), [d_cv], [d_sc])

    cnt_ada = 0
    if DBG:
        P.add("dve", lambda e: e.memset(modT[:].rearrange("p a b c -> p (a b c)"), 0.1), [], [d_mod])
    else:
        for q in range(36):
            s = cnt_ada % NGU
            cnt_ada += 1
            wv = gu_s[s][:, 0, :, :]
            P.add("pool", lambda e, q=q, wv=wv: e.dma_start(out=wv, in_=ada_w[q]), [], [d_gu[s]], dma=True)
            for k in range(8):
                P.mm(ps[7][:, 2 * q:2 * q + 2], wv[:, k, :], scT[:, k, :], k == 0, k == 7, [d_gu[s], d_sc], [d_ps[7]])
        d_mp = Dep("modpart")
        for v in range(2):
            P.add("dve", lambda e, v=v: e.tensor_tensor(
                out=modpart[:, :, v], in0=ps[7][:, 0:72].rearrange("p (q v) -> p q v", v=2)[:, :, v],
                in1=adab[:], op=ALU.add), [d_ps[7], d_adab], [d_mp])
        d_agmi, d_agmo, d_mall = Dep("agmi"), Dep("agmo"), Dep("modall")
        P.add("sp", lambda e: e.dma_start(out=agm_in, in_=modpart[:].rearrange("p q v -> p (q v)")), [d_mp], [d_agmi], dma=True)
        P.add("pool", lambda e: e.collective_compute("AllGather", ALU.bypass, replica_groups=GROUPS,
                                                     ins=[agm_in], outs=[agm_out]),
              [d_agmi], [d_agmo], dma=True, incv=1)
        P.add("sp", lambda e: e.dma_start(out=modall[:], in_=agm_out.rearrange("(r p) c -> p r c", p=128)),
              [d_agmo], [d_mall], dma=True)
        for l in range(DEPTH):
            srcv = modall[:, :, l * 36:(l + 1) * 36].rearrange("p r (j v) -> p r j v", v=2)
            dstv = modT[:, l].rearrange("p (r j) v -> p r j v", r=4)
            for v in range(2):
                P.add("dve", lambda e, srcv=srcv, dstv=dstv, v=v: e.tensor_copy(out=dstv[:, :, :, v], in_=srcv[:, :, :, v]),
                      [d_mall], [d_mod])
    for l in range(DEPTH):
        for s_ in range(3):
            ls = l * 3 + s_
            for v in range(2):
                sc_ap = modT[:, l, (3 * s_ + 1) * 8:(3 * s_ + 2) * 8, v]
                P.add("dve", lambda e, ls=ls, v=v, sc_ap=sc_ap: e.scalar_tensor_tensor(
                    out=Acoef[:, ls, :, v], in0=sc_ap, scalar=1.0, in1=ng[:, ls, :], op0=ALU.add, op1=ALU.mult),
                    [d_mod, d_ng], [d_coef])
                P.add("dve", lambda e, ls=ls, v=v: e.tensor_scalar(
                    out=Acoef[:, ls, :, v], in0=Acoef[:, ls, :, v], scalar1=32.0, scalar2=None, op0=ALU.mult),
                    [d_coef], [d_coef])
                g_ap = modT[:, l, (3 * s_ + 2) * 8:(3 * s_ + 3) * 8, v]
                fac = 1.0 if s_ == 1 else 0.5
                P.add("dve", lambda e, ls=ls, v=v, g_ap=g_ap, fac=fac: e.tensor_scalar(
                    out=Gcoef[:, ls, :, v], in0=g_ap, scalar1=fac, scalar2=None, op0=ALU.mult),
                    [d_mod], [d_coef])

    def rstd_from_ps6():
        P.add("act", lambda e: e.activation(out=rstd[:], in_=ps[6], func=AF.Ln, bias=epsb[:, 0:1], scale=1.0),
              [d_ps[6], d_ones], [d_rstd])
        P.add("act", lambda e: e.activation(out=rstd[:], in_=rstd[:], func=AF.Exp, scale=-0.5),
              [d_rstd], [d_rstd])

    def modulate(l, s_, b):
        ls = l * 3 + s_
        blk = slice(b * BLK, (b + 1) * BLK)
        P.add("act", lambda e: e.activation(out=sq[:], in_=xT[:, :, blk], func=AF.Square),
              d_x[b], [d_sq])
        for k in range(8):
            P.mm(ps[6], ones_bf[:], sq[:, k, :], k == 0, k == 7, [d_ones, d_sq], [d_ps[6]])
        rstd_from_ps6()
        for j in range(8):
            t = j % 2
            P.add("dve", lambda e, j=j, t=t: e.scalar_tensor_tensor(
                out=tmp[t][:], in0=xT[:, j, blk], scalar=Acoef[:, ls, j, b:b + 1], in1=rstd[:],
                op0=ALU.mult, op1=ALU.mult), [d_x[b][j], d_coef, d_rstd], [d_tmp[t]])
            P.add("act", lambda e, j=j, t=t: e.activation(
                out=hT[:, j, blk], in_=tmp[t][:], func=AF.Identity,
                bias=modT[:, l, 3 * s_ * 8 + j, b:b + 1], scale=1.0), [d_tmp[t], d_mod], [d_h[b][j]])

    state = {"gu": cnt_ada, "wd": 0, "psA": 0, "psB": 0, "sg": 0}

    def ffn(l, which):
        fi = l * 2 + which
        s_ = 0 if which == 0 else 2
        ls = l * 3 + s_
        for b in range(2):
            modulate(l, s_, b)
        for f in range(NF):
            s = state["gu"] % NGU
            state["gu"] += 1
            P.add("pool", lambda e, f=f, s=s: e.dma_start(out=gu_s[s][:], in_=ffn_gu[fi, f]), [], [d_gu[s]], dma=True)
            for b in range(2):
                blk = slice(b * BLK, (b + 1) * BLK)
                pa = (state["psA"] % 2) * 2
                state["psA"] += 1
                for w in range(2):
                    for k in range(8):
                        P.mm(ps[pa + w], gu_s[s][:, w, k, :], hT[:, k, blk], k == 0, k == 7,
                             [d_gu[s]] + d_h[b], [d_ps[pa + w]])
                t = state["sg"] % 2
                state["sg"] += 1
                P.add("act", lambda e, pa=pa, t=t: e.activation(out=sg[t][:], in_=ps[pa], func=AF.Silu),
                      [d_ps[pa]], [d_sg[t]])
                P.add("dve", lambda e, pa=pa, t=t, f=f, blk=blk: e.tensor_tensor(
                    out=aT[:, f, blk], in0=sg[t][:], in1=ps[pa + 1], op=ALU.mult),
                    [d_sg[t], d_ps[pa + 1]], [d_a[b]])
        for j in range(8):
            s = state["wd"] % NWD
            state["wd"] += 1
            P.add("pool", lambda e, j=j, s=s: e.dma_start(out=wd_s[s][:], in_=ffn_d[fi, j]), [], [d_wd[s]], dma=True)
            for b in range(2):
                blk = slice(b * BLK, (b + 1) * BLK)
                pb = 4 + (state["psB"] % 2)
                state["psB"] += 1
                for f in range(NF):
                    P.mm(ps[pb], wd_s[s][:, f, :], aT[:, f, blk], f == 0, f == NF - 1,
                         [d_wd[s], d_a[b]], [d_ps[pb]])
                P.add("dve", lambda e, j=j, b=b, blk=blk, pb=pb: e.scalar_tensor_tensor(
                    out=xT[:, j, blk], in0=ps[pb], scalar=Gcoef[:, ls, j, b:b + 1], in1=xT[:, j, blk],
                    op0=ALU.mult, op1=ALU.add), [d_ps[pb], d_coef, d_x[b][j]], [d_x[b][j]])

    def out_proj(l, wo_dram, mix, d_mix, b):
        ls = l * 3 + 1
        blk = slice(b * BLK, (b + 1) * BLK)
        for j in range(8):
            s = state["gu"] % NGU
            state["gu"] += 1
            wv = gu_s[s][:, 0, :, :]
            P.add("pool", lambda e, j=j, wv=wv: e.dma_start(out=wv, in_=wo_dram[j]), [], [d_gu[s]], dma=True)
            pb = 4 + (state["psB"] % 2)
            state["psB"] += 1
            for c in range(8):
                P.mm(ps[pb], wv[:, c, :], mix[:, c, :], c == 0, c == 7, [d_gu[s]] + d_mix, [d_ps[pb]])
            P.add("dve", lambda e, j=j, pb=pb: e.scalar_tensor_tensor(
                out=xT[:, j, blk], in0=ps[pb], scalar=Gcoef[:, ls, j, b:b + 1], in1=xT[:, j, blk],
                op0=ALU.mult, op1=ALU.add), [d_ps[pb], d_coef, d_x[b][j]], [d_x[b][j]])

    def allgather(src_ap_sb, d_src, ag_in_view, ag_in, ag_out, d_agin, d_agout):
        P.add("sp", lambda e: e.dma_start(out=ag_in_view, in_=src_ap_sb), d_src, [d_agin], dma=True)
        P.add("pool", lambda e: e.collective_compute("AllGather", ALU.bypass, replica_groups=GROUPS,
                                                     ins=[ag_in], outs=[ag_out]),
              [d_agin], [d_agout], dma=True, incv=1)

    def na_layer(l, stop=None):
        for b in range(2):
            modulate(l, 1, b)
        if stop == "P0":
            return
        d_ag1i, d_ag1o = Dep("ag1i"), Dep("ag1o")
        allgather(hT[:, :, BLK:2 * BLK], d_h[1], ag1_in.rearrange("p (k t) -> p k t", k=8), ag1_in, ag1_out, d_ag1i, d_ag1o)
        if stop == "P0b":
            return
        qT_p = AV(0, [128, 8, BLK], BF16)
        kT_p = AV(8, [128, 8, BLK], BF16)
        kT32 = AV(16, [128, 8, BLK], F32)
        v32 = AV(32, [128, 4, 1024], F32)
        Vpad = AV(48, [128, 4, 8, 2, 128], BF16)
        PTp = [AV(64 + 2 * i, [128, 4, 256], BF16) for i in range(2)]
        OT_p = AV(68, [128, 8, BLK], BF16)
        rcp = [AV(76 + i, [128, 256], F32) for i in range(2)]
        d_q, d_k, d_k32, d_v32, d_vpad = Dep("qTp"), Dep("kTp"), Dep("kT32"), Dep("v32"), Dep("Vpad")
        d_pt = [Dep("PTp0"), Dep("PTp1")]
        d_ot = Dep("OTp")
        d_rcp = [Dep("rcp0"), Dep("rcp1")]
        for dd in (d_q, d_k, d_k32, d_v32, d_vpad, d_pt[0], d_pt[1], d_ot, d_rcp[0], d_rcp[1]):
            dd.rs = list(d_a[0].rs) + list(d_a[1].rs)
            dd.w = d_a[1].w
        for i_ in range(4):
            P.add("dve", lambda e, i_=i_: e.memset(AV(48 + 4 * i_, [128, 2048], BF16), 0.0), [], [d_vpad])
        pj = 0
        for c in range(16):
            s = state["gu"] % NGU
            state["gu"] += 1
            wv = gu_s[s][:, 0, :, :]
            P.add("pool", lambda e, c=c, wv=wv: e.dma_start(out=wv, in_=na_wqk[c]), [], [d_gu[s]], dma=True)
            pb = 6 + (pj % 2)
            pj += 1
            for k in range(8):
                P.mm(ps[pb], wv[:, k, :], hT[:, k, 0:BLK], k == 0, k == 7, [d_gu[s]] + d_h[0], [d_ps[pb]])
            if c < 8:
                P.add("act", lambda e, c=c, pb=pb: e.activation(out=qT_p[:, c, :], in_=ps[pb], func=AF.Identity,
                                                              scale=0.125), [d_ps[pb]], [d_q])
            else:
                P.add("act", lambda e, c=c, pb=pb: e.activation(out=kT32[:, c - 8, :], in_=ps[pb], func=AF.Identity),
                      [d_ps[pb]], [d_k32])
                P.add("dve", lambda e, c=c, pb=pb: e.tensor_copy(out=kT_p[:, c - 8, :], in_=ps[pb]),
                      [d_ps[pb]], [d_k])
        if stop == "P0c":
            return
        P.add("sp", lambda e: e.dma_start(out=kT_out, in_=kT32), [d_k32], [d_y], dma=True)
        for qv in range(4):
            s = state["gu"] % NGU
            state["gu"] += 1
            wv = gu_s[s][:].rearrange("p a b c -> p (a b c)").rearrange("p (k n) -> p k n", k=8)
            P.add("pool", lambda e, qv=qv, wv=wv: e.dma_start(out=wv, in_=na_wv[qv]), [], [d_gu[s]], dma=True)
            for tt in range(4):
                pb = 6 + (pj % 2)
                pj += 1
                for k in range(8):
                    P.mm(ps[pb][:, 0:256], hT[:, k, tt * 128:(tt + 1) * 128], wv[:, k, :], k == 0, k == 7,
                         [d_gu[s]] + d_h[0], [d_ps[pb]])
                P.add("act", lambda e, tt=tt, qv=qv, pb=pb: e.activation(
                    out=v32[:, tt, qv * 256:(qv + 1) * 256], in_=ps[pb][:, 0:256], func=AF.Identity), [d_ps[pb]], [d_v32])
                for e_ in range(2):
                    src = ps[pb][:, 0:256].rearrange("p (c e d) -> p c e d", c=2, e=2)[:, :, e_, :]
                    P.add("dve", lambda e, tt=tt, qv=qv, e_=e_, src=src: e.tensor_copy(
                        out=Vpad[:, tt, qv * 2:(qv + 1) * 2, e_, e_ * 64:(e_ + 1) * 64], in_=src),
                        [d_ps[pb]], [d_vpad])
        P.add("sp", lambda e: e.dma_start(out=v_out, in_=v32), [d_v32], [d_y], dma=True)
        if stop == "P1":
            return
        it = 0
        for sq_ in range(2):
            t0 = sq_ * 256
            for c in range(8):
                S2 = ps2[it % 2]
                dS = [d_ps[(it % 2) * 2], d_ps[(it % 2) * 2 + 1]]
                for hh in range(2):
                    for sb in range(2):
                        i4 = hh * 2 + sb
                        P.mm(S2[:, i4 * 256:(i4 + 1) * 256],
                             kT_p[hh * 64:(hh + 1) * 64, c, t0 + sb * 128:t0 + (sb + 1) * 128],
                             qT_p[hh * 64:(hh + 1) * 64, c, t0:t0 + 256], True, True,
                             [d_k, d_q], [dS[i4 // 2]])
                pt = PTp[it % 2]
                P.add("act", lambda e, S2=S2, pt=pt: e.activation(
                    out=pt.rearrange("p a b -> p (a b)"), in_=S2[:, :], func=AF.Exp), dS, [d_pt[it % 2]])
                pb = 4 + (it % 2)
                n = 0
                for hh in range(2):
                    for sb in range(2):
                        P.mm(ps[pb][:, 0:256], Vpad[:, sq_ * 2 + sb, c, hh, :], pt[:, hh * 2 + sb, :], n == 0, n == 3,
                             [d_vpad, d_pt[it % 2]], [d_ps[pb]])
                        n += 1
                n = 0
                for hh in range(2):
                    for sb in range(2):
                        P.mm(ps[pb][:, 256:512], onespad[:, hh, :], pt[:, hh * 2 + sb, :], n == 0, n == 3,
                             [d_ones, d_pt[it % 2]], [d_ps[pb]])
                        n += 1
                rc = rcp[it % 2]
                P.add("dve", lambda e, rc=rc, pb=pb: e.reciprocal(out=rc, in_=ps[pb][:, 256:512]),
                      [d_ps[pb]], [d_rcp[it % 2]])
                P.add("dve", lambda e, rc=rc, pb=pb, c=c, t0=t0: e.tensor_tensor(
                    out=OT_p[:, c, t0:t0 + 256], in0=ps[pb][:, 0:256], in1=rc, op=ALU.mult),
                    [d_ps[pb], d_rcp[it % 2]], [d_ot])
                it += 1
        if stop == "P2":
            return
        out_proj(l, na_wo, OT_p, [d_ot], 0)
        if stop == "P3":
            return

        hfull = AV(0, [128, 8, 2048], BF16)
        qT_s = AV(32, [128, 2, 2048], BF16)
        kTpad = AV(40, [128, 2, 2, 2048], BF16)
        v64 = AV(56, [128, 32, 2, 2, 128], BF16)
        biasT = AV(0, [128, 15, 4, 64], BF16)
        kcpad = AV(8, [128, 2, 2, 256], BF16)
        vcb = AV(10, [128, 2, 4, 64], BF16)
        Vcpad = AV(11, [128, 2, 2, 2, 128], BF16)
        PTl = [AV(13 + 2 * i, [128, 2, 8, 64], BF16) for i in range(2)]
        PTc = [AV(17 + 0.5 * i, [128, 2, 2, 64], BF16) for i in range(2)]
        rcs = [AV(18 + 0.25 * i, [128, 64], F32) for i in range(2)]
        OT_s = AV(20, [128, 2, 2048], BF16)
        d_hf = Dep("hfull")
        d_qs, d_ks, d_v64 = Dep("qTs"), Dep("kTs"), Dep("v64")
        d_bias, d_kc, d_vcb, d_vcp = Dep("biasT"), Dep("kcT"), Dep("vcb"), Dep("Vcpad")
        d_ptl = [Dep("PTl0"), Dep("PTl1")]
        d_ptc = [Dep("PTc0"), Dep("PTc1")]
        d_rcs = [Dep("rcs0"), Dep("rcs1")]
        d_ots = Dep("OTs")
        last_prompt = [d_q, d_k, d_k32, d_v32, d_vpad, d_pt[0], d_pt[1], d_ot, d_rcp[0], d_rcp[1]]

        def users_of(deplist):
            rs = []
            for o in deplist:
                rs += list(o.rs)
                if o.w is not None:
                    rs.append(o.w)
            return rs
        for dd in (d_hf, d_qs, d_ks, d_v64):
            dd.rs = users_of(last_prompt) + users_of(d_a)
            dd.w = None
        for m in range(4):
            P.add("sp", lambda e, m=m: e.dma_start(
                out=hfull[:, :, m * BLK:(m + 1) * BLK],
                in_=ag1_out[m * 128:(m + 1) * 128, :].rearrange("p (k t) -> p k t", k=8)),
                [d_ag1o], [d_hf], dma=True)
        for i_ in range(4):
            P.add("dve", lambda e, i_=i_: e.memset(AV(40 + 4 * i_, [128, 2048], BF16), 0.0), [], [d_ks])
        for i_ in range(16):
            P.add("pool", lambda e, i_=i_: e.memset(AV(56 + 2 * i_, [128, 1024], BF16), 0.0), [], [d_v64])
        if stop == "S1":
            return
        for c4 in range(4):
            s = state["gu"] % NGU
            state["gu"] += 1
            wv = gu_s[s][:, 0, :, :]
            P.add("pool", lambda e, c4=c4, wv=wv: e.dma_start(out=wv, in_=na_wqk_s[c4]), [], [d_gu[s]], dma=True)
            for tb in range(4):
                pb = 6 + (pj % 2)
                pj += 1
                for k in range(8):
                    P.mm(ps[pb], wv[:, k, :], hfull[:, k, tb * BLK:(tb + 1) * BLK], k == 0, k == 7,
                         [d_gu[s], d_hf], [d_ps[pb]])
                if c4 < 2:
                    P.add("act", lambda e, c4=c4, tb=tb, pb=pb: e.activation(
                        out=qT_s[:, c4, tb * BLK:(tb + 1) * BLK], in_=ps[pb], func=AF.Identity, scale=0.125),
                        [d_ps[pb]], [d_qs])
                else:
                    P.add("act", lambda e, c4=c4, tb=tb, pb=pb: e.activation(
                        out=kTpad[0:64, c4 - 2, 0, tb * BLK:(tb + 1) * BLK], in_=ps[pb][0:64, :], func=AF.Identity),
                        [d_ps[pb]], [d_ks])
                    P.add("act", lambda e, c4=c4, tb=tb, pb=pb: e.activation(
                        out=kTpad[64:128, c4 - 2, 1, tb * BLK:(tb + 1) * BLK], in_=ps[pb][64:128, :],
                        func=AF.Identity), [d_ps[pb]], [d_ks])
        s = state["gu"] % NGU
        state["gu"] += 1
        wvs = gu_s[s][:].rearrange("p a b c -> p (a b c)").rearrange("p (k n) -> p k n", k=8)
        P.add("pool", lambda e: e.dma_start(out=wvs, in_=na_wv_s), [], [d_gu[s]], dma=True)
        for rp in range(16):
            pb = 6 + (pj % 2)
            pj += 1
            for r2 in range(2):
                rho = rp * 2 + r2
                for k in range(8):
                    P.mm(ps[pb][0:64, r2 * 256:(r2 + 1) * 256], hfull[:, k, rho * 64:(rho + 1) * 64], wvs[:, k, :],
                         k == 0, k == 7, [d_gu[s], d_hf], [d_ps[pb]])
            for e_ in range(2):
                src = ps[pb][0:64, :].rearrange("p (r c e d) -> p r c e d", r=2, c=2, e=2)[:, :, :, e_, :]
                P.add("act", lambda e, rp=rp, e_=e_, src=src: e.activation(
                    out=v64[0:64, rp * 2:rp * 2 + 2, :, e_, e_ * 64:(e_ + 1) * 64], in_=src, func=AF.Identity),
                    [d_ps[pb]], [d_v64])
        if stop == "S2":
            return
        for dd in (d_bias, d_kc, d_vcb, d_vcp, d_ptl[0], d_ptl[1], d_ptc[0], d_ptc[1], d_rcs[0], d_rcs[1], d_ots):
            dd.rs = users_of([d_hf])
            dd.w = None
        P.add("dve", lambda e: e.memset(AV(0, [128, 2048], BF16), 0.0), [], [d_bias])
        P.add("dve", lambda e: e.memset(AV(4, [128, 2048], BF16), 0.0), [], [d_bias])
        P.add("pool", lambda e: e.dma_start(out=biasT[0:64], in_=na_bias), [d_bias], [d_bias], dma=True)
        P.add("dve", lambda e: e.memset(AV(8, [128, 1024], BF16), 0.0), [], [d_kc])
        for hh in range(2):
            P.add("pool", lambda e, hh=hh: e.dma_start(out=kcpad[hh * 64:(hh + 1) * 64, :, hh, :],
                                                       in_=na_kc[hh * 64:(hh + 1) * 64]),
                  [d_kc], [d_kc], dma=True)
        P.add("pool", lambda e: e.dma_start(out=vcb, in_=na_vc), [], [d_vcb], dma=True)
        P.add("dve", lambda e: e.memset(AV(11, [128, 1024], BF16), 0.0), [], [d_vcp])
        for e_ in range(2):
            for c in range(2):
                P.add("dve", lambda e, e_=e_, c=c: e.tensor_copy(
                    out=Vcpad[:, :, c, e_, e_ * 64:(e_ + 1) * 64], in_=vcb[:, :, 2 * c + e_, :]),
                    [d_vcb, d_vcp], [d_vcp])
        for i in range(2):
            P.add("dve", lambda e, i=i: e.memset(AV(13 + 2 * i, [128, 1024], BF16), 0.0), [], [d_ptl[i]])
        it = 0
        for c in range(2):
            for R in range(32):
                k0 = min(max(R - 4, 0), 24)
                bi = it % 2
                S2 = ps2[bi]
                dS = [d_ps[bi * 2], d_ps[bi * 2 + 1]]
                for hh in range(2):
                    for i in range(8):
                        rho = k0 + i
                        dr = rho - R
                        o_ap = S2[0:64, hh * 512 + i * 64: hh * 512 + (i + 1) * 64]
                        P.mm(o_ap, kTpad[:, c, hh, rho * 64:(rho + 1) * 64],
                             qT_s[:, c, R * 64:(R + 1) * 64], True, False,
                             [d_ks, d_qs], [dS[hh]])
                        P.mm(o_ap, ident_b[:, 0:64], biasT[:, dr + 7, 2 * c + hh, :], False, True,
                             [d_ones, d_bias], [dS[hh]])
                pc = 6 + bi
                for hh in range(2):
                    for sb in range(2):
                        i4 = hh * 2 + sb
                        P.mm(ps[pc][:, i4 * 64:(i4 + 1) * 64],
                             kcpad[:, c, hh, sb * 128:(sb + 1) * 128],
                             qT_s[:, c, R * 64:(R + 1) * 64], True, True,
                             [d_kc, d_qs], [d_ps[pc]])
                ptl, ptc = PTl[bi], PTc[bi]
                P.add("act", lambda e, S2=S2, ptl=ptl: e.activation(
                    out=ptl[0:64].rearrange("p a b c -> p (a b c)"), in_=S2[0:64, :], func=AF.Exp), dS, [d_ptl[bi]])
                P.add("act", lambda e, pc=pc, ptc=ptc: e.activation(
                    out=ptc.rearrange("p a b c -> p (a b c)"), in_=ps[pc][:, 0:256], func=AF.Exp),
                    [d_ps[pc]], [d_ptc[bi]])
                pb = 4 + bi
                for which in range(2):
                    n = 0
                    o_ap = ps[pb][:, which * 64:(which + 1) * 64]
                    for hh in range(2):
                        for i in range(8):
                            rho = k0 + i
                            lhs = v64[:, rho, c, hh, :] if which == 0 else onespad[:, hh, :]
                            P.mm(o_ap, lhs, ptl[:, hh, i, :], n == 0, False,
                                 [d_v64 if which == 0 else d_ones, d_ptl[bi]], [d_ps[pb]])
                            n += 1
                    for hh in range(2):
                        for sb in range(2):
                            lhs = Vcpad[:, sb, c, hh, :] if which == 0 else onespad[:, hh, :]
                            P.mm(o_ap, lhs, ptc[:, hh, sb, :], False, (hh == 1 and sb == 1),
                                 [d_vcp if which == 0 else d_ones, d_ptc[bi]], [d_ps[pb]])
                rc = rcs[bi]
                P.add("dve", lambda e, rc=rc, pb=pb: e.reciprocal(out=rc, in_=ps[pb][:, 64:128]),
                      [d_ps[pb]], [d_rcs[bi]])
                P.add("dve", lambda e, rc=rc, pb=pb, c=c, R=R: e.tensor_tensor(
                    out=OT_s[:, c, R * 64:(R + 1) * 64], in0=ps[pb][:, 0:64], in1=rc, op=ALU.mult),
                    [d_ps[pb], d_rcs[bi]], [d_ots])
                it += 1
        if stop == "S3":
            return
        d_ag2i, d_ag2o = Dep("ag2i"), Dep("ag2o")
        allgather(OT_s, [d_ots], ag2_in.rearrange("p (c t) -> p c t", c=2), ag2_in, ag2_out, d_ag2i, d_ag2o)
        cand = AV(28, [128, 4, 8, BLK], BF16)
        mixS = AV(60, [128, 8, BLK], BF16)
        d_cand, d_mixs = Dep("cand"), Dep("mixS")
        d_cand.rs = users_of([d_qs, d_ks, d_v64])
        d_mixs.rs = users_of([d_v64])
        for m in range(4):
            for c in range(2):
                P.add("sp", lambda e, m=m, c=c: e.dma_start(
                    out=cand[:, :, 2 * m + c, :],
                    in_=ag2_out[m * 128:(m + 1) * 128, c * 2048:(c + 1) * 2048].rearrange("p (r t) -> p r t", r=4)),
                    [d_ag2o], [d_cand], dma=True)
        for rr in range(4):
            if rr == 0:
                P.add("dve", lambda e: e.tensor_scalar(
                    out=mixS.rearrange("p a b -> p (a b)"), in0=cand[:, 0].rearrange("p a b -> p (a b)"),
                    scalar1=oh[:, 0:1], scalar2=None, op0=ALU.mult), [d_cand, d_oh], [d_mixs])
            else:
                P.add("dve", lambda e, rr=rr: e.scalar_tensor_tensor(
                    out=mixS.rearrange("p a b -> p (a b)"), in0=cand[:, rr].rearrange("p a b -> p (a b)"),
                    scalar=oh[:, rr:rr + 1], in1=mixS.rearrange("p a b -> p (a b)"), op0=ALU.mult, op1=ALU.add),
                    [d_cand, d_oh, d_mixs], [d_mixs])
        out_proj(l, na_wo, mixS, [d_mixs], 1)
        tail = []
        for o in (d_cand, d_mixs, d_ots, d_v64, d_ks, d_qs, d_bias, d_kc, d_vcb, d_vcp, d_ptl[0], d_ptl[1],
                  d_ptc[0], d_ptc[1], d_rcs[0], d_rcs[1]):
            tail += list(o.rs)
            if o.w is not None:
                tail.append(o.w)
        for b in range(2):
            d_a[b].rs = list(d_a[b].rs) + tail

    def ab_stage(cfg):
        nh, T, seqs, src, d_src = cfg["nh"], cfg["T"], cfg["seqs"], cfg["src"], cfg["d_src"]
        wf, wtA, wtB, gp_d, cw_d, ngm_d, ngd_d = cfg["wf"], cfg["wtA"], cfg["wtB"], cfg["gp"], cfg["cw"], cfg["ngm"], cfg["ngd"]
        d_mix = cfg["d_mix"]
        NT, NCHK, n2 = T // 128, T // 64, 2 * nh
        qmT = AV(0, [128, nh, T], BF16)
        kmT = AV(4, [128, nh, T], BF16)
        qnT = AV(8, [128, nh, T], BF16)
        knT = AV(12, [128, nh, T], BF16)
        vdT = AV(16, [128, nh, T], BF16)
        vaug = AV(20, [128, NT * nh, 130], BF16)
        kmtok = AV(24.5, [128, NT * nh, 64], BF16)
        gates = AV(26.5, [128, NT, 4, n2], F32)
        NQ = 13
        gt = AV(27, [128, NT, NQ, n2], F32)
        om = AV(29, [64, NCHK * nh, 128], BF16, parts=64)
        zd = AV(37, [64, NCHK * nh, 128], BF16, parts=64)
        hacc = AV(45, [64, NCHK * nh, 128], BF16, parts=64)
        oacc = AV(53, [64, NCHK * nh, 128], BF16, parts=64)
        XL = T + 4 * len(seqs)
        xin = AV(45, [128, XL], BF16)
        caccs = [AV(49.5, [128, 512], F32), AV(54, [128, 512], F32)]
        csqs = [AV(49.5, [128, 512], BF16), AV(54, [128, 512], BF16)]
        knT32 = sq[:].rearrange("p a b -> p (a b)").bitcast(F32).rearrange("p (h t) -> p h t", h=nh)
        ctmps = [AV(51.5, [128, 512], F32), AV(56, [128, 512], F32)]
        Caug = AV(78, [128, 2, nh, 130], F32)
        Cbf = AV(82.5, [128, 2, nh, 130], BF16)
        Sst = AV(85, [128, 2, nh, 128], F32)
        Sbf = AV(89, [128, 2, nh, 128], BF16)
        hflat = hT[:].rearrange("p a b -> p (a b)")

        def AVH(off_kb, shape, dt, parts=128):
            n = 1
            for s_ in shape[1:]:
                n *= s_
            nbytes = n * (2 if dt == BF16 else 4)
            e0 = int(off_kb * 512)
            ne = (nbytes + 1) // 2
            assert e0 + ne <= 8 * NTOK
            v = hflat[0:parts, e0:e0 + ne]
            if dt == F32:
                v = v.bitcast(F32)
            if len(shape) == 2:
                return v
            names = "abcdef"[:len(shape) - 1]
            pat = "p (" + " ".join(names) + ") -> p " + " ".join(names)
            kw_ = {names[i]: shape[1 + i] for i in range(len(shape) - 1)}
            return v.rearrange(pat, **kw_)

        def alloc_set(sn):
            o_ = [91.0 if sn == 0 else 0.0]
            AVx = AV if sn == 0 else AVH

            def TMP(kb, shape, dt, parts=128):
                v = AVx(o_[0], shape, dt, parts)
                o_[0] += kb
                return v
            T = {}
            for nm in ("rhs1", "rhs2", "rhs3", "rhsD", "EB", "DT", "EG", "FA", "FB", "FC"):
                T[nm] = TMP(0.5, [128, 128], F32)
            for nm in ("swT", "qbT", "QKdT", "wT", "qgT", "vnew"):
                T[nm] = TMP(0.25, [128, 128], BF16)
            T["kw"] = TMP(0.25, [128, 2, 64], BF16)
            T["wsc"] = TMP(0.25, [128, 2], F32)
            T["Pm"] = [TMP(0.5, [128, 128], F32) for _ in range(2)]
            T["PTm"] = [TMP(0.5, [128, 128], F32) for _ in range(2)]
            T["X"] = TMP(1.0, [128, 256], F32)
            T["kd"] = TMP(0.5, [128, 2, 128], BF16)
            T["kdcc"] = TMP(0.25, [128, 2], F32)
            T["dn"] = TMP(0.25, [128, 2], F32)
            T["_end"] = o_[0]
            return T
        TS = [alloc_set(0), alloc_set(1)]
        o_ = [TS[0]["_end"]]

        def TMP(kb, shape, dt, parts=128):
            v = AV(o_[0], shape, dt, parts)
            o_[0] += kb
            return v
        assert o_[0] <= ARENA_KB, o_[0]
        assert TS[1]["_end"] <= 16.0
        gpt = P.sb([128, 4, NT, n2], F32, f"gpt{nh}")
        expA = P.sb([128, NT, n2], F32, f"expA{nh}")
        cwt = P.sb([128, 3 * nh, 5], F32, f"cwt{nh}")
        ngm = P.sb([64, nh * 128], F32, f"ngm{nh}")
        ngd = P.sb([64, 128], F32, f"ngd{nh}")
        D_ = {}

        ALIAS = {"hacc": ["xin", "cacc0", "ctmp0", "cacc1", "ctmp1"], "oacc": ["xin", "cacc0", "ctmp0", "cacc1", "ctmp1"]}

        def dp(name):
            if name not in D_:
                D_[name] = Dep(f"ab{nh}_{name}")
                rs = list(cfg["prev_users"]) + list(cfg["extra_users"]())
                if name.endswith("_1"):
                    for b_ in range(2):
                        for j_ in range(8):
                            rs += list(d_h[b_][j_].rs)
                            if d_h[b_][j_].w is not None:
                                rs.append(d_h[b_][j_].w)
                for o in ALIAS.get(name, []):
                    if o in D_:
                        rs += list(D_[o].rs)
                        if D_[o].w is not None:
                            rs.append(D_[o].w)
                D_[name].rs = rs
            return D_[name]
        d_par = dp("par")
        d_par.rs = []
        P.add("sp", lambda e: e.dma_start(out=gpt[:], in_=gp_d), [], [d_par], dma=True)
        d_cw = Dep(f"cw{nh}")
        P.add("sp", lambda e: e.dma_start(out=cwt[:], in_=cw_d), [], [d_cw], dma=True)
        d_ngm = Dep(f"ngm{nh}")
        P.add("sp", lambda e: e.dma_start(out=ngm[:], in_=ngm_d), [], [d_ngm], dma=True)
        d_ngd = Dep(f"ngd{nh}")
        P.add("sp", lambda e: e.dma_start(out=ngd[:], in_=ngd_d), [], [d_ngd], dma=True)
        d_expA = Dep(f"expA{nh}")
        P.add("act", lambda e: e.activation(out=expA[:], in_=gpt[:, 3], func=AF.Exp), [d_par], [d_expA])

        P.add("dve", lambda e: e.memset(xin, 0.0), [], [dp("xin")])
        segs = []
        xo = 0
        for (t0, nt) in seqs:
            L = nt * 128
            for a in range(0, L, 512):
                ln = min(512, L - a)
                segs.append((xo + a, t0 * 128 + a, ln))
            xo += L + 4
        seq_x0 = {}
        xo = 0
        for (t0, nt) in seqs:
            seq_x0[t0] = xo
            xo += nt * 128 + 4
        pj = [0]
        segc = [0]

        def xin_pos(tok):
            for (t0, nt) in seqs:
                if t0 * 128 <= tok < (t0 + nt) * 128:
                    return seq_x0[t0] + 2 + (tok - t0 * 128)
            raise ValueError
        for ci in range(5 * nh):
            kind, h = ci // nh, ci % nh
            s = state["gu"] % NGU
            state["gu"] += 1
            wv = gu_s[s][:, 0, :, :]
            P.add("pool", lambda e, ci=ci, wv=wv: e.dma_start(out=wv, in_=wf[ci]), [], [d_gu[s]], dma=True)
            for p0 in range(0, T, 512):
                pb = 6 + (pj[0] % 2)
                pj[0] += 1
                for k in range(8):
                    P.mm(ps[pb], wv[:, k, :], src(k, p0, p0 + 512), k == 0, k == 7, [d_gu[s]] + d_src, [d_ps[pb]])
                if kind == 0:
                    P.add("act", lambda e, h=h, p0=p0, pb=pb: e.activation(
                        out=qmT[:, h, p0:p0 + 512], in_=ps[pb], func=AF.Identity, scale=0.125), [d_ps[pb]], [dp("qmT")])
                elif kind == 1:
                    P.add("act", lambda e, h=h, p0=p0, pb=pb: e.activation(
                        out=kmT[:, h, p0:p0 + 512], in_=ps[pb], func=AF.Identity), [d_ps[pb]], [dp("kmT")])
                else:
                    a = p0
                    while a < p0 + 512:
                        for (t0, nt) in seqs:
                            if t0 * 128 <= a < (t0 + nt) * 128:
                                b_ = min(p0 + 512, (t0 + nt) * 128)
                        xp = xin_pos(a)
                        P.add("act", lambda e, a=a, b_=b_, xp=xp, pb=pb, p0=p0: e.activation(
                            out=xin[:, xp:xp + (b_ - a)], in_=ps[pb][:, a - p0:b_ - p0], func=AF.Identity),
                            [d_ps[pb]], [dp("xin")])
                        a = b_
            if kind >= 2:
                cidx = (kind - 2) * nh + h
                for (xo_, tok0, ln) in segs:
                    bsel = segc[0] % 2
                    segc[0] += 1
                    cacc, csq, ctmp = caccs[bsel], csqs[bsel], ctmps[bsel]
                    CA, CT = "cacc%d" % bsel, "ctmp%d" % bsel
                    for j in range(5):
                        sl = xin[:, xo_ + j:xo_ + j + ln]
                        if j == 0:
                            P.add("dve", lambda e, cacc=cacc, csq=csq, ctmp=ctmp, sl=sl, ln=ln, cidx=cidx: e.tensor_scalar(
                                out=cacc[:, 0:ln], in0=sl, scalar1=cwt[:, cidx, 0:1], scalar2=None, op0=ALU.mult),
                                [dp("xin"), d_cw], [dp(CA)])
                        else:
                            P.add("dve", lambda e, cacc=cacc, csq=csq, ctmp=ctmp, sl=sl, ln=ln, cidx=cidx, j=j: e.scalar_tensor_tensor(
                                out=cacc[:, 0:ln], in0=sl, scalar=cwt[:, cidx, j:j + 1], in1=cacc[:, 0:ln],
                                op0=ALU.mult, op1=ALU.add), [dp("xin"), d_cw, dp(CA)], [dp(CA)])
                    if kind == 4:
                        P.add("act", lambda e, cacc=cacc, csq=csq, ctmp=ctmp, h=h, tok0=tok0, ln=ln: e.activation(
                            out=vdT[:, h, tok0:tok0 + ln], in_=cacc[:, 0:ln], func=AF.Silu), [dp(CA)], [dp("vdT")])
                    else:
                        P.add("act", lambda e, cacc=cacc, csq=csq, ctmp=ctmp, ln=ln: e.activation(out=ctmp[:, 0:ln], in_=cacc[:, 0:ln], func=AF.Silu),
                              [dp(CA)], [dp(CT)])
                        P.add("act", lambda e, cacc=cacc, csq=csq, ctmp=ctmp, ln=ln: e.activation(out=csq[:, 0:ln], in_=ctmp[:, 0:ln], func=AF.Square),
                              [dp(CT), dp(CA)], [dp(CA)])
                        P.mm(ps[6][:, 0:ln], ones_bf[:], csq[:, 0:ln], True, True, [d_ones, dp(CA)], [d_ps[6]])
                        P.add("act", lambda e, cacc=cacc, csq=csq, ctmp=ctmp, ln=ln: e.activation(out=rstd[:, 0:ln], in_=ps[6][:, 0:ln], func=AF.Ln,
                                                                   bias=epsb[:, 1:2], scale=1.0), [d_ps[6], d_ones], [d_rstd])
                        P.add("act", lambda e, cacc=cacc, csq=csq, ctmp=ctmp, ln=ln: e.activation(out=rstd[:, 0:ln], in_=rstd[:, 0:ln], func=AF.Exp,
                                                                   scale=-0.5), [d_rstd], [d_rstd])
                        dst = qnT if kind == 2 else knT
                        scl = float(128 ** -0.5) if kind == 2 else 1.0
                        P.add("dve", lambda e, cacc=cacc, csq=csq, ctmp=ctmp, dst=dst, h=h, tok0=tok0, ln=ln, scl=scl: e.scalar_tensor_tensor(
                            out=dst[:, h, tok0:tok0 + ln], in0=ctmp[:, 0:ln], scalar=scl, in1=rstd[:, 0:ln],
                            op0=ALU.mult, op1=ALU.mult), [dp(CT), d_rstd], [dp("qnT" if kind == 2 else "knT")])
                        if kind == 3:
                            P.add("dve", lambda e, cacc=cacc, csq=csq, ctmp=ctmp, h=h, tok0=tok0, ln=ln: e.tensor_tensor(
                                out=knT32[:, h, tok0:tok0 + ln], in0=ctmp[:, 0:ln], in1=rstd[:, 0:ln], op=ALU.mult),
                                [dp(CT), d_rstd], [d_sq])

        P.add("dve", lambda e: e.memset(vaug[:, :, 128:130], 1.0), [], [dp("vaug")])
        nA = (128 * nh + 64 * nh + 4 * n2 + 255) // 256
        for blk in range(nA):
            s = state["gu"] % NGU
            state["gu"] += 1
            wv = gu_s[s][:].rearrange("p a b c -> p (a b c)").rearrange("p (k n) -> p k n", k=8)
            P.add("pool", lambda e, blk=blk, wv=wv: e.dma_start(out=wv, in_=wtA[blk]), [], [d_gu[s]], dma=True)
            for tl_ in range(NT):
                pb = 6 + (pj[0] % 2)
                pj[0] += 1
                for k in range(8):
                    P.mm(ps[pb][:, 0:256], src(k, tl_ * 128, (tl_ + 1) * 128), wv[:, k, :], k == 0, k == 7,
                         [d_gu[s]] + d_src, [d_ps[pb]])
                c0 = blk * 256
                for h in range(nh):
                    a = h * 128
                    if c0 <= a < c0 + 256:
                        P.add("act", lambda e, tl_=tl_, h=h, a=a, c0=c0, pb=pb: e.activation(
                            out=vaug[:, tl_ * nh + h, 0:128], in_=ps[pb][:, a - c0:a - c0 + 128], func=AF.Identity),
                            [d_ps[pb]], [dp("vaug")])
                for h in range(nh):
                    a = 128 * nh + h * 64
                    if c0 <= a < c0 + 256:
                        P.add("dve", lambda e, tl_=tl_, h=h, a=a, c0=c0, pb=pb: e.tensor_copy(
                            out=kmtok[:, tl_ * nh + h, :], in_=ps[pb][:, a - c0:a - c0 + 64]),
                            [d_ps[pb]], [dp("kmtok")])
                a = 128 * nh + 64 * nh
                if c0 <= a < c0 + 256:
                    P.add("dve", lambda e, tl_=tl_, a=a, c0=c0, pb=pb: e.tensor_copy(
                        out=gates[:, tl_].rearrange("p a b -> p (a b)"), in_=ps[pb][:, a - c0:a - c0 + 4 * n2]),
                        [d_ps[pb]], [dp("gates")])
        nB = (256 * nh) // 256
        for blk in range(nB):
            s = state["gu"] % NGU
            state["gu"] += 1
            wv = gu_s[s][:].rearrange("p a b c -> p (a b c)").rearrange("p (k n) -> p k n", k=8)
            P.add("pool", lambda e, blk=blk, wv=wv: e.dma_start(out=wv, in_=wtB[blk]), [], [d_gu[s]], dma=True)
            for ch in range(NCHK):
                pb = 6 + (pj[0] % 2)
                pj[0] += 1
                for k in range(8):
                    P.mm(ps[pb][0:64, 0:256], src(k, ch * 64, (ch + 1) * 64), wv[:, k, :], k == 0, k == 7,
                         [d_gu[s]] + d_src, [d_ps[pb]])
                for half in range(2):
                    gcol = blk * 256 + half * 128
                    isz = gcol >= nh * 128
                    h = (gcol - (nh * 128 if isz else 0)) // 128
                    dst = zd if isz else om
                    P.add("act", lambda e, ch=ch, h=h, half=half, dst=dst, pb=pb: e.activation(
                        out=dst[:, ch * nh + h, :], in_=ps[pb][0:64, half * 128:(half + 1) * 128], func=AF.Identity),
                        [d_ps[pb]], [dp("zd" if isz else "om")])

        gtmp = P.sb([128, 6, NT, n2], F32, f"gtmp{nh}")
        cin = P.sb([128, 2, NT, 2, nh], F32, f"cin{nh}")
        d_g = dp("gt")
        d_gtmp = Dep(f"gtmp{nh}")
        G4 = lambda w: gates[:, :, w, :]
        Q = lambda q: gt[:, :, q, :]
        TM = lambda i: gtmp[:, i]
        ONE = epsb[:, 2:3]
        P.add("dve", lambda e: e.tensor_tensor(out=TM(0), in0=G4(0), in1=gpt[:, 0], op=ALU.add),
              [dp("gates"), d_par], [d_gtmp])
        P.add("dve", lambda e: e.tensor_tensor(out=TM(1), in0=G4(1), in1=gpt[:, 1], op=ALU.add),
              [dp("gates"), d_par], [d_gtmp])
        P.add("act", lambda e: e.activation(out=TM(1), in_=TM(1), func=AF.Exp, scale=-1.0), [d_gtmp], [d_gtmp])
        P.add("act", lambda e: e.activation(out=TM(1), in_=TM(1), func=AF.Ln, bias=ONE, scale=1.0),
              [d_gtmp, d_ones], [d_gtmp])
        P.add("dve", lambda e: e.tensor_scalar(out=Q(1), in0=TM(1), scalar1=-1.0, scalar2=None, op0=ALU.mult),
              [d_gtmp], [d_g])
        P.add("act", lambda e: e.activation(out=TM(2), in_=G4(2), func=AF.Exp, scale=-1.0), [dp("gates")], [d_gtmp])
        P.add("act", lambda e: e.activation(out=TM(2), in_=TM(2), func=AF.Ln, bias=ONE, scale=1.0),
              [d_gtmp, d_ones], [d_gtmp])
        P.add("dve", lambda e: e.tensor_scalar(out=Q(5), in0=TM(2), scalar1=-1.0, scalar2=None, op0=ALU.mult),
              [d_gtmp], [d_g])
        P.add("dve", lambda e: e.tensor_tensor(out=TM(3), in0=G4(3), in1=gpt[:, 2], op=ALU.add),
              [dp("gates"), d_par], [d_gtmp])
        P.add("act", lambda e: e.activation(out=TM(3), in_=TM(3), func=AF.Exp), [d_gtmp], [d_gtmp])
        P.add("act", lambda e: e.activation(out=TM(3), in_=TM(3), func=AF.Ln, bias=ONE, scale=1.0),
              [d_gtmp, d_ones], [d_gtmp])
        P.add("dve", lambda e: e.tensor_tensor(out=Q(4), in0=TM(3), in1=expA[:], op=ALU.mult),
              [d_gtmp, d_expA], [d_g])
        P.add("dve", lambda e: e.tensor_scalar(out=Q(3), in0=Q(4), scalar1=-1.0, scalar2=None, op0=ALU.mult),
              [d_g], [d_g])
        for dr in range(2):
            P.add("dve", lambda e, dr=dr: e.tensor_copy(out=cin[:, dr, :, 0, :], in_=Q(1)[:, :, dr * nh:(dr + 1) * nh]),
                  [d_g, d_gtmp], [d_gtmp])
            P.add("dve", lambda e, dr=dr: e.tensor_copy(out=cin[:, dr, :, 1, :], in_=Q(3)[:, :, dr * nh:(dr + 1) * nh]),
                  [d_g, d_gtmp], [d_gtmp])
        pb = 6 + (pj[0] % 2)
        pj[0] += 1
        W_ = NT * n2
        for dr in range(2):
            P.mm(ps[pb][:, dr * W_:(dr + 1) * W_], mk[:, dr, :], cin[:, dr].rearrange("p t a b -> p (t a b)"), True, True,
                 [d_msk, d_gtmp], [d_ps[pb]])
        for dr in range(2):
            P.mm(ps[pb][:, 256 + dr * W_:256 + (dr + 1) * W_], mk[:, 2, :], cin[:, dr].rearrange("p t a b -> p (t a b)"),
                 True, True, [d_msk, d_gtmp], [d_ps[pb]])
        cums = lambda dr, w: ps[pb][:, dr * W_:(dr + 1) * W_].rearrange("p (t a b) -> p t a b", t=NT, a=2)[:, :, w, :]
        tots = lambda dr, w: ps[pb][:, 256 + dr * W_:256 + (dr + 1) * W_].rearrange("p (t a b) -> p t a b", t=NT, a=2)[:, :, w, :]
        for dr in range(2):
            hs_ = slice(dr * nh, (dr + 1) * nh)
            P.add("dve", lambda e, dr=dr, hs_=hs_: e.tensor_tensor(
                out=Q(0)[:, :, hs_], in0=TM(0)[:, :, hs_], in1=cums(dr, 0), op=ALU.subtract), [d_gtmp, d_ps[pb]], [d_g])
            P.add("dve", lambda e, dr=dr, hs_=hs_: e.tensor_tensor(
                out=Q(11)[:, :, hs_], in0=Q(0)[:, :, hs_], in1=tots(dr, 0), op=ALU.add), [d_g, d_ps[pb]], [d_g])
            P.add("dve", lambda e, dr=dr, hs_=hs_: e.tensor_copy(
                out=Q(12)[:, :, hs_], in_=tots(dr, 0)), [d_ps[pb]], [d_g])
            P.add("dve", lambda e, dr=dr, hs_=hs_: e.tensor_scalar(
                out=Q(6)[:, :, hs_], in0=cums(dr, 1), scalar1=-1.0, scalar2=None, op0=ALU.mult), [d_ps[pb]], [d_g])
            P.add("dve", lambda e, dr=dr, hs_=hs_: e.tensor_tensor(
                out=Q(7)[:, :, hs_], in0=Q(5)[:, :, hs_], in1=cums(dr, 1), op=ALU.add), [d_g, d_ps[pb]], [d_g])
            P.add("dve", lambda e, dr=dr, hs_=hs_: e.tensor_tensor(
                out=Q(10)[:, :, hs_], in0=Q(6)[:, :, hs_], in1=tots(dr, 1), op=ALU.add), [d_g, d_ps[pb]], [d_g])
        P.add("act", lambda e: e.activation(out=Q(2), in_=Q(11), func=AF.Exp), [d_g], [d_g])
        P.add("act", lambda e: e.activation(out=Q(8), in_=Q(5), func=AF.Exp), [d_g], [d_g])
        P.add("act", lambda e: e.activation(out=Q(9), in_=Q(7), func=AF.Exp), [d_g], [d_g])
        P.add("act", lambda e: e.activation(out=Q(10), in_=Q(10), func=AF.Exp), [d_g], [d_g])

        def col(tl_, q, j):
            return gt[:, tl_, q, j:j + 1]
        NEGns = {0: 3, 1: 5}
        NEGst = {0: 4, 1: 6}
        NEGts = {0: 6, 1: 4}
        bcnt = [0]

        def mlstm_unit(Q, dr, h, tl_):
            sn = dr
            T = TS[sn]
            rhs1, rhs2, rhs3, EB, DT, EG, FA, FB, FC = (T[k_] for k_ in ("rhs1", "rhs2", "rhs3", "EB", "DT", "EG", "FA", "FB", "FC"))
            swT, qbT, QKdT, wT, qgT, vnew = (T[k_] for k_ in ("swT", "qbT", "QKdT", "wT", "qgT", "vnew"))
            kw, wsc, Pm, PTm, X, kd, kdcc, dn = (T[k_] for k_ in ("kw", "wsc", "Pm", "PTm", "X", "kd", "kdcc", "dn"))
            j = dr * nh + h
            tsl = slice(tl_ * 128, (tl_ + 1) * 128)
            ba = 0
            A = ps[ba][:, dr * 256:(dr + 1) * 256]
            bb = 1 + dr
            Bk = ps[bb]
            Q.add("dve", lambda e: e.tensor_scalar(out=rhs1, in0=mk[:, dr, :], scalar1=col(tl_, 1, j), scalar2=None,
                                                   op0=ALU.mult), [d_msk, d_g], [dp("rhs1_%d" % sn)])
            Q.mm(A[:, 0:128], ones_f[:], rhs1, True, True, [d_ones, dp("rhs1_%d" % sn)], [d_ps[ba]])
            Q.mm(A[:, 128:256], ones_f[:], rhs1, True, False, [d_ones, dp("rhs1_%d" % sn)], [d_ps[ba]])
            Q.mm(A[:, 128:256], ident_f[:], mk[:, NEGns[dr], :], False, True, [d_id, d_msk], [d_ps[ba]])
            Q.mm(Bk[:, 0:128], kmT[:, h, tsl], qmT[:, h, tsl], True, True, [dp("kmT"), dp("qmT")], [d_ps[bb]])
            Q.add("act", lambda e: e.activation(out=EB, in_=A[:, 0:128], func=AF.Exp), [d_ps[ba]], [dp("EB_%d" % sn)])
            Q.add("act", lambda e: e.activation(out=DT, in_=A[:, 128:256], func=AF.Exp, bias=col(tl_, 0, j), scale=1.0),
                  [d_ps[ba], d_g], [dp("DT_%d" % sn)])
            Q.add("dve", lambda e: e.tensor_tensor(out=swT, in0=Bk[:, 0:128], in1=DT, op=ALU.mult),
                  [d_ps[bb], dp("DT_%d" % sn)], [dp("swT_%d" % sn)])
            Q.add("dve", lambda e: e.tensor_tensor(out=qbT, in0=qmT[:, h, tsl], in1=EB, op=ALU.mult),
                  [dp("qmT"), dp("EB_%d" % sn)], [dp("qbT_%d" % sn)])
            Q.add("dve", lambda e: e.tensor_scalar(out=wsc, in0=chind[:], scalar1=col(tl_, 2, j), scalar2=None,
                                                   op0=ALU.mult), [d_msk, d_g], [dp("wsc_%d" % sn)])
            for c in range(2):
                Q.add("dve", lambda e, c=c: e.tensor_scalar(out=kw[:, c, :], in0=kmtok[:, tl_ * nh + h, :],
                                                            scalar1=wsc[:, c:c + 1], scalar2=None, op0=ALU.mult),
                      [dp("kmtok"), dp("wsc_%d" % sn)], [dp("kw_%d" % sn)])

        def mlstm_chunk(Q, dr, h, tl_, c, first):
            sn = dr
            T = TS[sn]
            rhs1, rhs2, rhs3, EB, DT, EG, FA, FB, FC = (T[k_] for k_ in ("rhs1", "rhs2", "rhs3", "EB", "DT", "EG", "FA", "FB", "FC"))
            swT, qbT, QKdT, wT, qgT, vnew = (T[k_] for k_ in ("swT", "qbT", "QKdT", "wT", "qgT", "vnew"))
            kw, wsc, Pm, PTm, X, kd, kdcc, dn = (T[k_] for k_ in ("kw", "wsc", "Pm", "PTm", "X", "kd", "kdcc", "dn"))
            tc = slice(c * 64, (c + 1) * 64)
            tl1 = c * 64 + 63 if dr == 0 else c * 64
            chg = tl_ * 2 + c
            bb = 1 + dr
            B = ps[bb]
            Q.mm(B[0:64, 128:257], swT[:, tc], vaug[:, tl_ * nh + h, 0:129], True, False, [dp("swT_%d" % sn), dp("vaug")], [d_ps[bb]])
            Q.mm(B[0:64, 128:257], qbT[:, tc], Cbf[:, dr, h, 0:129], False, True, [dp("qbT_%d" % sn), dp("Cbf%d" % dr)], [d_ps[bb]])
            Q.mm(B[0:64, 320:449], kw[:, c, :], vaug[:, tl_ * nh + h, 0:129], True, True, [dp("kw_%d" % sn), dp("vaug")], [d_ps[bb]])
            Q.add("dve", lambda e: e.tensor_scalar(out=dn[0:64, 1:2], in0=B[0:64, 256:257], scalar1=-1.0, scalar2=1.0,
                                                   op0=ALU.mult, op1=ALU.max), [d_ps[bb]], [dp("dn_%d" % sn)])
            Q.add("dve", lambda e: e.scalar_tensor_tensor(out=dn[0:64, 0:1], in0=B[0:64, 256:257], scalar=1.0,
                                                          in1=dn[0:64, 1:2], op0=ALU.max, op1=ALU.max),
                  [d_ps[bb], dp("dn_%d" % sn)], [dp("dn_%d" % sn)])
            Q.add("dve", lambda e: e.reciprocal(out=dn[0:64, 1:2], in_=dn[0:64, 0:1]), [dp("dn_%d" % sn)], [dp("dn_%d" % sn)])
            if first:
                Q.add("dve", lambda e: e.tensor_scalar(out=hacc[:, chg * nh + h, :], in0=B[0:64, 128:256],
                                                       scalar1=dn[0:64, 1:2], scalar2=None, op0=ALU.mult),
                      [d_ps[bb], dp("dn_%d" % sn)], [dp("hacc")])
            else:
                Q.add("dve", lambda e: e.scalar_tensor_tensor(out=hacc[:, chg * nh + h, :], in0=B[0:64, 128:256],
                                                              scalar=dn[0:64, 1:2], in1=hacc[:, chg * nh + h, :],
                                                              op0=ALU.mult, op1=ALU.add),
                      [d_ps[bb], dp("dn_%d" % sn), dp("hacc")], [dp("hacc")])
            Q.add("dve", lambda e: e.scalar_tensor_tensor(out=Caug[0:64, dr, h, 0:129], in0=Caug[0:64, dr, h, 0:129],
                                                          scalar=EB[0:64, tl1:tl1 + 1], in1=B[0:64, 320:449],
                                                          op0=ALU.mult, op1=ALU.add),
                  [dp("Caug%d" % dr), dp("EB_%d" % sn), d_ps[bb]], [dp("Caug%d" % dr)])
            Q.add("act", lambda e: e.activation(out=Cbf[0:64, dr, h, 0:129], in_=Caug[0:64, dr, h, 0:129],
                                                func=AF.Identity), [dp("Caug%d" % dr)], [dp("Cbf%d" % dr)])

        def delta_unit(Q, dr, h, tl_):
            sn = dr
            T = TS[sn]
            rhs1, rhs2, rhs3, EB, DT, EG, FA, FB, FC = (T[k_] for k_ in ("rhsD", "rhs2", "rhs3", "EB", "DT", "EG", "FA", "FB", "FC"))
            swT, qbT, QKdT, wT, qgT, vnew = (T[k_] for k_ in ("swT", "qbT", "QKdT", "wT", "qgT", "vnew"))
            kw, wsc, Pm, PTm, X, kd, kdcc, dn = (T[k_] for k_ in ("kw", "wsc", "Pm", "PTm", "X", "kd", "kdcc", "dn"))
            j = dr * nh + h
            tsl = slice(tl_ * 128, (tl_ + 1) * 128)
            ba = 3 if dr == 0 else 5
            A = ps[ba]
            Q.add("dve", lambda e: e.tensor_scalar(out=rhs1, in0=mk[:, dr, :], scalar1=col(tl_, 3, j), scalar2=None,
                                                   op0=ALU.mult), [d_msk, d_g], [dp("rhsD_%d" % sn)])
            Q.add("dve", lambda e: e.scalar_tensor_tensor(out=rhs2, in0=ident_f[:], scalar=col(tl_, 5, j), in1=rhs1,
                                                          op0=ALU.mult, op1=ALU.add), [d_id, d_g, dp("rhsD_%d" % sn)], [dp("rhs2_%d" % sn)])
            Q.add("dve", lambda e: e.tensor_scalar(out=rhs3, in0=mk[:, dr, :], scalar1=col(tl_, 4, j), scalar2=None,
                                                   op0=ALU.mult), [d_msk, d_g], [dp("rhs3_%d" % sn)])
            Q.mm(A[:, 0:128], ones_f[:], rhs1, True, True, [d_ones, dp("rhsD_%d" % sn)], [d_ps[ba]])
            Q.mm(A[:, 128:256], ones_f[:], rhs2, True, False, [d_ones, dp("rhs2_%d" % sn)], [d_ps[ba]])
            Q.mm(A[:, 128:256], ident_f[:], mk[:, NEGst[dr], :], False, True, [d_id, d_msk], [d_ps[ba]])
            Q.mm(A[:, 256:384], ones_f[:], rhs3, True, False, [d_ones, dp("rhs3_%d" % sn)], [d_ps[ba]])
            Q.mm(A[:, 256:384], ident_f[:], mk[:, NEGts[dr], :], False, True, [d_id, d_msk], [d_ps[ba]])
            Q.mm(A[:, 384:512], ones_f[:], rhs1, True, False, [d_ones, dp("rhsD_%d" % sn)], [d_ps[ba]])
            Q.mm(A[:, 384:512], ident_f[:], mk[:, NEGns[dr], :], False, True, [d_id, d_msk], [d_ps[ba]])
            Q.add("act", lambda e: e.activation(out=EG, in_=A[:, 0:128], func=AF.Exp), [d_ps[ba]], [dp("EG_%d" % sn)])
            Q.add("act", lambda e: e.activation(out=FA, in_=A[:, 128:256], func=AF.Exp, bias=col(tl_, 6, j), scale=1.0),
                  [d_ps[ba], d_g], [dp("FA_%d" % sn)])
            Q.add("act", lambda e: e.activation(out=FB, in_=A[:, 256:384], func=AF.Exp, bias=col(tl_, 7, j), scale=1.0),
                  [d_ps[ba], d_g], [dp("FB_%d" % sn)])
            Q.add("act", lambda e: e.activation(out=FC, in_=A[:, 384:512], func=AF.Exp, bias=col(tl_, 6, j), scale=1.0),
                  [d_ps[ba], d_g], [dp("FC_%d" % sn)])
            cb_ = 4 if dr == 0 else 6
            C_ = ps[cb_]
            Q.mm(C_[:, 0:128], knT32[:, h, tsl], knT32[:, h, tsl], True, True, [d_sq], [d_ps[cb_]])
            Q.mm(C_[:, 128:256], knT[:, h, tsl], qnT[:, h, tsl], True, True, [dp("knT"), dp("qnT")], [d_ps[cb_]])
            Q.mm(C_[:, 256:384], knT32[:, h, tsl], ident_f[:], True, True, [d_sq, d_id], [d_ps[cb_]])
            Q.mm(C_[:, 384:512], vdT[:, h, tsl], ident_b[:], True, True, [dp("vdT"), d_ones], [d_ps[cb_]])
            Q.add("dve", lambda e: e.tensor_tensor(out=PTm[0], in0=C_[:, 0:128], in1=FA, op=ALU.mult),
                  [d_ps[cb_], dp("FA_%d" % sn)], [dp("PT0_%d" % sn)])
            Q.add("dve", lambda e: e.tensor_tensor(out=Pm[0], in0=C_[:, 0:128], in1=FB, op=ALU.mult),
                  [d_ps[cb_], dp("FB_%d" % sn)], [dp("P0_%d" % sn)])
            Q.add("dve", lambda e: e.tensor_tensor(out=QKdT, in0=C_[:, 128:256], in1=FC, op=ALU.mult),
                  [d_ps[cb_], dp("FC_%d" % sn)], [dp("QKdT_%d" % sn)])
            Q.add("dve", lambda e: e.tensor_scalar(out=X[:, 128:256], in0=C_[:, 256:384], scalar1=col(tl_, 9, j),
                                                   scalar2=None, op0=ALU.mult), [d_ps[cb_], d_g], [dp("X_%d" % sn)])
            Q.add("dve", lambda e: e.tensor_scalar(out=X[:, 0:128], in0=C_[:, 384:512], scalar1=col(tl_, 8, j),
                                                   scalar2=None, op0=ALU.mult), [d_ps[cb_], d_g], [dp("X_%d" % sn)])
            Q.add("dve", lambda e: e.tensor_scalar(out=kdcc, in0=chind[:], scalar1=col(tl_, 10, j), scalar2=None,
                                                   op0=ALU.mult), [d_msk, d_g], [dp("kdcc_%d" % sn)])
            for c in range(2):
                Q.add("dve", lambda e, c=c: e.tensor_scalar(out=kd[:, c, :], in0=C_[:, 256:384],
                                                            scalar1=kdcc[:, c:c + 1], scalar2=None, op0=ALU.mult),
                      [d_ps[cb_], dp("kdcc_%d" % sn)], [dp("kd_%d" % sn)])
            Q.add("dve", lambda e: e.tensor_tensor(out=qgT, in0=qnT[:, h, tsl], in1=EG, op=ALU.mult),
                  [dp("qnT"), dp("EG_%d" % sn)], [dp("qgT_%d" % sn)])
            Y = ps[ba]
            cur = 0
            for lev in range(6):
                Q.mm(Y[:, 0:256], PTm[cur], X, True, True, [dp(f"PT{cur}_{sn}"), dp("X_%d" % sn)], [d_ps[ba]])
                Q.add("dve", lambda e, lev=lev: e.tensor_tensor(out=X, in0=X, in1=Y[:, 0:256],
                                                                op=ALU.subtract if lev == 0 else ALU.add),
                      [dp("X_%d" % sn), d_ps[ba]], [dp("X_%d" % sn)])
                if lev < 5:
                    nx = 1 - cur
                    Q.mm(Y[:, 256:384], PTm[cur], Pm[cur], True, True, [dp(f"PT{cur}_{sn}"), dp(f"P{cur}_{sn}")], [d_ps[ba]])
                    Q.mm(Y[:, 384:512], Pm[cur], PTm[cur], True, True, [dp(f"PT{cur}_{sn}"), dp(f"P{cur}_{sn}")], [d_ps[ba]])
                    Q.add("act", lambda e, nx=nx: e.activation(out=Pm[nx], in_=Y[:, 256:384], func=AF.Identity),
                          [d_ps[ba]], [dp(f"P{nx}_{sn}")])
                    Q.add("dve", lambda e, nx=nx: e.tensor_copy(out=PTm[nx], in_=Y[:, 384:512]),
                          [d_ps[ba]], [dp(f"PT{nx}_{sn}")])
                    cur = nx
            Q.mm(Y[:, 0:128], X[:, 128:256], ident_f[:], True, True, [dp("X_%d" % sn), d_id], [d_ps[ba]])
            Q.add("act", lambda e: e.activation(out=wT, in_=Y[:, 0:128], func=AF.Identity), [d_ps[ba]], [dp("wT_%d" % sn)])

        def delta_chunk(Q, dr, h, tl_, c, first):
            sn = dr
            T = TS[sn]
            rhs1, rhs2, rhs3, EB, DT, EG, FA, FB, FC = (T[k_] for k_ in ("rhs1", "rhs2", "rhs3", "EB", "DT", "EG", "FA", "FB", "FC"))
            swT, qbT, QKdT, wT, qgT, vnew = (T[k_] for k_ in ("swT", "qbT", "QKdT", "wT", "qgT", "vnew"))
            kw, wsc, Pm, PTm, X, kd, kdcc, dn = (T[k_] for k_ in ("kw", "wsc", "Pm", "PTm", "X", "kd", "kdcc", "dn"))
            tc = slice(c * 64, (c + 1) * 64)
            tl1 = c * 64 + 63 if dr == 0 else c * 64
            chg = tl_ * 2 + c
            eb_ = 4 if dr == 0 else 6
            E_ = ps[eb_]
            Q.mm(E_[:, 0:128], wT, Sbf[:, dr, h, :], True, True, [dp("wT_%d" % sn), dp("Sbf%d" % dr)], [d_ps[eb_]])
            Q.add("dve", lambda e: e.tensor_tensor(out=vnew, in0=X[:, 0:128], in1=E_[:, 0:128], op=ALU.subtract),
                  [dp("X_%d" % sn), d_ps[eb_]], [dp("vnew_%d" % sn)])
            Q.mm(E_[0:64, 128:256], qgT[:, tc], Sbf[:, dr, h, :], True, False, [dp("qgT_%d" % sn), dp("Sbf%d" % dr)], [d_ps[eb_]])
            Q.mm(E_[0:64, 128:256], QKdT[:, tc], vnew, False, True, [dp("QKdT_%d" % sn), dp("vnew_%d" % sn)], [d_ps[eb_]])
            Q.mm(E_[:, 256:384], kd[:, c, :], vnew, True, True, [dp("kd_%d" % sn), dp("vnew_%d" % sn)], [d_ps[eb_]])
            if first:
                Q.add("act", lambda e: e.activation(out=oacc[:, chg * nh + h, :], in_=E_[0:64, 128:256], func=AF.Identity),
                      [d_ps[eb_]], [dp("oacc")])
            else:
                Q.add("dve", lambda e: e.tensor_tensor(out=oacc[:, chg * nh + h, :], in0=oacc[:, chg * nh + h, :],
                                                       in1=E_[0:64, 128:256], op=ALU.add),
                      [d_ps[eb_], dp("oacc")], [dp("oacc")])
            Q.add("dve", lambda e: e.scalar_tensor_tensor(out=Sst[:, dr, h, :], in0=Sst[:, dr, h, :],
                                                          scalar=EG[:, tl1:tl1 + 1], in1=E_[:, 256:384],
                                                          op0=ALU.mult, op1=ALU.add),
                  [dp("Sst%d" % dr), dp("EG_%d" % sn), d_ps[eb_]], [dp("Sst%d" % dr)])
            Q.add("act", lambda e: e.activation(out=Sbf[:, dr, h, :], in_=Sst[:, dr, h, :], func=AF.Identity),
                  [dp("Sst%d" % dr)], [dp("Sbf%d" % dr)])

        class Rec:
            def __init__(self):
                self.ops = []

            def add(self, *a_, **k_):
                self.ops.append((0, a_, k_))

            def mm(self, *a_, **k_):
                self.ops.append((1, a_, k_))

        for si, (t0, nt) in enumerate(seqs):
            cfg["init_state"](si, Caug, Cbf, Sst, Sbf, dp)
            chains = []
            for dr in range(2):
                qm_, qd_ = Rec(), Rec()
                for step in range(nt):
                    tl_ = t0 + step if dr == 0 else t0 + nt - 1 - step
                    other_step = nt - 1 - step
                    first = step < other_step
                    for h in range(nh):
                        mlstm_unit(qm_, dr, h, tl_)
                        for c in ((0, 1) if dr == 0 else (1, 0)):
                            mlstm_chunk(qm_, dr, h, tl_, c, first)
                        delta_unit(qd_, dr, h, tl_)
                        for c in ((0, 1) if dr == 0 else (1, 0)):
                            delta_chunk(qd_, dr, h, tl_, c, first)
                chains += [qm_.ops, qd_.ops]
            pos = [0] * len(chains)
            live = True
            while live:
                live = False
                for ci_, ch_ in enumerate(chains):
                    while pos[ci_] < len(ch_):
                        kind_, a_, k_ = ch_[pos[ci_]]
                        pos[ci_] += 1
                        (P.mm if kind_ else P.add)(*a_, **k_)
                        live = True
                        if not (kind_ == 1 and a_[4] is False):
                            break
            cfg["final_state"](si, t0, nt, Caug, Sst, gt, dp, d_g)

        mixg = cfg["mix_group"]
        FB_ = []
        NFB = 3
        for i_ in range(NFB):
            o6 = 5.25 * i_
            FB_.append(dict(f1=AVH(o6, [64, 4, 128], F32, 64), f2=AVH(o6 + 2, [64, 4, 128], F32, 64),
                            f3=AVH(o6 + 4, [64, 4, 128], BF16, 64), fc=AVH(o6 + 5, [64, 3, 4], F32, 64)))
        NI = NCHK * nh
        gi = 0
        for kind in range(2):
            acc = hacc if kind == 0 else oacc
            d_acc = dp("hacc") if kind == 0 else dp("oacc")
            for g in range(NI // 4):
                bf_ = FB_[gi % NFB]
                sfx = "_%d_1" % (gi % NFB)
                f1, f2, f3, fc = bf_["f1"], bf_["f2"], bf_["f3"], bf_["fc"]
                d1, d2, d3, dc = dp("f1" + sfx), dp("f2" + sfx), dp("f3" + sfx), dp("fc" + sfx)
                pbk = 5 + (gi % NFB)
                gi += 1
                i0 = 4 * g
                P.add("act", lambda e, acc=acc, i0=i0, f1=f1: e.activation(out=f1, in_=acc[:, i0:i0 + 4, :], func=AF.Square),
                      [d_acc], [d1])
                P.add("dve", lambda e, f1=f1, fc=fc: e.reduce_sum(out=fc[:, 0, :], in_=f1, axis=AX.X), [d1], [dc])
                P.add("act", lambda e, fc=fc: e.activation(out=fc[:, 1, :], in_=fc[:, 0, :], func=AF.Ln,
                                                           bias=epsb[0:64, 3:4], scale=1.0 / 128.0), [dc, d_ones], [dc])
                P.add("act", lambda e, fc=fc: e.activation(out=fc[:, 2, :], in_=fc[:, 1, :], func=AF.Exp, scale=-0.5),
                      [dc], [dc])
                if kind == 0:
                    P.add("act", lambda e, i0=i0, f2=f2: e.activation(out=f2, in_=om[:, i0:i0 + 4, :], func=AF.Sigmoid),
                          [dp("om")], [d2])
                else:
                    P.add("act", lambda e, i0=i0, f2=f2: e.activation(out=f2, in_=zd[:, i0:i0 + 4, :], func=AF.Silu),
                          [dp("zd")], [d2])
                for k in range(4):
                    h = (i0 + k) % nh
                    gsl = ngm[:, h * 128:(h + 1) * 128] if kind == 0 else ngd[:, :]
                    P.add("dve", lambda e, acc=acc, i0=i0, k=k, gsl=gsl, f1=f1, fc=fc: e.scalar_tensor_tensor(
                        out=f1[:, k, :], in0=acc[:, i0 + k, :], scalar=fc[:, 2, k:k + 1], in1=gsl,
                        op0=ALU.mult, op1=ALU.mult), [d_acc, dc, d_ngm, d_ngd, d1], [d1])
                P.add("dve", lambda e, f1=f1, f2=f2, f3=f3: e.tensor_tensor(out=f3, in0=f1, in1=f2, op=ALU.mult),
                      [d1, d2], [d3])
                for k in range(4):
                    P.mm(ps[pbk][:, k * 64:(k + 1) * 64], f3[:, k, :], ident_b[0:64, 0:64], True, True, [d3, d_ones],
                         [d_ps[pbk]])
                P.add("act", lambda e, kind=kind, g=g, pbk=pbk: e.activation(
                    out=mixg(kind, g), in_=ps[pbk][:, 0:256].rearrange("p (k t) -> p k t", k=4), func=AF.Identity),
                    [d_ps[pbk]], d_mix)
        return D_

    def ab_layer(l):
        for b in range(2):
            modulate(l, 1, b)
        d_ag1i, d_ag1o = Dep("ab_ag1i"), Dep("ab_ag1o")
        allgather(hT[:, :, BLK:2 * BLK], d_h[1], ag1_in.rearrange("p (k t) -> p k t", k=8), ag1_in, ag1_out,
                  d_ag1i, d_ag1o)

        def users_of(deplist):
            rs = []
            for o in deplist:
                rs += list(o.rs)
                if o.w is not None:
                    rs.append(o.w)
            return rs
        ffn_users = users_of(d_a)
        mixP = AV(61, [128, 8, BLK], BF16)
        d_mixP = Dep("ab_mixP")
        d_mixP.rs = list(ffn_users)
        cst = AV(69, [64, 2, 4, 130], F32, parts=64)
        mfin = P.sb([4, 2, 2], F32, "ab_mfin")
        mrun = P.sb([4, 4], F32, "ab_mrun")
        gmx = P.sb([4, 2, 2], F32, "ab_gmx")
        btt = P.sb([4, 2, 2], F32, "ab_btt")
        dg4 = P.sb([4, 4], F32, "ab_dg4")
        esc = P.sb([64, 4], F32, "ab_esc")
        d_fs = Dep("ab_fs")
        d_cst = Dep("ab_cst")
        d_mfin = Dep("ab_mfin")

        def p_init(si, Caug, Cbf, Sst, Sbf, dp):
            P.add("dve", lambda e: e.memset(Caug.rearrange("p a b c -> p (a b c)"), 0.0), [], [dp("Caug0"), dp("Caug1")])
            P.add("dve", lambda e: e.memset(Cbf.rearrange("p a b c -> p (a b c)"), 0.0), [], [dp("Cbf0"), dp("Cbf1")])
            P.add("dve", lambda e: e.memset(Sst.rearrange("p a b c -> p (a b c)"), 0.0), [], [dp("Sst0"), dp("Sst1")])
            P.add("dve", lambda e: e.memset(Sbf.rearrange("p a b c -> p (a b c)"), 0.0), [], [dp("Sbf0"), dp("Sbf1")])

        def p_final(si, t0, nt, Caug, Sst, gt, dp, d_g):
            nh = 4
            F_ = ps[7]
            for dr in range(2):
                for ti in range(nt):
                    tl_ = t0 + ti
                    P.mm(F_[0:nh, 0:128], gt[:, tl_, 11, dr * nh:(dr + 1) * nh], ident_f[:], True, True,
                         [d_g, d_id], [d_ps[7]])
                    P.mm(F_[0:nh, 128:256], gt[:, tl_, 12, dr * nh:(dr + 1) * nh], ident_f[:], True, True,
                         [d_g, d_id], [d_ps[7]])
                    P.add("dve", lambda e, ti=ti: e.tensor_reduce(
                        out=gmx[:, ti, :], in_=F_[0:nh, 0:128].rearrange("p (c s) -> p c s", c=2), axis=AX.X, op=ALU.max),
                        [d_ps[7]], [d_fs])
                    P.add("dve", lambda e, ti=ti: e.tensor_copy(
                        out=btt[:, ti, :], in_=F_[0:nh, 128:256].rearrange("p (c s) -> p c s", c=2)[:, :, 0]),
                        [d_ps[7]], [d_fs])
                P.add("dve", lambda e: e.memset(mrun[:, 0:1], 0.0), [d_fs], [d_fs])
                order = [(ti, c) for ti in range(nt) for c in range(2)]
                if dr == 1:
                    order = order[::-1]
                for (ti, c) in order:
                    P.add("dve", lambda e, ti=ti, c=c: e.tensor_tensor(out=mrun[:, 1:2], in0=mrun[:, 0:1],
                                                                       in1=btt[:, ti, c:c + 1], op=ALU.add), [d_fs], [d_fs])
                    P.add("dve", lambda e, ti=ti, c=c: e.tensor_tensor(out=mrun[:, 0:1], in0=mrun[:, 1:2],
                                                                       in1=gmx[:, ti, c:c + 1], op=ALU.max), [d_fs], [d_fs])
                P.add("dve", lambda e, dr=dr: e.tensor_copy(out=mfin[:, si, dr:dr + 1], in_=mrun[:, 0:1]), [d_fs], [d_mfin])
                P.add("dve", lambda e: e.tensor_scalar(out=dg4[:], in0=ident_f[0:4, 0:4], scalar1=mrun[:, 0:1],
                                                       scalar2=None, op0=ALU.mult), [d_fs, d_id], [d_fs])
                P.mm(F_[0:64, 256:260], ones_f[0:4, 0:64], dg4[:], True, True, [d_ones, d_fs], [d_ps[7]])
                P.add("act", lambda e: e.activation(out=esc[:], in_=F_[0:64, 256:260], func=AF.Exp, scale=-1.0),
                      [d_ps[7]], [d_fs])
                for h in range(nh):
                    P.add("dve", lambda e, dr=dr, h=h: e.tensor_scalar(
                        out=cst[:, dr, h, :], in0=Caug[0:64, dr, h, :], scalar1=esc[:, h:h + 1], scalar2=None,
                        op0=ALU.mult), [dp("Caug0"), dp("Caug1"), d_fs], [d_cst])
            P.add("sp", lambda e: e.dma_start(out=c_out[si], in_=cst.rearrange("p a b c -> p (a b c)")),
                  [d_cst], [d_y], dma=True)
            P.add("sp", lambda e: e.dma_start(out=s_out[si], in_=Sst.rearrange("p a b c -> p (a b c)")),
                  [dp("Sst0"), dp("Sst1")], [d_y], dma=True)

        cfgp = dict(nh=4, T=BLK, seqs=[(0, 2), (2, 2)], src=lambda k, a, b: hT[:, k, a:b], d_src=d_h[0],
                    wf=abp_wf, wtA=abp_wtA, wtB=abp_wtB, gp=abp_gp, cw=abp_cw, ngm=abp_ngm, ngd=ab_ngd,
                    mix_group=lambda kind, g: mixP[:, kind * 4:(kind + 1) * 4, g * 64:(g + 1) * 64], d_mix=[d_mixP],
                    prev_users=ffn_users, extra_users=lambda: [], init_state=p_init, final_state=p_final)
        Dp = ab_stage(cfgp)
        P.add("sp", lambda e: e.dma_start(out=m_out, in_=mfin[:].rearrange("p a b -> p (a b)")), [d_mfin], [d_y], dma=True)
        out_proj(l, ab_wo, mixP, [d_mixP], 0)
        prompt_users = users_of(list(Dp.values()) + [d_mixP]) + ffn_users
        hfull = AV(72, [128, 8, 2048], BF16)
        d_hf = Dep("ab_hfull")
        d_hf.rs = list(prompt_users)
        for m in range(4):
            P.add("sp", lambda e, m=m: e.dma_start(
                out=hfull[:, :, m * BLK:(m + 1) * BLK],
                in_=ag1_out[m * 128:(m + 1) * 128, :].rearrange("p (k t) -> p k t", k=8)),
                [d_ag1o], [d_hf], dma=True)
        headout = AV(61, [128, 2, 2048], BF16)
        d_ho = Dep("ab_headout")
        d_ho.rs = list(prompt_users)
        m0t = P.sb([128, 2], F32, "ab_m0t")
        d_c0 = Dep("ab_c0t")

        def s_init(si, Caug, Cbf, Sst, Sbf, dp):
            P.add("sp", lambda e: e.dma_start(out=Caug.rearrange("p a b c -> p (a b c)"),
                                              in_=ab_c0.rearrange("p a c -> p (a c)")), [], [dp("Caug0"), dp("Caug1")], dma=True)
            P.add("sp", lambda e: e.dma_start(out=m0t[:], in_=ab_m0), [], [d_c0], dma=True)
            P.add("act", lambda e: e.activation(out=m0t[:], in_=m0t[:], func=AF.Exp), [d_c0], [d_c0])
            for dr in range(2):
                P.add("dve", lambda e, dr=dr: e.tensor_scalar(out=Caug[:, dr, 0, :], in0=Caug[:, dr, 0, :],
                                                              scalar1=m0t[:, dr:dr + 1], scalar2=None, op0=ALU.mult),
                      [dp("Caug0"), dp("Caug1"), d_c0], [dp("Caug0"), dp("Caug1")])
            P.add("act", lambda e: e.activation(out=Cbf.rearrange("p a b c -> p (a b c)"),
                                                in_=Caug.rearrange("p a b c -> p (a b c)"), func=AF.Identity),
                  [dp("Caug0"), dp("Caug1")], [dp("Cbf0"), dp("Cbf1")])
            P.add("sp", lambda e: e.dma_start(out=Sst.rearrange("p a b c -> p (a b c)"),
                                              in_=ab_s0.rearrange("p a c -> p (a c)")), [], [dp("Sst0"), dp("Sst1")], dma=True)
            P.add("act", lambda e: e.activation(out=Sbf.rearrange("p a b c -> p (a b c)"),
                                                in_=Sst.rearrange("p a b c -> p (a b c)"), func=AF.Identity),
                  [dp("Sst0"), dp("Sst1")], [dp("Sbf0"), dp("Sbf1")])

        cfgs = dict(nh=1, T=2048, seqs=[(0, 16)], src=lambda k, a, b: hfull[:, k, a:b], d_src=[d_hf],
                    wf=abs_wf, wtA=abs_wtA, wtB=abs_wtB, gp=abs_gp, cw=abs_cw, ngm=abs_ngm, ngd=ab_ngd,
                    mix_group=lambda kind, g: headout[:, kind, g * 256:(g + 1) * 256].rearrange("p (k t) -> p k t", k=4),
                    d_mix=[d_ho],
                    prev_users=prompt_users, extra_users=lambda: users_of([d_hf]),
                    init_state=s_init, final_state=lambda *a: None)
        Ds = ab_stage(cfgs)
        d_ag2i, d_ag2o = Dep("ab_ag2i"), Dep("ab_ag2o")
        allgather(headout, [d_ho], ag2_in.rearrange("p (c t) -> p c t", c=2), ag2_in, ag2_out, d_ag2i, d_ag2o)
        cand = AV(0, [128, 4, 8, BLK], BF16)
        mixS = AV(32, [128, 8, BLK], BF16)
        d_cand, d_mixs = Dep("ab_cand"), Dep("ab_mixS")
        sample_users = users_of(list(Ds.values()) + [d_hf, d_ho])
        d_cand.rs = list(sample_users)
        d_mixs.rs = list(sample_users)
        for m in range(4):
            for kind in range(2):
                P.add("sp", lambda e, m=m, kind=kind: e.dma_start(
                    out=cand[:, :, kind * 4 + m, :],
                    in_=ag2_out[m * 128:(m + 1) * 128, kind * 2048:(kind + 1) * 2048].rearrange("p (r t) -> p r t", r=4)),
                    [d_ag2o], [d_cand], dma=True)
        for rr in range(4):
            if rr == 0:
                P.add("dve", lambda e: e.tensor_scalar(
                    out=mixS.rearrange("p a b -> p (a b)"), in0=cand[:, 0].rearrange("p a b -> p (a b)"),
                    scalar1=oh[:, 0:1], scalar2=None, op0=ALU.mult), [d_cand, d_oh], [d_mixs])
            else:
                P.add("dve", lambda e, rr=rr: e.scalar_tensor_tensor(
                    out=mixS.rearrange("p a b -> p (a b)"), in0=cand[:, rr].rearrange("p a b -> p (a b)"),
                    scalar=oh[:, rr:rr + 1], in1=mixS.rearrange("p a b -> p (a b)"), op0=ALU.mult, op1=ALU.add),
                    [d_cand, d_oh, d_mixs], [d_mixs])
        out_proj(l, ab_wo, mixS, [d_mixs], 1)
        tail = users_of([d_cand, d_mixs]) + sample_users
        for b in range(2):
            d_a[b].rs = list(d_a[b].rs) + tail
        hT_tmp_users = users_of([v_ for k_, v_ in list(Dp.items()) + list(Ds.items()) if k_.endswith("_1")])
        for b in range(2):
            for j in range(8):
                d_h[b][j].rs = list(d_h[b][j].rs) + hT_tmp_users


    def final_norm():
        for b in range(2):
            blk = slice(b * BLK, (b + 1) * BLK)
            P.add("act", lambda e, blk=blk: e.activation(out=sq[:], in_=xT[:, :, blk], func=AF.Square),
                  d_x[b], [d_sq])
            for k in range(8):
                P.mm(ps[6], ones_bf[:], sq[:, k, :], k == 0, k == 7, [d_ones, d_sq], [d_ps[6]])
            rstd_from_ps6()
            for j in range(8):
                P.add("dve", lambda e, j=j, blk=blk: e.scalar_tensor_tensor(
                    out=xT[:, j, blk], in0=xT[:, j, blk], scalar=ng[:, DEPTH * 3, j:j + 1], in1=rstd[:],
                    op0=ALU.mult, op1=ALU.mult), [d_x[b][j], d_ng, d_rstd], [d_x[b][j]])
                P.add("dve", lambda e, j=j, blk=blk: e.tensor_scalar(
                    out=xT[:, j, blk], in0=xT[:, j, blk], scalar1=32.0, scalar2=None, op0=ALU.mult),
                    [d_x[b][j]], [d_x[b][j]])
        P.add("sp", lambda e: e.dma_start(out=yT_out, in_=xT[:]),
              [d_x[b_][j_] for b_ in range(2) for j_ in range(8)], [d_y], dma=True)

    if stage.startswith("dbg"):
        na_layer(1, stop=stage[3:] or None)
    elif stage == "naA":
        ffn(1, 0)
        na_layer(1)
    elif stage == "naB":
        na_layer(1)
        ffn(1, 1)
    else:
        for l in range(DEPTH):
            ffn(l, 0)
            if l % 2 == 0 and stage in ("full", "ab"):
                ab_layer(l)
            if l % 2 == 1 and stage in ("full", "na"):
                na_layer(l)
            ffn(l, 1)
    final_norm()

    P.emit()
    return nc, es


NA_BIG = -30000.0


def na_bias_table(rel_bias_h):
    col = np.arange(64)
    cs = np.clip(col - 8, 0, 48)
    valid = (col[None, :] >= cs[:, None]) & (col[None, :] < cs[:, None] + 16)
    cidx = np.clip(col[None, :] - col[:, None], -15, 15) + 15
    t = rel_bias_h[:, :, cidx]
    t = np.where(valid[None, None], t, np.float32(NA_BIG))
    return np.ascontiguousarray(t.transpose(3, 1, 0, 2)).astype(np.float32)


def prep_inputs(inp):
    f32 = np.float32
    xp = np.asarray(inp["x_prompt"], f32)
    xs = np.asarray(inp["x_sample"], f32)
    c = np.asarray(inp["c"], f32)
    c_ctx = np.asarray(inp["c_ctx"], f32)
    shared = {}
    aw = np.asarray(inp["ada_w"], f32)
    aw6 = aw.reshape(DEPTH, 8, 128, 4, 18, 128)
    ab = np.asarray(inp["ada_b"], f32).reshape(DEPTH, 4, 18, 128)
    ngm = np.concatenate([np.asarray(inp["norm_g"], f32).reshape(DEPTH * 3, D),
                          np.asarray(inp["final_norm_g"], f32).reshape(1, D)], axis=0)
    shared["normg"] = np.ascontiguousarray(ngm.reshape(DEPTH * 3 + 1, 8, 128).transpose(2, 0, 1))
    wg = np.asarray(inp["ffn_wg"], f32).reshape(DEPTH * 2, 8, 128, NF, 128)
    wu = np.asarray(inp["ffn_wu"], f32).reshape(DEPTH * 2, 8, 128, NF, 128)
    gu = np.stack([wg, wu], axis=0)
    shared["ffn_gu"] = np.ascontiguousarray(gu.transpose(1, 4, 3, 0, 2, 5))
    wd = np.asarray(inp["ffn_wd"], f32).reshape(DEPTH * 2, NF, 128, 8, 128)
    shared["ffn_d"] = np.ascontiguousarray(wd.transpose(0, 3, 2, 1, 4))
    shared["ident"] = np.eye(128, dtype=f32)
    nw = np.asarray(inp["na_w_in"], f32)[0]
    wqk = nw[:, :2048].reshape(8, 128, 16, 128)
    shared["na_wqk"] = np.ascontiguousarray(wqk.transpose(2, 1, 0, 3))
    wv = nw[:, 2048:].reshape(8, 128, 4, 256)
    shared["na_wv"] = np.ascontiguousarray(wv.transpose(2, 1, 0, 3))
    wo = np.asarray(inp["na_w_out"], f32)[0].reshape(8, 128, 8, 128)
    shared["na_wo"] = np.ascontiguousarray(wo.transpose(2, 1, 0, 3))
    W = np.asarray(inp["ab_w_in"], f32)[0]
    Wx = np.concatenate([W, np.zeros((D, 1), f32)], axis=1)
    ZC = W.shape[1]
    O_QM, O_KM, O_VM, O_OM, O_I, O_F, O_QD, O_KD, O_VD, O_ZD, O_B, O_A = 0, 256, 512, 1024, 1536, 1544, 1552, 2064, 2576, 3088, 3600, 3608

    def ab_variant(heads):
        nh_ = len(heads)
        cf = []
        for h in heads:
            cf += [O_QM + h * 64 + d for d in range(64)] + [ZC] * 64
        for h in heads:
            cf += [O_KM + h * 64 + d for d in range(64)] + [ZC] * 64
        for base in (O_QD, O_KD, O_VD):
            for h in heads:
                cf += [base + h * 128 + d for d in range(128)]
        ca = []
        for h in heads:
            ca += [O_VM + h * 128 + d for d in range(128)]
        for h in heads:
            ca += [O_KM + h * 64 + d for d in range(64)]
        for base in (O_I, O_F, O_B, O_A):
            for dr in range(2):
                for h in heads:
                    ca.append(base + dr * 4 + h)
        ca += [ZC] * ((-len(ca)) % 256)
        cb = []
        for base in (O_OM, O_ZD):
            for h in heads:
                cb += [base + h * 128 + d for d in range(128)]
        wf_ = Wx[:, cf].reshape(8, 128, len(cf) // 128, 128).transpose(2, 1, 0, 3)
        wa_ = Wx[:, ca].reshape(8, 128, len(ca) // 256, 256).transpose(2, 1, 0, 3)
        wb_ = Wx[:, cb].reshape(8, 128, len(cb) // 256, 256).transpose(2, 1, 0, 3)

        def gsel(a):
            return np.asarray(a, f32)[:, heads].reshape(-1)
        gpv = np.stack([gsel(inp["mlstm_b_i"][0]), gsel(inp["mlstm_b_f"][0]), gsel(inp["delta_dt_bias"][0]),
                        gsel(inp["delta_a_log"][0])], axis=0)
        nt_ = 16 // nh_
        gp_ = np.broadcast_to(gpv[None, :, None, :], (128, 4, nt_, 2 * nh_))
        cwv = np.asarray(inp["delta_conv_w"], f32)[0]
        cw_ = np.stack([cwv[:, kind * 512 + h * 128: kind * 512 + (h + 1) * 128].T
                        for kind in range(3) for h in heads], axis=1)
        gm = np.asarray(inp["mlstm_norm_g"], f32)[0]
        ngm_ = np.broadcast_to(np.concatenate([gm[h * 128:(h + 1) * 128] for h in heads])[None], (64, 128 * nh_))
        return dict(wf=np.ascontiguousarray(wf_), wtA=np.ascontiguousarray(wa_), wtB=np.ascontiguousarray(wb_),
                    gp=np.ascontiguousarray(gp_), cw=np.ascontiguousarray(cw_), ngm=np.ascontiguousarray(ngm_))
    vp_ = ab_variant([0, 1, 2, 3])
    for k_, v_ in vp_.items():
        shared["abp_" + k_] = v_
    vs_ = [ab_variant([r_]) for r_ in range(4)]
    shared["ab_ngd"] = np.ascontiguousarray(np.broadcast_to(np.asarray(inp["delta_norm_g"], f32)[0][None], (64, 128)))
    awo = np.asarray(inp["ab_w_out"], f32)[0].reshape(8, 128, 8, 128)
    shared["ab_wo"] = np.ascontiguousarray(awo.transpose(2, 1, 0, 3))
    pp = np.arange(128)
    same = (pp[:, None] // 64) == (pp[None, :] // 64)
    le = same & (pp[:, None] <= pp[None, :])
    lt = same & (pp[:, None] < pp[None, :])
    ge = same & (pp[:, None] >= pp[None, :])
    gt_ = same & (pp[:, None] > pp[None, :])
    NEGV = np.float32(-30000.0)
    msk = np.stack([le.astype(f32), ge.astype(f32), same.astype(f32),
                    np.where(le, 0, NEGV), np.where(lt, 0, NEGV), np.where(ge, 0, NEGV), np.where(gt_, 0, NEGV)],
                   axis=1).astype(f32)
    shared["ab_masks"] = np.ascontiguousarray(msk)
    shared["ab_chind"] = np.ascontiguousarray(np.stack([(pp < 64), (pp >= 64)], axis=1).astype(f32))
    sC = np.asarray(inp["state_mlstm_C"], f32)[:, 0]
    sN = np.asarray(inp["state_mlstm_n"], f32)[:, 0]
    sM = np.asarray(inp["state_mlstm_m"], f32)[:, 0]
    sS = np.asarray(inp["state_delta_S"], f32)[:, 0]
    rel = np.asarray(inp["na_rel_bias"], f32)[0]
    ck = np.asarray(inp["cache_na_k"], f32)[:, 0]
    cvv_ = np.asarray(inp["cache_na_v"], f32)[:, 0]
    in_maps = []
    for core in range(NCORES):
        g = core // 4
        r = core % 4
        xtok = np.concatenate([xp[2 * core].reshape(256, D), xp[2 * core + 1].reshape(256, D),
                               xs[g, 512 * r:512 * (r + 1)]], axis=0)
        m = dict(shared)
        m["xT_in"] = np.ascontiguousarray(xtok.reshape(NTOK, 8, 128).transpose(2, 1, 0))
        m["ada_w"] = np.ascontiguousarray(aw6[:, :, :, r].transpose(0, 3, 2, 1, 4).reshape(36, 128, 8, 128))
        m["ada_b"] = np.ascontiguousarray(ab[:, r].transpose(2, 0, 1).reshape(128, 36))
        cvv = np.stack([c_ctx, c[g]], axis=-1)
        m["cvec3"] = np.ascontiguousarray(cvv.reshape(8, 128, 2).transpose(1, 0, 2))
        ohm = np.zeros((128, 4), f32)
        ohm[:, r] = 1.0
        m["onehot"] = ohm
        hs = slice(4 * r, 4 * r + 4)
        m["na_wqk_s"] = np.ascontiguousarray(np.stack(
            [shared["na_wqk"][2 * r], shared["na_wqk"][2 * r + 1],
             shared["na_wqk"][8 + 2 * r], shared["na_wqk"][8 + 2 * r + 1]], axis=0))
        wvs = nw[:, 2048 + 256 * r: 2048 + 256 * (r + 1)].reshape(8, 128, 256)
        m["na_wv_s"] = np.ascontiguousarray(wvs.transpose(1, 0, 2))
        kc = ck[g, hs]
        kc = kc.reshape(2, 2, 256, 64).transpose(1, 3, 0, 2)
        m["na_kc"] = np.ascontiguousarray(kc.reshape(128, 2, 256))
        vc = cvv_[g, hs]
        vc = vc.reshape(4, 2, 128, 64).transpose(2, 1, 0, 3)
        m["na_vc"] = np.ascontiguousarray(vc)
        m["na_bias"] = na_bias_table(rel[hs])
        for k_, v_ in vs_[r].items():
            m["abs_" + k_] = v_
        c0 = np.zeros((128, 2, 130), f32)
        c0[0:64, :, 0:128] = sC[g, :, r].transpose(1, 0, 2)
        c0[0:64, :, 128] = sN[g, :, r].T
        m["ab_c0"] = c0
        m["ab_m0"] = np.ascontiguousarray(np.broadcast_to(sM[g, :, r][None], (128, 2)))
        m["ab_s0"] = np.ascontiguousarray(sS[g, :, r].transpose(1, 0, 2))
        in_maps.append(m)
    return in_maps


def kernel(**inp):
    stage = inp.pop("_stage", "full")
    nc, es = build_program(stage)
    with es:
        in_maps = prep_inputs(inp)
        if stage.startswith("dbg"):
            for m in in_maps:
                for k_ in ("ada_w", "ffn_gu", "ffn_d"):
                    m.pop(k_, None)
        res = run_bass_kernel_spmd(nc, in_maps, core_ids=list(range(NCORES)))
    outs = res.results
    y_prompt = np.zeros((16, 256, D), np.float32)
    y_sample = np.zeros((2, 2048, D), np.float32)
    new_k = np.zeros((16, 1, 16, 256, 64), np.float32)
    new_v = np.zeros((16, 1, 16, 256, 64), np.float32)
    new_C = np.zeros((16, 1, 2, 4, 64, 128), np.float32)
    new_n = np.zeros((16, 1, 2, 4, 64), np.float32)
    new_m = np.zeros((16, 1, 2, 4), np.float32)
    new_S = np.zeros((16, 1, 2, 4, 128, 128), np.float32)
    for core in range(NCORES):
        g = core // 4
        r = core % 4
        yT = outs[core]["yT_out"]
        y = yT.transpose(2, 1, 0).reshape(NTOK, D)
        y_prompt[2 * core] = y[0:256]
        y_prompt[2 * core + 1] = y[256:512]
        y_sample[g, 512 * r:512 * (r + 1)] = y[512:1024]
        kT = outs[core]["kT_out"].reshape(2, 64, 8, 2, 256)
        new_k[2 * core:2 * core + 2, 0] = kT.transpose(3, 2, 0, 4, 1).reshape(2, 16, 256, 64)
        vv = outs[core]["v_out"].transpose(1, 0, 2).reshape(2, 256, 16, 64)
        new_v[2 * core:2 * core + 2, 0] = vv.transpose(0, 2, 1, 3)
        co = outs[core]["c_out"].reshape(2, 64, 2, 4, 130)
        so = outs[core]["s_out"].reshape(2, 128, 2, 4, 128)
        mo = outs[core]["m_out"].reshape(4, 2, 2)
        for si in range(2):
            bidx = 2 * core + si
            new_C[bidx, 0] = co[si, :, :, :, 0:128].transpose(1, 2, 0, 3)
            new_n[bidx, 0] = co[si, :, :, :, 128].transpose(1, 2, 0)
            new_m[bidx, 0] = mo[:, si, :].T
            new_S[bidx, 0] = so[si].transpose(1, 2, 0, 3)
    if stage in ("full", "ab"):
        return y_prompt, y_sample, new_C, new_n, new_m, new_S, new_k, new_v
    return y_prompt, y_sample, new_k, new_v
```
